# Optimizing a Trainium2 kernel written in Bass

```python
import math
import jax, jax.numpy as jnp
from jax import lax
import numpy as np

D_MODEL = 2048
BATCH = 1
SEQ = 16384
DEPTH = 2

D_MIX = D_MODEL
SSD_WIDTH = D_MIX // 2
SSD_HEAD_DIM = 64
SSD_HEADS = SSD_WIDTH // SSD_HEAD_DIM
SSD_GROUPS = 2
SSD_STATE = 128
SSD_CONV = 4
SSD_CHUNK = 128
SSD_CONV_DIM = SSD_WIDTH + 2 * SSD_GROUPS * SSD_STATE
CFM_WIDTH = D_MIX // 4
CFM_KERNEL = 31
ATT_WIDTH = D_MIX // 4
ATT_HEAD_DIM = 64
ATT_V_DIM = 2 * ATT_HEAD_DIM
ATT_HEADS = ATT_WIDTH // ATT_V_DIM
Q_BLOCK = 128
ROPE_THETA = 10000.0
EPS = 1e-6

SPLIT_SIZES = (
    SSD_WIDTH,
    SSD_CONV_DIM,
    SSD_HEADS,
    CFM_WIDTH,
    CFM_WIDTH,
    CFM_WIDTH,
    ATT_HEADS * 2 * ATT_HEAD_DIM,
    ATT_HEADS * 2 * ATT_HEAD_DIM,
    ATT_HEADS * ATT_V_DIM,
    ATT_WIDTH,
)
D_IN = 6160

kernel_name = "hybrid_ssd_conformer_diffattn_parallel_heads"


def rmsnorm(x, w):
    xf = x.astype(jnp.float32)
    y = xf * lax.rsqrt(jnp.mean(xf * xf, axis=-1, keepdims=True) + EPS)
    return (y * w.astype(jnp.float32)).astype(x.dtype)


def layernorm(x, w, b):
    xf = x.astype(jnp.float32)
    mu = jnp.mean(xf, axis=-1, keepdims=True)
    var = jnp.mean(jnp.square(xf - mu), axis=-1, keepdims=True)
    y = (xf - mu) * lax.rsqrt(var + EPS)
    return (y * w.astype(jnp.float32) + b.astype(jnp.float32)).astype(x.dtype)


def causal_depthwise_conv(x, w, b):
    K, C = w.shape
    y = lax.conv_general_dilated(
        x, w[:, None, :].astype(x.dtype), window_strides=(1,), padding=[(K - 1, 0)],
        dimension_numbers=("NWC", "WIO", "NWC"), feature_group_count=C)
    return y + b.astype(x.dtype)


def rope_tables(seq, dim):
    inv_freq = 1.0 / (ROPE_THETA ** (jnp.arange(0, dim, 2, dtype=jnp.float32) / dim))
    pos = jnp.arange(seq, dtype=jnp.float32)
    ang = pos[:, None] * inv_freq[None, :]
    return jnp.cos(ang), jnp.sin(ang)


def apply_rope(x, cos, sin):
    half = x.shape[-1] // 2
    xf = x.astype(jnp.float32)
    x1, x2 = xf[..., :half], xf[..., half:]
    c = cos[None, :, None, None, :]
    s = sin[None, :, None, None, :]
    return jnp.concatenate([x1 * c - x2 * s, x2 * c + x1 * s], axis=-1).astype(x.dtype)


def segsum_exp(a_cs):
    n = a_cs.shape[-1]
    diff = a_cs[..., :, None] - a_cs[..., None, :]
    mask = jnp.tril(jnp.ones((n, n), dtype=bool))
    return jnp.where(mask, jnp.exp(jnp.where(mask, diff, 0.0)), 0.0)


def ssd_chunked(x, dt, A, B, C):
    b, S, H, P = x.shape
    G, N = B.shape[-2], B.shape[-1]
    J = H // G
    L = SSD_CHUNK
    nc = S // L
    xdt = (x * dt[..., None]).reshape(b, nc, L, G, J, P)
    a = (dt * A).reshape(b, nc, L, G, J).transpose(0, 1, 3, 4, 2)
    Bc = B.reshape(b, nc, L, G, N)
    Cc = C.reshape(b, nc, L, G, N)
    a_cs = jnp.cumsum(a, axis=-1)
    decay_in = segsum_exp(a_cs)
    cb = jnp.einsum("bclgn,bcsgn->bcgls", Cc, Bc)
    y_diag = jnp.einsum("bcgls,bcgjls,bcsgjp->bclgjp", cb, decay_in, xdt)
    decay_states = jnp.exp(a_cs[..., -1:] - a_cs)
    states = jnp.einsum("bclgn,bcgjl,bclgjp->bcgjpn", Bc, decay_states, xdt)
    chunk_decay = jnp.exp(a_cs[..., -1])

    def step(carry, inp):
        st, dec = inp
        return carry * dec[..., None, None] + st, carry

    init = jnp.zeros((b, G, J, P, N), dtype=states.dtype)
    _, prev = lax.scan(step, init, (jnp.moveaxis(states, 1, 0), jnp.moveaxis(chunk_decay, 1, 0)))
    prev = jnp.moveaxis(prev, 0, 1)
    y_off = jnp.einsum("bclgn,bcgjpn,bcgjl->bclgjp", Cc, prev, jnp.exp(a_cs))
    return (y_diag + y_off).reshape(b, S, H, P)


def ssd_branch(z, xbc, dt_raw, conv_w, conv_b, dt_bias, a_log, d_skip, norm_w):
    b, S, _ = xbc.shape
    xbc = jax.nn.silu(causal_depthwise_conv(xbc, conv_w, conv_b))
    gn = SSD_GROUPS * SSD_STATE
    xs, bm, cm = jnp.split(xbc, [SSD_WIDTH, SSD_WIDTH + gn], axis=-1)
    xs = xs.reshape(b, S, SSD_HEADS, SSD_HEAD_DIM).astype(jnp.float32)
    bm = bm.reshape(b, S, SSD_GROUPS, SSD_STATE).astype(jnp.float32)
    cm = cm.reshape(b, S, SSD_GROUPS, SSD_STATE).astype(jnp.float32)
    dt = jax.nn.softplus(dt_raw.astype(jnp.float32) + dt_bias.astype(jnp.float32))
    A = -jnp.exp(a_log.astype(jnp.float32))
    y = ssd_chunked(xs, dt, A, bm, cm) + d_skip.astype(jnp.float32)[:, None] * xs
    y = y.reshape(b, S, SSD_WIDTH).astype(z.dtype)
    return rmsnorm(y * jax.nn.silu(z), norm_w)


def conformer_branch(a, g, z, conv_w, conv_b, ln_w, ln_b):
    u = a * jax.nn.sigmoid(g)
    u = causal_depthwise_conv(u, conv_w, conv_b)
    u = jax.nn.silu(layernorm(u, ln_w, ln_b))
    return u * jax.nn.silu(z)


def diff_attention_branch(q, k, v, z, cos, sin, q_norm_w, k_norm_w,
                          lq1, lk1, lq2, lk2, subln_w, lambda_init):
    b, S, _ = q.shape
    H, D, E = ATT_HEADS, ATT_HEAD_DIM, ATT_V_DIM
    q = apply_rope(rmsnorm(q.reshape(b, S, H, 2, D), q_norm_w), cos, sin)
    k = apply_rope(rmsnorm(k.reshape(b, S, H, 2, D), k_norm_w), cos, sin)
    v = v.reshape(b, S, H, E)
    lam = (jnp.exp(jnp.sum(lq1.astype(jnp.float32) * lk1.astype(jnp.float32)))
           - jnp.exp(jnp.sum(lq2.astype(jnp.float32) * lk2.astype(jnp.float32)))
           + lambda_init)
    scale = 1.0 / math.sqrt(D)
    nblk = S // Q_BLOCK
    qb = jnp.moveaxis(q.reshape(b, nblk, Q_BLOCK, H, 2, D), 1, 0)
    key_pos = jnp.arange(S)

    def one_block(args):
        q_blk, blk = args
        s = jnp.einsum("bqhcd,bkhcd->bhcqk", q_blk, k,
                       preferred_element_type=jnp.float32) * scale
        q_pos = blk * Q_BLOCK + jnp.arange(Q_BLOCK)
        mask = key_pos[None, :] <= q_pos[:, None]
        p = jax.nn.softmax(jnp.where(mask, s, -jnp.inf), axis=-1)
        w = p[:, :, 0] - lam * p[:, :, 1]
        return jnp.einsum("bhqk,bkhe->bqhe", w.astype(v.dtype), v)

    o = lax.map(one_block, (qb, jnp.arange(nblk)))
    o = jnp.moveaxis(o, 0, 1).reshape(b, S, H, E)
    o = rmsnorm(o, subln_w) * (1.0 - lambda_init)
    return o.reshape(b, S, ATT_WIDTH) * jax.nn.silu(z)


def setup_inputs(seed: int = 0) -> dict:
    key = jax.random.key(seed)
    ks = jax.random.split(key, 24)
    f32 = jnp.float32
    nrm = lambda k, shape, s: jax.random.normal(k, shape, f32) * s
    u = jax.random.uniform(ks[6], (DEPTH, SSD_HEADS), f32)
    dt0 = jnp.exp(u * (math.log(0.1) - math.log(0.001)) + math.log(0.001))
    return {
        "x": nrm(ks[0], (BATCH, SEQ, D_MODEL), 1.0),
        "norm_w": 1.0 + nrm(ks[1], (DEPTH, D_MODEL), 0.02),
        "w_in": nrm(ks[2], (DEPTH, D_MODEL, D_IN), D_MODEL ** -0.5),
        "ssd_conv_w": nrm(ks[3], (DEPTH, SSD_CONV, SSD_CONV_DIM), SSD_CONV ** -0.5),
        "ssd_conv_b": nrm(ks[4], (DEPTH, SSD_CONV_DIM), 0.02),
        "ssd_dt_bias": dt0 + jnp.log(-jnp.expm1(-dt0)),
        "ssd_a_log": jnp.log(jax.random.uniform(ks[7], (DEPTH, SSD_HEADS), f32, 1.0, 16.0)),
        "ssd_d": 1.0 + nrm(ks[8], (DEPTH, SSD_HEADS), 0.1),
        "ssd_norm_w": 1.0 + nrm(ks[9], (DEPTH, SSD_WIDTH), 0.02),
        "cfm_conv_w": nrm(ks[10], (DEPTH, CFM_KERNEL, CFM_WIDTH), CFM_KERNEL ** -0.5),
        "cfm_conv_b": nrm(ks[11], (DEPTH, CFM_WIDTH), 0.02),
        "cfm_ln_w": 1.0 + nrm(ks[12], (DEPTH, CFM_WIDTH), 0.02),
        "cfm_ln_b": nrm(ks[13], (DEPTH, CFM_WIDTH), 0.02),
        "att_q_norm_w": 1.0 + nrm(ks[14], (DEPTH, ATT_HEAD_DIM), 0.02),
        "att_k_norm_w": 1.0 + nrm(ks[15], (DEPTH, ATT_HEAD_DIM), 0.02),
        "att_lambda_q1": nrm(ks[16], (DEPTH, ATT_HEAD_DIM), 0.1),
        "att_lambda_k1": nrm(ks[17], (DEPTH, ATT_HEAD_DIM), 0.1),
        "att_lambda_q2": nrm(ks[18], (DEPTH, ATT_HEAD_DIM), 0.1),
        "att_lambda_k2": nrm(ks[19], (DEPTH, ATT_HEAD_DIM), 0.1),
        "att_subln_w": 1.0 + nrm(ks[20], (DEPTH, ATT_V_DIM), 0.02),
        "w_out": nrm(ks[21], (DEPTH, D_MIX, D_MODEL), 0.5 * D_MIX ** -0.5),
    }


def reference(x, norm_w, w_in, ssd_conv_w, ssd_conv_b, ssd_dt_bias, ssd_a_log, ssd_d,
              ssd_norm_w, cfm_conv_w, cfm_conv_b, cfm_ln_w, cfm_ln_b,
              att_q_norm_w, att_k_norm_w, att_lambda_q1, att_lambda_k1,
              att_lambda_q2, att_lambda_k2, att_subln_w, w_out):
    S = x.shape[1]
    cos, sin = rope_tables(S, ATT_HEAD_DIM)
    split_idx = [int(i) for i in np.cumsum(SPLIT_SIZES)[:-1]]
    for l in range(DEPTH):
        lambda_init = 0.8 - 0.6 * math.exp(-0.3 * l)
        h = rmsnorm(x, norm_w[l])
        proj = jnp.einsum("bsd,de->bse", h, w_in[l])
        (ssd_z, ssd_xbc, ssd_dt, cfm_a, cfm_g, cfm_z,
         att_q, att_k, att_v, att_z) = jnp.split(proj, split_idx, axis=-1)
        y_ssd = ssd_branch(ssd_z, ssd_xbc, ssd_dt, ssd_conv_w[l], ssd_conv_b[l],
                           ssd_dt_bias[l], ssd_a_log[l], ssd_d[l], ssd_norm_w[l])
        y_cfm = conformer_branch(cfm_a, cfm_g, cfm_z, cfm_conv_w[l], cfm_conv_b[l],
                                 cfm_ln_w[l], cfm_ln_b[l])
        y_att = diff_attention_branch(att_q, att_k, att_v, att_z, cos, sin,
                                      att_q_norm_w[l], att_k_norm_w[l],
                                      att_lambda_q1[l], att_lambda_k1[l],
                                      att_lambda_q2[l], att_lambda_k2[l],
                                      att_subln_w[l], lambda_init)
        y = jnp.concatenate([y_ssd, y_cfm, y_att], axis=-1)
        x = x + jnp.einsum("bse,ed->bsd", y, w_out[l])
    return x
```

```python
import math
from contextlib import ExitStack

import numpy as np
import ml_dtypes

import concourse.bass as bass
import concourse.mybir as mybir
from concourse.bass_utils import run_bass_kernel_spmd

F32 = mybir.dt.float32
BF16 = mybir.dt.bfloat16
AF = mybir.ActivationFunctionType
ALU = mybir.AluOpType
AX = mybir.AxisListType
NPBF = ml_dtypes.bfloat16

NCORES = 8
SEQ = 16384
DM = 2048
DIN = 6160
TOK = SEQ // NCORES
EPS = 1e-6
ENGS = ["pe", "dve", "act", "pool", "sp"]


class Buf:
    def __init__(self, t, name, is_ap=False):
        self.t = t
        self.name = name
        self.is_ap = is_ap
        self.w = None
        self.r = {}
        self.dsem = None
        self.dcnt = 0

    def __getitem__(self, k):
        return self.t[k]

    def ap(self):
        return self.t if self.is_ap else self.t[:]


def alias(dst, srcs):
    for s in srcs:
        if s.w is not None:
            dst.r[("w", id(s.w[0]))] = s.w
        for k, v in s.r.items():
            dst.r[("r", k)] = v


class Prog:
    def __init__(self, nc, es):
        self.nc = nc
        self.es = es
        self.sem = {e: es.enter_context(nc.semaphore("sem_" + e)) for e in ENGS}
        self.cnt = {e: 0 for e in ENGS}
        self.q = {e: [] for e in ENGS}
        self.waited = {e: {} for e in ENGS}
        self.nds = 0

    def sb(self, name, shape, dt):
        return Buf(self.es.enter_context(self.nc.sbuf_tensor(name, list(shape), dt)), name)

    def ps(self, name, shape, dt):
        return Buf(self.es.enter_context(self.nc.psum_tensor(name, list(shape), dt)), name)

    def dram(self, name, shape, dt, kind):
        return Buf(self.nc.dram_tensor(name, list(shape), dt, kind=kind).ap(), name, is_ap=True)

    def _deps(self, eng, reads, writes):
        deps = []
        for b in reads:
            if b.w is not None:
                deps.append(b.w)
        for b in writes:
            if b.w is not None:
                deps.append(b.w)
            deps.extend(b.r.values())
        out = []
        for (sem, val, src) in deps:
            if src == "pe" and eng == "pe":
                continue
            key = id(sem)
            if self.waited[eng].get(key, 0) >= val:
                continue
            self.waited[eng][key] = val
            out.append((sem, val))
        return out

    def _mark(self, ev, reads, writes):
        for b in reads:
            b.r[id(ev[0])] = ev
        for b in writes:
            b.w = ev
            b.r = {}

    def op(self, eng, fn, reads=(), writes=()):
        waits = self._deps(eng, reads, writes)
        self.cnt[eng] += 1
        ev = (self.sem[eng], self.cnt[eng], eng)
        self._mark(ev, reads, writes)
        self.q[eng].append((waits, fn, (self.sem[eng], 1)))

    def dma(self, queue, fn, reads, writes, sembuf):
        waits = self._deps(queue, reads, writes)
        if sembuf.dsem is None:
            sembuf.dsem = self.es.enter_context(self.nc.semaphore("ds%d" % self.nds))
            self.nds += 1
        sembuf.dcnt += 1
        ev = (sembuf.dsem, 16 * sembuf.dcnt, "dma")
        self._mark(ev, reads, writes)
        self.q[queue].append((waits, fn, (sembuf.dsem, 16)))

    def finish(self, outs):
        waits = []
        for b in outs:
            if b.w is not None:
                waits.append((b.w[0], b.w[1]))
        self.q["sp"].append((waits, None, None))

    def emit(self):
        nc = self.nc

        def replay(name, e):
            for (waits, fn, inc) in self.q[name]:
                for (sem, val) in waits:
                    e.wait_ge(sem, val)
                if fn is None:
                    continue
                ins = fn(e)
                ins.then_inc(inc[0], inc[1])

        with nc.Block() as block:
            @block.tensor
            def _(e):
                replay("pe", e)

            @block.vector
            def _(e):
                replay("dve", e)

            @block.scalar
            def _(e):
                replay("act", e)

            @block.gpsimd
            def _(e):
                replay("pool", e)

            @block.sync
            def _(e):
                replay("sp", e)


def bc3(ap2, n):
    p, k = ap2.shape
    return ap2.unsqueeze(2).to_broadcast([p, k, n])


def _consts():
    c = {}
    c["ident_bf"] = np.eye(128, dtype=np.float32).astype(NPBF)
    c["ident_f"] = np.eye(128, dtype=np.float32)
    c["ones_f"] = np.ones((128, 128), np.float32)
    blk = np.zeros((128, 128), np.float32)
    blk[:64, :64] = 1.0
    blk[64:, 64:] = 1.0
    c["blk_f"] = blk
    rot = np.zeros((128, 128), np.float32)
    for b0 in (0, 64):
        for m in range(32):
            rot[b0 + m + 32, b0 + m] = -1.0
        for m in range(32, 64):
            rot[b0 + m - 32, b0 + m] = 1.0
    c["rot_f"] = rot
    tri = np.triu(np.ones((128, 128), np.float32))
    c["tri_f"] = tri
    c["tri_bf"] = tri.astype(NPBF)
    return c


def _rope_tables():
    inv_freq = (1.0 / (10000.0 ** (np.arange(0, 64, 2, dtype=np.float32) / np.float32(64)))).astype(np.float32)
    pos = np.arange(SEQ, dtype=np.float32)
    ang = (pos[:, None] * inv_freq[None, :]).astype(np.float32)
    return np.cos(ang).astype(np.float32), np.sin(ang).astype(np.float32)


NT_A = 17
TA = NT_A * 128


def build_A():
    nc = bass.Bass("TRN2", target_bir_lowering=False)
    with ExitStack() as es:
        P = Prog(nc, es)
        D = lambda n, s, d, k="ExternalInput": P.dram(n, s, d, k)
        xe = D("xe", [TA, DM], F32)
        w_in = D("w_in", [DM, DIN], F32)
        normw_d = D("normw", [128, 16], F32)
        scw_d = D("scw", [128, 12, 4], F32)
        scb_d = D("scb", [128, 12], F32)
        dtb_d = D("dtb", [128, 16], F32)
        cfw_d = D("cfw", [128, 4, 31], F32)
        cfb_d = D("cfb", [128, 4], F32)
        lnw_d = D("lnw", [128, 4], F32)
        lnb_d = D("lnb", [128, 4], F32)
        qkw_d = D("qkw", [128, 2], F32)
        cos_d = D("cosT", [128, TOK], F32)
        sin_d = D("sinT", [128, TOK], F32)
        identb_d = D("ident_bf", [128, 128], BF16)
        identf_d = D("ident_f", [128, 128], F32)
        ones_d = D("ones_f", [128, 128], F32)
        blk_d = D("blk_f", [128, 128], F32)
        rot_d = D("rot_f", [128, 128], F32)
        O = "ExternalOutput"
        gzs_o = D("gz_ssdT", [1024, TOK], BF16, O)
        xbc_o = D("xbcT", [1536, TOK], BF16, O)
        dt_o = D("dt", [TOK, 16], F32, O)
        ycf_o = D("ycfmT", [512, TOK], BF16, O)
        q_o = D("qT", [512, TOK], BF16, O)
        k_o = D("kT", [512, TOK], BF16, O)
        v_o = D("v", [TOK, 512], BF16, O)
        gza_o = D("gz_attT", [512, TOK], BF16, O)

        xnT = P.sb("xnT", [128, 16, TA], BF16)
        rstd_bc = P.sb("rstd_bc", [128, TA], F32)
        ssq = P.sb("ssq", [128, NT_A], F32)
        rstd_all = P.sb("rstd_all", [128, NT_A], F32)
        normw = P.sb("normw_s", [128, 16], F32)
        scw = P.sb("scw_s", [128, 12, 4], F32)
        scb = P.sb("scb_s", [128, 12], F32)
        dtb = P.sb("dtb_s", [128, 16], F32)
        cfw = P.sb("cfw_s", [128, 4, 31], F32)
        cfb = P.sb("cfb_s", [128, 4], F32)
        lnw = P.sb("lnw_s", [128, 4], F32)
        lnb = P.sb("lnb_s", [128, 4], F32)
        qkw = P.sb("qkw_s", [128, 2], F32)
        identb = P.sb("identb_s", [128, 128], BF16)
        identf = P.sb("identf_s", [128, 128], F32)
        ones = P.sb("ones_s", [128, 128], F32)
        blk = P.sb("blk_s", [128, 128], F32)
        rot = P.sb("rot_s", [128, 128], F32)
        cc_t = es.enter_context(nc.sbuf_tensor("cc_t", [128, 4, TA], F32))
        junk = Buf(cc_t[:, 0, :], "junk", is_ap=True)
        accb = Buf(cc_t[:, 1, :], "accb", is_ap=True)
        big = [Buf(cc_t[:, 2, :], "big0", is_ap=True), Buf(cc_t[:, 3, :], "big1", is_ap=True)]
        cc = Buf(cc_t[:, :, :], "cc", is_ap=True)
        cosT = Buf(cc_t[:, 0, 0:TOK], "cosT", is_ap=True)
        sinT = Buf(cc_t[:, 1, 0:TOK], "sinT", is_ap=True)
        xb = P.sb("xb", [128, DM], BF16)
        diag = P.sb("diag", [128, 128], F32)
        wslab = [P.sb("wslab%d" % i, [128, 16, 512], BF16) for i in range(2)]
        wdt = P.sb("wdt", [128, 16, 16], BF16)
        tmp = [P.sb("tmp%d" % i, [128, 512], F32) for i in range(3)]
        ostage = [P.sb("ostage%d" % i, [128, TOK], BF16) for i in range(2)]
        vstage = [P.sb("vstage%d" % i, [128, 512], BF16) for i in range(2)]
        ubf = P.sb("ubf", [128, TA], BF16)
        dg = P.sb("dg", [128, 31, 128], BF16)
        mean_t = P.sb("mean_t", [128, 512], F32)
        rln_t = P.sb("rln_t", [128, 512], F32)
        dtbuf = P.sb("dtbuf", [128, 16, 16], F32)

        pT = P.ps("pT", [128, DM], BF16)
        pbc = P.ps("pbc", [128, 512], F32)
        pacc = [P.ps("pacc%d" % i, [128, 512], F32) for i in range(3)]
        paux = [P.ps("paux%d" % i, [128, 512], F32) for i in range(2)]

        def load(dst, src, q="sp"):
            P.dma(q, lambda e: e.dma_start(out=dst.ap(), in_=src.ap()), [src], [dst], dst)

        for dst, src in [(normw, normw_d), (scw, scw_d), (scb, scb_d), (dtb, dtb_d), (cfw, cfw_d), (cfb, cfb_d),
                         (lnw, lnw_d), (lnb, lnb_d), (qkw, qkw_d), (identb, identb_d),
                         (identf, identf_d), (ones, ones_d), (blk, blk_d), (rot, rot_d)]:
            load(dst, src)
        P.op("dve", lambda e: e.tensor_scalar(out=qkw[:, 0:1], in0=qkw[:, 0:1], scalar1=0.125, scalar2=None,
                                              op0=ALU.mult), [qkw], [qkw])

        GCOLS = [0, 512, 1024, 1536, 2048, 2576, 3088, 3600, 4112, 4624, 5136, 5648]
        slab_state = {"next": 0}

        def issue_slab(gi):
            slot = wslab[gi % 2]
            c0 = GCOLS[gi]
            for part in range(4):
                src = w_in.t[part * 512:(part + 1) * 512, c0:c0 + 512].rearrange("(k p) c -> p k c", p=128)
                dst = slot[:, part * 4:(part + 1) * 4, :]
                P.dma("pool", lambda e, s=src, d=dst: e.dma_start(out=d, in_=s), [w_in], [slot], slot)

        def need_slab(gi):
            while slab_state["next"] <= min(gi + 1, len(GCOLS) - 1):
                issue_slab(slab_state["next"])
                slab_state["next"] += 1
            return wslab[gi % 2]

        P.dma("pool", lambda e: e.dma_start(out=wdt[:, :, :],
                                            in_=w_in.t[:, 2560:2576].rearrange("(k p) c -> p k c", p=128)),
              [w_in], [wdt], wdt)
        need_slab(0)

        for t in range(NT_A):
            xt = big[t % 2]
            P.dma("sp", lambda e, t=t, xt=xt: e.dma_start(out=xt[:, 0:DM], in_=xe.t[t * 128:(t + 1) * 128, :]),
                  [xe], [xt], xt)
            P.op("pool", lambda e, xt=xt: e.tensor_copy(out=xb[:, :], in_=xt[:, 0:DM]), [xt], [xb])
            P.op("act", lambda e, xt=xt: e.activation(out=junk[:, 0:DM], in_=xt[:, 0:DM], func=AF.Square),
                 [xt], [junk])
            P.op("dve", lambda e, t=t: e.tensor_reduce(out=ssq[:, t:t + 1], in_=junk[:, 0:DM], axis=AX.X,
                                                       op=ALU.add), [junk], [ssq])
            for k in range(16):
                P.op("pe", lambda e, k=k: e.transpose(out=pT[:, k * 128:(k + 1) * 128],
                                                      in_=xb[:, k * 128:(k + 1) * 128], identity=identb[:, :]),
                     [xb, identb], [pT])
            P.op("dve", lambda e, t=t: e.tensor_tensor(
                out=xnT[:, :, t * 128:(t + 1) * 128],
                in0=pT[:, :].rearrange("p (k t) -> p k t", k=16),
                in1=bc3(normw[:, :], 128), op=ALU.mult), [pT, normw], [xnT])
        P.op("act", lambda e: e.activation(out=rstd_all[:, :], in_=ssq[:, :], func=AF.Sqrt, bias=EPS,
                                           scale=1.0 / DM), [ssq], [rstd_all])
        P.op("dve", lambda e: e.reciprocal(out=rstd_all[:, :], in_=rstd_all[:, :]), [rstd_all], [rstd_all])
        for t in range(NT_A):
            P.op("dve", lambda e, t=t: e.tensor_scalar(out=diag[:, :], in0=identf[:, :], scalar1=rstd_all[:, t:t + 1],
                                                       scalar2=None, op0=ALU.mult), [identf, rstd_all], [diag])
            P.op("pe", lambda e: e.matmul(out=pbc[:, 0:128], lhsT=ones[:, :], rhs=diag[:, :], start=True, stop=True),
                 [ones, diag], [pbc])
            P.op("act", lambda e, t=t: e.copy(out=rstd_bc[:, t * 128:(t + 1) * 128], in_=pbc[:, 0:128]),
                 [pbc], [rstd_bc])

        rr = {"pacc": 0, "ost": 0, "vst": 0}

        def fm_mm(slab, j, tok0, n):
            pb = pacc[rr["pacc"] % 3]
            rr["pacc"] += 1
            for k in range(16):
                P.op("pe", lambda e, k=k, pb=pb: e.matmul(out=pb[:, 0:n], lhsT=slab[:, k, j * 128:(j + 1) * 128],
                                                           rhs=xnT[:, k, tok0:tok0 + n], start=(k == 0),
                                                           stop=(k == 15)), [slab, xnT], [pb])
            return pb

        OWN = [(128 + 512 * g, 512, 512 * g) for g in range(4)]
        ALLT = [(0, 128, -128)] + OWN

        def store_rows(ost, dram, row0):
            P.dma("sp", lambda e: e.dma_start(out=dram.t[row0:row0 + 128, :], in_=ost[:, :]), [ost], [dram], ost)

        def gate_group(gi, dram, row_base):
            slab = need_slab(gi)
            for j in range(4):
                ost = ostage[rr["ost"] % 2]
                rr["ost"] += 1
                for (x0, n, o0) in OWN:
                    pb = fm_mm(slab, j, x0, n)
                    P.op("dve", lambda e, pb=pb, x0=x0: e.tensor_tensor(out=tmp[0][:, :], in0=pb[:, :],
                                                                         in1=rstd_bc[:, x0:x0 + 512], op=ALU.mult),
                         [pb, rstd_bc], [tmp[0]])
                    P.op("act", lambda e, ost=ost, o0=o0: e.activation(out=ost[:, o0:o0 + 512], in_=tmp[0][:, :],
                                                                       func=AF.Silu), [tmp[0]], [ost])
                store_rows(ost, dram, row_base + j * 128)

        gate_group(0, gzs_o, 0)
        gate_group(1, gzs_o, 512)

        for g3 in range(3):
            slab = need_slab(2 + g3)
            for j in range(4):
                c = g3 * 4 + j
                xr = junk
                acc = accb
                for (x0, n, o0) in ALLT:
                    pb = fm_mm(slab, j, x0, n)
                    P.op("dve", lambda e, pb=pb, x0=x0, n=n: e.tensor_tensor(out=xr[:, x0:x0 + n], in0=pb[:, 0:n],
                                                                              in1=rstd_bc[:, x0:x0 + n], op=ALU.mult),
                         [pb, rstd_bc], [xr])
                P.op("dve", lambda e, c=c: e.tensor_scalar(out=acc[:, 0:TOK], in0=xr[:, 128:TA],
                                                           scalar1=scw[:, c, 3:4], scalar2=scb[:, c:c + 1],
                                                           op0=ALU.mult, op1=ALU.add), [xr, scw, scb], [acc])
                for s in (1, 2, 3):
                    P.op("dve", lambda e, c=c, s=s: e.scalar_tensor_tensor(
                        out=acc[:, 0:TOK], in0=xr[:, 128 - s:TA - s], scalar=scw[:, c, 3 - s:4 - s],
                        in1=acc[:, 0:TOK], op0=ALU.mult, op1=ALU.add), [xr, scw, acc], [acc])
                ost = ostage[rr["ost"] % 2]
                rr["ost"] += 1
                P.op("act", lambda e, ost=ost: e.activation(out=ost[:, :], in_=acc[:, 0:TOK], func=AF.Silu),
                     [acc], [ost])
                store_rows(ost, xbc_o, c * 128)

        for t in range(1, NT_A):
            pb = paux[t % 2]
            for k in range(16):
                P.op("pe", lambda e, k=k, pb=pb, t=t: e.matmul(out=pb[:, 0:16], lhsT=xnT[:, k, t * 128:(t + 1) * 128],
                                                               rhs=wdt[:, k, :], start=(k == 0), stop=(k == 15)),
                     [xnT, wdt], [pb])
            P.op("dve", lambda e, pb=pb, t=t: e.scalar_tensor_tensor(
                out=dtbuf[:, t - 1, :], in0=pb[:, 0:16], scalar=rstd_all[:, t:t + 1], in1=dtb[:, :],
                op0=ALU.mult, op1=ALU.add), [pb, rstd_all, dtb], [dtbuf])
        P.op("act", lambda e: e.activation(out=dtbuf[:, :, :], in_=dtbuf[:, :, :], func=AF.Exp), [dtbuf], [dtbuf])
        P.op("act", lambda e: e.activation(out=dtbuf[:, :, :], in_=dtbuf[:, :, :], func=AF.Ln, bias=1.0, scale=1.0),
             [dtbuf], [dtbuf])
        P.dma("sp", lambda e: e.dma_start(out=dt_o.t.rearrange("(t p) h -> p t h", p=128), in_=dtbuf[:, :, :]),
              [dtbuf], [dt_o], dtbuf)

        slab_a = need_slab(5)
        slab_g = wslab[6 % 2]
        alias(cc, [junk, accb, big[0], big[1]])
        for i in range(4):
            for (x0, n, o0) in ALLT:
                pa = fm_mm(slab_a, i, x0, n)
                P.op("dve", lambda e, pa=pa, x0=x0, n=n: e.tensor_tensor(out=tmp[0][:, 0:n], in0=pa[:, 0:n],
                                                                          in1=rstd_bc[:, x0:x0 + n], op=ALU.mult),
                     [pa, rstd_bc], [tmp[0]])
                pg = fm_mm(slab_g, i, x0, n)
                P.op("dve", lambda e, pg=pg, x0=x0, n=n: e.tensor_tensor(out=tmp[1][:, 0:n], in0=pg[:, 0:n],
                                                                          in1=rstd_bc[:, x0:x0 + n], op=ALU.mult),
                     [pg, rstd_bc], [tmp[1]])
                P.op("act", lambda e, n=n: e.activation(out=tmp[1][:, 0:n], in_=tmp[1][:, 0:n], func=AF.Sigmoid),
                     [tmp[1]], [tmp[1]])
                P.op("dve", lambda e, x0=x0, n=n: e.tensor_tensor(out=ubf[:, x0:x0 + n], in0=tmp[0][:, 0:n],
                                                                   in1=tmp[1][:, 0:n], op=ALU.mult),
                     [tmp[0], tmp[1]], [ubf])
            for k in range(31):
                P.op("dve", lambda e, i=i, k=k: e.tensor_scalar(out=dg[:, k, :], in0=identf[:, :],
                                                                scalar1=cfw[:, i, k:k + 1], scalar2=None,
                                                                op0=ALU.mult), [identf, cfw], [dg])
            for (x0, n, o0) in OWN:
                pb = pacc[rr["pacc"] % 3]
                rr["pacc"] += 1
                for k in range(31):
                    P.op("pe", lambda e, pb=pb, k=k, x0=x0: e.matmul(
                        out=pb[:, :], lhsT=dg[:, k, :], rhs=ubf[:, x0 - 30 + k:x0 - 30 + k + 512],
                        start=(k == 0), stop=(k == 30)), [dg, ubf], [pb])
                P.op("dve", lambda e, pb=pb, i=i, o0=o0: e.tensor_scalar(out=cc[:, i, o0:o0 + 512], in0=pb[:, :],
                                                                          scalar1=cfb[:, i:i + 1], scalar2=None,
                                                                          op0=ALU.add), [pb, cfb], [cc])
        for (x0, n, o0) in OWN:
            for i in range(4):
                P.op("act", lambda e, i=i, o0=o0: e.activation(out=tmp[2][:, :], in_=cc[:, i, o0:o0 + 512],
                                                               func=AF.Square), [cc], [tmp[2]])
                P.op("pe", lambda e, i=i, o0=o0: e.matmul(out=paux[0][:, :], lhsT=ones[:, :], rhs=cc[:, i, o0:o0 + 512],
                                                          start=(i == 0), stop=(i == 3)), [ones, cc], [paux[0]])
                P.op("pe", lambda e, i=i: e.matmul(out=paux[1][:, :], lhsT=ones[:, :], rhs=tmp[2][:, :],
                                                   start=(i == 0), stop=(i == 3)), [ones, tmp[2]], [paux[1]])
            P.op("dve", lambda e: e.tensor_scalar(out=mean_t[:, :], in0=paux[0][:, :], scalar1=1.0 / 512,
                                                  scalar2=None, op0=ALU.mult), [paux[0]], [mean_t])
            P.op("dve", lambda e: e.tensor_tensor(out=tmp[0][:, :], in0=mean_t[:, :], in1=mean_t[:, :], op=ALU.mult),
                 [mean_t], [tmp[0]])
            P.op("dve", lambda e: e.scalar_tensor_tensor(out=tmp[0][:, :], in0=paux[1][:, :], scalar=1.0 / 512,
                                                         in1=tmp[0][:, :], op0=ALU.mult, op1=ALU.subtract),
                 [paux[1], tmp[0]], [tmp[0]])
            P.op("act", lambda e: e.activation(out=tmp[0][:, :], in_=tmp[0][:, :], func=AF.Ln, bias=EPS, scale=1.0),
                 [tmp[0]], [tmp[0]])
            P.op("act", lambda e: e.activation(out=rln_t[:, :], in_=tmp[0][:, :], func=AF.Exp, scale=-0.5),
                 [tmp[0]], [rln_t])
            for i in range(4):
                P.op("dve", lambda e, i=i, o0=o0: e.tensor_tensor(out=cc[:, i, o0:o0 + 512], in0=cc[:, i, o0:o0 + 512],
                                                                   in1=mean_t[:, :], op=ALU.subtract),
                     [cc, mean_t], [cc])
                P.op("dve", lambda e, i=i, o0=o0: e.tensor_tensor(out=cc[:, i, o0:o0 + 512], in0=cc[:, i, o0:o0 + 512],
                                                                   in1=rln_t[:, :], op=ALU.mult),
                     [cc, rln_t], [cc])
        slab = need_slab(7)
        for i in range(4):
            ost = ostage[rr["ost"] % 2]
            rr["ost"] += 1
            for (x0, n, o0) in OWN:
                pz = fm_mm(slab, i, x0, n)
                P.op("dve", lambda e, pz=pz, x0=x0: e.tensor_tensor(out=tmp[0][:, :], in0=pz[:, :],
                                                                     in1=rstd_bc[:, x0:x0 + 512], op=ALU.mult),
                     [pz, rstd_bc], [tmp[0]])
                P.op("act", lambda e: e.activation(out=tmp[0][:, :], in_=tmp[0][:, :], func=AF.Silu),
                     [tmp[0]], [tmp[0]])
                P.op("act", lambda e, i=i, o0=o0: e.activation(out=tmp[1][:, :], in_=cc[:, i, o0:o0 + 512], func=AF.Silu,
                                                               bias=lnb[:, i:i + 1], scale=lnw[:, i:i + 1]),
                     [cc, lnb, lnw], [tmp[1]])
                P.op("dve", lambda e, ost=ost, o0=o0: e.tensor_tensor(out=ost[:, o0:o0 + 512], in0=tmp[1][:, :],
                                                                       in1=tmp[0][:, :], op=ALU.mult),
                     [tmp[0], tmp[1]], [ost])
            store_rows(ost, ycf_o, i * 128)

        alias(cosT, [cc])
        alias(sinT, [cc])
        P.dma("sp", lambda e: e.dma_start(out=cosT.ap(), in_=cos_d.ap()), [cos_d], [cosT], cosT)
        P.dma("sp", lambda e: e.dma_start(out=sinT.ap(), in_=sin_d.ap()), [sin_d], [sinT], sinT)

        for (gi, dram, wc) in ((8, q_o, 0), (9, k_o, 1)):
            slab = need_slab(gi)
            for i in range(4):
                ost = ostage[rr["ost"] % 2]
                rr["ost"] += 1
                for (x0, n, o0) in OWN:
                    pq = fm_mm(slab, i, x0, n)
                    P.op("dve", lambda e, pq=pq, x0=x0: e.tensor_tensor(out=tmp[0][:, :], in0=pq[:, :],
                                                                         in1=rstd_bc[:, x0:x0 + 512], op=ALU.mult),
                         [pq, rstd_bc], [tmp[0]])
                    P.op("act", lambda e: e.activation(out=tmp[1][:, :], in_=tmp[0][:, :], func=AF.Square),
                         [tmp[0]], [tmp[1]])
                    P.op("pe", lambda e: e.matmul(out=paux[0][:, :], lhsT=blk[:, :], rhs=tmp[1][:, :], start=True,
                                                  stop=True), [blk, tmp[1]], [paux[0]])
                    P.op("act", lambda e: e.activation(out=tmp[1][:, :], in_=paux[0][:, :], func=AF.Ln, bias=EPS,
                                                       scale=1.0 / 64), [paux[0]], [tmp[1]])
                    P.op("act", lambda e: e.activation(out=tmp[1][:, :], in_=tmp[1][:, :], func=AF.Exp, scale=-0.5),
                         [tmp[1]], [tmp[1]])
                    P.op("dve", lambda e, wc=wc: e.scalar_tensor_tensor(out=tmp[0][:, :], in0=tmp[0][:, :],
                                                                        scalar=qkw[:, wc:wc + 1], in1=tmp[1][:, :],
                                                                        op0=ALU.mult, op1=ALU.mult),
                         [tmp[0], tmp[1], qkw], [tmp[0]])
                    P.op("pe", lambda e: e.matmul(out=paux[1][:, :], lhsT=rot[:, :], rhs=tmp[0][:, :], start=True,
                                                  stop=True), [rot, tmp[0]], [paux[1]])
                    P.op("dve", lambda e, o0=o0: e.tensor_tensor(out=tmp[1][:, :], in0=tmp[0][:, :],
                                                                 in1=cosT[:, o0:o0 + 512], op=ALU.mult),
                         [tmp[0], cosT], [tmp[1]])
                    P.op("dve", lambda e, o0=o0: e.tensor_tensor(out=tmp[2][:, :], in0=paux[1][:, :],
                                                                 in1=sinT[:, o0:o0 + 512], op=ALU.mult),
                         [paux[1], sinT], [tmp[2]])
                    P.op("dve", lambda e, ost=ost, o0=o0: e.tensor_tensor(out=ost[:, o0:o0 + 512], in0=tmp[1][:, :],
                                                                           in1=tmp[2][:, :], op=ALU.add),
                         [tmp[1], tmp[2]], [ost])
                store_rows(ost, dram, i * 128)

        slab = need_slab(10)
        for t in range(1, NT_A):
            pb = pacc[rr["pacc"] % 3]
            rr["pacc"] += 1
            for k in range(16):
                P.op("pe", lambda e, k=k, pb=pb, t=t: e.matmul(out=pb[:, :], lhsT=xnT[:, k, t * 128:(t + 1) * 128],
                                                               rhs=slab[:, k, :], start=(k == 0), stop=(k == 15)),
                     [xnT, slab], [pb])
            vs = vstage[rr["vst"] % 2]
            rr["vst"] += 1
            P.op("act", lambda e, pb=pb, vs=vs, t=t: e.activation(out=vs[:, :], in_=pb[:, :], func=AF.Copy,
                                                                  scale=rstd_all[:, t:t + 1]),
                 [pb, rstd_all], [vs])
            P.dma("sp", lambda e, vs=vs, t=t: e.dma_start(out=v_o.t[(t - 1) * 128:t * 128, :], in_=vs[:, :]),
                  [vs], [v_o], vs)

        gate_group(11, gza_o, 0)

        P.finish([gzs_o, xbc_o, dt_o, ycf_o, q_o, k_o, v_o, gza_o])
        P.emit()
    return nc


_NC_CACHE = {}


def _get(name, fn):
    if name not in _NC_CACHE:
        _NC_CACHE[name] = fn()
    return _NC_CACHE[name]


def _chunk_cols(vec, nchunk):
    return np.ascontiguousarray(np.asarray(vec, np.float32).reshape(nchunk, 128).T)


def run_A(x_full, inp, l, consts, cos, sin):
    nc = _get("A", build_A)
    in_maps = []
    scw = np.ascontiguousarray(np.asarray(inp["ssd_conv_w"][l], np.float32).T.reshape(12, 128, 4).transpose(1, 0, 2))
    cfw = np.ascontiguousarray(np.asarray(inp["cfm_conv_w"][l], np.float32).T.reshape(4, 128, 31).transpose(1, 0, 2))
    qkw = np.ascontiguousarray(np.stack([np.tile(np.asarray(inp["att_q_norm_w"][l], np.float32), 2),
                                         np.tile(np.asarray(inp["att_k_norm_w"][l], np.float32), 2)], axis=1))
    shared = {
        "w_in": np.ascontiguousarray(inp["w_in"][l]),
        "normw": _chunk_cols(inp["norm_w"][l], 16),
        "scw": scw, "scb": _chunk_cols(inp["ssd_conv_b"][l], 12),
        "dtb": np.ascontiguousarray(np.tile(np.asarray(inp["ssd_dt_bias"][l], np.float32)[None, :], (128, 1))),
        "cfw": cfw, "cfb": _chunk_cols(inp["cfm_conv_b"][l], 4),
        "lnw": _chunk_cols(inp["cfm_ln_w"][l], 4), "lnb": _chunk_cols(inp["cfm_ln_b"][l], 4),
        "qkw": qkw,
        "ident_bf": consts["ident_bf"], "ident_f": consts["ident_f"], "ones_f": consts["ones_f"],
        "blk_f": consts["blk_f"], "rot_f": consts["rot_f"],
    }
    for c in range(NCORES):
        t0 = c * TOK
        xe = np.zeros((TA, DM), np.float32)
        xe[128:] = x_full[t0:t0 + TOK]
        if c > 0:
            xe[:128] = x_full[t0 - 128:t0]
        cT = np.ascontiguousarray(np.tile(cos[t0:t0 + TOK].T, (4, 1)))
        sT = np.ascontiguousarray(np.tile(sin[t0:t0 + TOK].T, (4, 1)))
        m = dict(shared)
        m.update({"xe": xe, "cosT": cT, "sinT": sT})
        in_maps.append(m)
    res = run_bass_kernel_spmd(nc, in_maps, core_ids=list(range(NCORES)))
    return res.results


NQT = SEQ // 512
NCH = SEQ // 128
SEG = 2048


def build_B():
    nc = bass.Bass("TRN2", target_bir_lowering=False)
    with ExitStack() as es:
        P = Prog(nc, es)
        D = lambda n, s, d, k="ExternalInput": P.dram(n, s, d, k)
        q_d = D("qT", [64, SEQ], BF16)
        k_d = D("kT", [64, SEQ], BF16)
        v_d = D("v", [SEQ, 128], BF16)
        xs_d = D("xsT", [64, 2, SEQ], BF16)
        b_d = D("BT", [128, SEQ], BF16)
        c_d = D("CT", [128, SEQ], BF16)
        dt_d = D("dt2", [SEQ, 2], F32)
        xtok_d = D("xs_tok", [SEQ, 128], BF16)
        btok_d = D("b_tok", [SEQ, 128], BF16)
        alog_d = D("alog", [128, 2], F32)
        dcol_d = D("dcol", [64, 2], F32)
        onesb_d = D("ones_bf", [128, 128], BF16)
        trib_d = D("tri_bf", [128, 128], BF16)
        identb_d = D("ident_bf", [128, 128], BF16)
        onesf_d = D("ones_f", [128, 128], F32)
        trif_d = D("tri_f", [128, 128], F32)
        o_o = D("oT", [128, SEQ], F32, "ExternalOutput")
        y_o = D("yT", [64, 2, SEQ], F32, "ExternalOutput")

        qs = P.sb("qs", [64, SEQ], BF16)
        ks = P.sb("ks", [64, SEQ], BF16)
        vs = P.sb("vs", [128, NCH, 128], BF16)
        onesb = P.sb("onesb", [128, 128], BF16)
        trib = P.sb("trib", [128, 128], BF16)
        identb = P.sb("identb", [128, 128], BF16)
        onesf = P.sb("onesf", [128, 128], F32)
        trif = P.sb("trif", [128, 128], F32)
        alog = P.sb("alog_s", [128, 2], F32)
        dcol = P.sb("dcol_s", [64, 2], F32)
        dts = P.sb("dts", [128, NCH, 2], F32)
        ptile = [P.sb("ptile%d" % i, [128, 512], BF16) for i in range(3)]
        rl = P.sb("rl", [128, 512], F32)
        ost = [P.sb("ost%d" % i, [128, 512], F32) for i in range(2)]
        xseg = [P.sb("xseg%d" % i, [64, 2, SEG], BF16) for i in range(2)]
        bseg = [P.sb("bseg%d" % i, [128, SEG], BF16) for i in range(2)]
        cseg = [P.sb("cseg%d" % i, [128, SEG], BF16) for i in range(2)]
        yst = [P.sb("yst%d" % i, [64, 2, SEG], F32) for i in range(2)]
        xtseg = [P.sb("xtseg%d" % i, [128, 16, 128], BF16) for i in range(2)]
        btseg = [P.sb("btseg%d" % i, [128, 16, 128], BF16) for i in range(2)]
        a_t = [P.sb("a_t%d" % i, [128, 2], F32) for i in range(2)]
        acs = [P.sb("acs%d" % i, [128, 2], F32) for i in range(2)]
        dsta = [P.sb("dsta%d" % i, [128, 2], F32) for i in range(2)]
        X = [P.sb("X%d" % i, [128, 2, 128], F32) for i in range(2)]
        D1 = [P.sb("D1%d" % i, [128, 2, 128], F32) for i in range(2)]
        ER = [P.sb("ER%d" % i, [128, 2, 128], F32) for i in range(2)]
        Gm = [P.sb("Gm%d" % i, [128, 128], F32) for i in range(2)]
        LT = [P.sb("LT%d" % i, [128, 2, 128], BF16) for i in range(2)]
        Ce = [P.sb("Ce%d" % i, [128, 2, 128], BF16) for i in range(2)]
        xdt = [P.sb("xdt%d" % i, [128, 2, 64], BF16) for i in range(2)]
        xw = [P.sb("xw%d" % i, [128, 2, 64], BF16) for i in range(2)]
        Bt = P.sb("Bt", [128, 128], BF16)
        S = P.sb("S", [128, 2, 64], F32)
        Sbf = [P.sb("Sbf%d" % i, [128, 2, 64], BF16) for i in range(4)]

        ps_s = [P.ps("ps_s%d" % i, [128, 512], F32) for i in range(2)]
        po = P.ps("po", [128, 512], F32)
        pl = P.ps("pl", [128, 512], F32)
        bankA = [P.ps("bankA%d" % i, [128, 512], F32) for i in range(2)]
        bankB = [P.ps("bankB%d" % i, [128, 512], F32) for i in range(2)]

        def load(dst, src, q="sp"):
            P.dma(q, lambda e: e.dma_start(out=dst.ap(), in_=src.ap()), [src], [dst], dst)

        for dst, src in [(onesb, onesb_d), (trib, trib_d), (identb, identb_d), (onesf, onesf_d), (trif, trif_d),
                         (alog, alog_d), (dcol, dcol_d), (qs, q_d), (ks, k_d)]:
            load(dst, src)
        P.dma("sp", lambda e: e.dma_start(out=dts[:, :, :], in_=dt_d.t.rearrange("(j t) h -> t j h", t=128)),
              [dt_d], [dts], dts)
        for part in range(8):
            P.dma("pool", lambda e, part=part: e.dma_start(
                out=vs[:, part * 16:(part + 1) * 16, :],
                in_=v_d.t[part * 2048:(part + 1) * 2048, :].rearrange("(b p) e -> p b e", p=128)),
                [v_d], [vs], vs)
        P.op("act", lambda e: e.activation(out=alog[:, :], in_=alog[:, :], func=AF.Exp), [alog], [alog])
        P.op("dve", lambda e: e.tensor_scalar(out=alog[:, :], in0=alog[:, :], scalar1=-1.0, scalar2=None, op0=ALU.mult),
             [alog], [alog])
        P.op("dve", lambda e: e.memset(S[:, :, :], 0.0), [], [S])
        P.op("dve", lambda e: e.memset(Sbf[0][:, :, :], 0.0), [], [Sbf[0]])

        st = {"pt": 0, "ost": 0}
        _DM = "block"
        _DN = 4
        pend = []

        def Q(eng, fn, reads=(), writes=()):
            pend.append(("op", eng, fn, reads, writes))

        def QD(queue, fn, reads, writes, sembuf):
            pend.append(("dma", queue, fn, reads, writes, sembuf))

        def drain(n):
            while n > 0 and pend:
                it = pend.pop(0)
                if it[0] == "op":
                    P.op(it[1], it[2], it[3], it[4])
                else:
                    P.dma(it[1], it[2], it[3], it[4], it[5])
                n -= 1

        def attn_qtile(qi):
            nkv = 4 * (qi + 1)
            q0 = qi * 512
            for j in range(nkv):
                r = j - 4 * qi
                c0 = 128 * r if r > 0 else 0
                pss = ps_s[j % 2]
                pt = ptile[st["pt"] % 3]
                st["pt"] += 1
                P.op("pe", lambda e, pss=pss, j=j, c0=c0: e.matmul(out=pss[:, c0:512], lhsT=ks[:, j * 128:(j + 1) * 128],
                                                                  rhs=qs[:, q0 + c0:q0 + 512], start=True, stop=True),
                     [ks, qs], [pss])
                P.op("act", lambda e, pss=pss, pt=pt, c0=c0: e.activation(out=pt[:, c0:512], in_=pss[:, c0:512],
                                                                          func=AF.Exp), [pss], [pt])
                if r >= 0:
                    P.op("pool", lambda e, pt=pt, c0=c0: e.tensor_tensor(out=pt[:, c0:c0 + 128], in0=pt[:, c0:c0 + 128],
                                                                         in1=trib[:, :], op=ALU.mult), [pt, trib], [pt])
                P.op("pe", lambda e, pt=pt, j=j, c0=c0: e.matmul(out=po[:, c0:512], lhsT=vs[:, j, :], rhs=pt[:, c0:512],
                                                                 start=(j == 0), stop=(j == nkv - 1)), [vs, pt], [po])
                P.op("pe", lambda e, pt=pt, j=j, c0=c0: e.matmul(out=pl[:, c0:512], lhsT=onesb[:, :], rhs=pt[:, c0:512],
                                                                 start=(j == 0), stop=(j == nkv - 1)), [onesb, pt], [pl])
                if _DM == 'block':
                    drain(_DN)
            o = ost[st["ost"] % 2]
            st["ost"] += 1
            P.op("dve", lambda e: e.reciprocal(out=rl[:, :], in_=pl[:, :]), [pl], [rl])
            P.op("dve", lambda e, o=o: e.tensor_tensor(out=o[:, :], in0=po[:, :], in1=rl[:, :], op=ALU.mult),
                 [po, rl], [o])
            P.dma("sp", lambda e, o=o: e.dma_start(out=o_o.t[:, q0:q0 + 512], in_=o[:, :]), [o], [o_o], o)
            if _DM == 'tile':
                drain(1 << 30)

        def load_seg(sg, dma=None):
            dma = dma or P.dma
            s0 = sg * SEG
            dma("sp", lambda e: e.dma_start(out=xseg[sg % 2][:, :, :], in_=xs_d.t[:, :, s0:s0 + SEG]),
                [xs_d], [xseg[sg % 2]], xseg[sg % 2])
            dma("sp", lambda e: e.dma_start(out=bseg[sg % 2][:, :], in_=b_d.t[:, s0:s0 + SEG]),
                [b_d], [bseg[sg % 2]], bseg[sg % 2])
            dma("sp", lambda e: e.dma_start(out=cseg[sg % 2][:, :], in_=c_d.t[:, s0:s0 + SEG]),
                [c_d], [cseg[sg % 2]], cseg[sg % 2])
            dma("sp", lambda e: e.dma_start(out=xtseg[sg % 2][:, :, :],
                                            in_=xtok_d.t[s0:s0 + SEG, :].rearrange("(j t) c -> t j c", t=128)),
                [xtok_d], [xtseg[sg % 2]], xtseg[sg % 2])
            dma("sp", lambda e: e.dma_start(out=btseg[sg % 2][:, :, :],
                                            in_=btok_d.t[s0:s0 + SEG, :].rearrange("(j t) c -> t j c", t=128)),
                [btok_d], [btseg[sg % 2]], btseg[sg % 2])

        def ssd_chunk(j):
            H1, H2 = [], []
            cur = [H1]

            def Q(eng, fn, reads=(), writes=()):
                cur[0].append(("op", eng, fn, reads, writes))

            def QD(queue, fn, reads, writes, sembuf):
                cur[0].append(("dma", queue, fn, reads, writes, sembuf))

            p = j % 2
            sg = j // 16
            c = (j % 16) * 128
            jj = j % 16
            xg, bg, cg, yg = xseg[sg % 2], bseg[sg % 2], cseg[sg % 2], yst[sg % 2]
            xtg, btg = xtseg[sg % 2], btseg[sg % 2]
            bA, bB = bankA[p], bankB[p]
            a_, acs_, dsta_, X_, D1_, ER_, Gm_, LT_, Ce_, xdt_, xw_ = (a_t[p], acs[p], dsta[p], X[p], D1[p], ER[p],
                                                                      Gm[p], LT[p], Ce[p], xdt[p], xw[p])
            sb_in, sb_out = Sbf[j % 4], Sbf[(j + 1) % 4]
            R3 = lambda: bA[:, 0:256].rearrange("p (h l) -> p h l", h=2)
            Q("dve", lambda e: e.tensor_tensor(out=a_[:, :], in0=dts[:, j, :], in1=alog[:, :], op=ALU.mult),
              [dts, alog], [a_])
            Q("pe", lambda e: e.matmul(out=bB[:, 256:258], lhsT=trif[:, :], rhs=a_[:, :], start=True, stop=True),
              [trif, a_], [bB])
            Q("dve", lambda e: e.tensor_copy(out=acs_[:, :], in_=bB[:, 256:258]), [bB], [acs_])
            for h in range(2):
                Q("dve", lambda e, h=h: e.tensor_scalar(out=X_[:, h, :], in0=trif[:, :], scalar1=a_[:, h:h + 1],
                                                        scalar2=None, op0=ALU.mult), [trif, a_], [X_])
            Q("pe", lambda e: e.matmul(out=bA[:, 0:256], lhsT=onesf[:, :],
                                       rhs=X_[:, :, :].rearrange("p h l -> p (h l)"), start=True, stop=True),
              [onesf, X_], [bA])
            Q("pe", lambda e: e.matmul(out=bB[:, 0:128], lhsT=bg[:, c:c + 128], rhs=cg[:, c:c + 128], start=True,
                                       stop=True), [bg, cg], [bB])
            Q("dve", lambda e: e.tensor_tensor(out=Gm_[:, :], in0=bB[:, 0:128], in1=trif[:, :], op=ALU.mult),
              [bB, trif], [Gm_])
            Q("dve", lambda e: e.tensor_tensor(out=xdt_[:, :, :],
                                               in0=xtg[:, jj, :].rearrange("p (h d) -> p h d", h=2),
                                               in1=bc3(dts[:, j, :], 64), op=ALU.mult), [xtg, dts], [xdt_])
            for h in range(2):
                Q("dve", lambda e, h=h: e.tensor_scalar(out=D1_[:, h, :], in0=bA[:, h * 128:(h + 1) * 128],
                                                        scalar1=acs_[:, h:h + 1], scalar2=0.0, op0=ALU.subtract,
                                                        op1=ALU.min), [bA, acs_], [D1_])
            Q("act", lambda e: e.activation(out=D1_[:, :, :], in_=D1_[:, :, :], func=AF.Exp), [D1_], [D1_])
            Q("act", lambda e: e.activation(out=ER_[:, :, :], in_=R3(), func=AF.Exp), [bA], [ER_])
            Q("dve", lambda e: e.tensor_tensor(out=dsta_[:, :], in0=R3()[:, :, 127], in1=acs_[:, :], op=ALU.subtract),
              [bA, acs_], [dsta_])
            Q("act", lambda e: e.activation(out=dsta_[:, :], in_=dsta_[:, :], func=AF.Exp), [dsta_], [dsta_])
            Q("dve", lambda e: e.tensor_tensor(out=LT_[:, :, :], in0=D1_[:, :, :],
                                               in1=Gm_[:, :].unsqueeze(1).to_broadcast([128, 2, 128]), op=ALU.mult),
              [D1_, Gm_], [LT_])
            Q("dve", lambda e: e.tensor_tensor(out=Ce_[:, :, :], in0=ER_[:, :, :],
                                               in1=cg[:, c:c + 128].unsqueeze(1).to_broadcast([128, 2, 128]),
                                               op=ALU.mult), [ER_, cg], [Ce_])
            Q("dve", lambda e: e.tensor_tensor(out=xw_[:, :, :], in0=xdt_[:, :, :], in1=bc3(dsta_[:, :], 64),
                                               op=ALU.mult), [xdt_, dsta_], [xw_])
            Q("pe", lambda e: e.matmul(out=bB[:, 128:256], lhsT=btg[:, jj, :],
                                       rhs=xw_[:, :, :].rearrange("p h d -> p (h d)"), start=True, stop=True),
              [btg, xw_], [bB])
            cur[0] = H2
            for h in range(2):
                Q("pe", lambda e, h=h: e.matmul(out=bA[0:64, 256 + h * 128:256 + (h + 1) * 128], lhsT=xdt_[:, h, :],
                                                rhs=LT_[:, h, :], start=True, stop=False), [xdt_, LT_], [bA])
                Q("pe", lambda e, h=h: e.matmul(out=bA[0:64, 256 + h * 128:256 + (h + 1) * 128], lhsT=sb_in[:, h, :],
                                                rhs=Ce_[:, h, :], start=False, stop=True), [sb_in, Ce_], [bA])
            for h in range(2):
                Q("dve", lambda e, h=h: e.scalar_tensor_tensor(out=S[:, h, :], in0=S[:, h, :],
                                                               scalar=ER_[:, h, 127:128],
                                                               in1=bB[:, 128 + h * 64:128 + (h + 1) * 64],
                                                               op0=ALU.mult, op1=ALU.add), [S, ER_, bB], [S])
            Q("act", lambda e: e.copy(out=sb_out[:, :, :], in_=S[:, :, :]), [S], [sb_out])
            for h in range(2):
                Q("dve", lambda e, h=h: e.scalar_tensor_tensor(out=yg[:, h, c:c + 128], in0=xg[:, h, c:c + 128],
                                                               scalar=dcol[:, h:h + 1],
                                                               in1=bA[0:64, 256 + h * 128:256 + (h + 1) * 128],
                                                               op0=ALU.mult, op1=ALU.add), [xg, dcol, bA], [yg])
            if j % 16 == 15:
                s0 = sg * SEG
                QD("sp", lambda e: e.dma_start(out=y_o.t[:, :, s0:s0 + SEG], in_=yg[:, :, :]), [yg], [y_o], yg)
            if j % 16 == 0 and sg + 1 < SEQ // SEG:
                load_seg(sg + 1, QD)
            return H1, H2

        def merge(a, b):
            out, ia, ib = [], 0, 0
            na, nb = len(a), len(b)
            while ia < na or ib < nb:
                if ib >= nb or (ia < na and ia * nb <= ib * na):
                    out.append(a[ia]); ia += 1
                else:
                    out.append(b[ib]); ib += 1
            return out

        load_seg(0)
        prev_tail = []
        for j in range(NCH):
            h1, h2 = ssd_chunk(j)
            pend.extend(merge(prev_tail, h1))
            prev_tail = h2
        pend.extend(prev_tail)
        for qi in range(NQT):
            attn_qtile(qi)
        drain(1 << 30)
        P.finish([o_o, y_o])
        P.emit()
    return nc


def build_C():
    nc = bass.Bass("TRN2", target_bir_lowering=False)
    with ExitStack() as es:
        P = Prog(nc, es)
        D = lambda n, s, d, k="ExternalInput": P.dram(n, s, d, k)
        y_d = D("yssdT", [1024, TOK], F32)
        gzs_d = D("gz_ssdT", [1024, TOK], BF16)
        ycf_d = D("ycfmT", [512, TOK], BF16)
        o_d = D("oT", [8, 128, TOK], F32)
        gza_d = D("gz_attT", [512, TOK], BF16)
        x_d = D("x", [TOK, DM], F32)
        w_d = D("w_out", [DM, DM], F32)
        snw_d = D("snw", [128, 8], F32)
        subw_d = D("subw", [128, 1], F32)
        lam_d = D("lamv", [128, 4, 64], F32)
        linit_d = D("linit", [128, 2], F32)
        onesf_d = D("ones_f", [128, 128], F32)
        x_o = D("x_out", [TOK, DM], F32, "ExternalOutput")

        Wsb = P.sb("Wsb", [128, 16, DM], BF16)
        ybuf = [P.sb("ybuf%d" % i, [128, 16, 512], BF16) for i in range(2)]
        gbuf = P.sb("gbuf", [128, 8, 512], F32)
        yin = [P.sb("yin%d" % i, [128, 512], F32) for i in range(2)]
        gzin = [P.sb("gzin%d" % i, [128, 512], BF16) for i in range(2)]
        oin = [P.sb("oin%d" % i, [128, 512], F32) for i in range(4)]
        tmp = [P.sb("tmpc%d" % i, [128, 512], F32) for i in range(3)]
        xt = [P.sb("xt%d" % i, [128, DM], F32) for i in range(2)]
        xo = [P.sb("xo%d" % i, [128, DM], F32) for i in range(2)]
        snw = P.sb("snw_s", [128, 8], F32)
        subw = P.sb("subw_s", [128, 1], F32)
        lam = P.sb("lam_s", [128, 4, 64], F32)
        linit = P.sb("linit_s", [128, 2], F32)
        onesf = P.sb("onesf_s", [128, 128], F32)
        lsc = P.sb("lsc", [128, 4], F32)
        pacc = [P.ps("pacc%d" % i, [128, 512], F32) for i in range(4)]
        paux = [P.ps("paux%d" % i, [128, 512], F32) for i in range(2)]

        def load(dst, src, q="sp"):
            P.dma(q, lambda e: e.dma_start(out=dst.ap(), in_=src.ap()), [src], [dst], dst)

        for dst, src in [(snw, snw_d), (subw, subw_d), (lam, lam_d), (linit, linit_d), (onesf, onesf_d)]:
            load(dst, src)
        for part in range(4):
            P.dma("pool", lambda e, part=part: e.dma_start(
                out=Wsb[:, part * 4:(part + 1) * 4, :],
                in_=w_d.t[part * 512:(part + 1) * 512, :].rearrange("(k p) c -> p k c", p=128)), [w_d], [Wsb], Wsb)
        P.op("dve", lambda e: e.tensor_tensor(out=lam[:, 0, :], in0=lam[:, 0, :], in1=lam[:, 1, :], op=ALU.mult),
             [lam], [lam])
        P.op("dve", lambda e: e.tensor_tensor(out=lam[:, 2, :], in0=lam[:, 2, :], in1=lam[:, 3, :], op=ALU.mult),
             [lam], [lam])
        P.op("dve", lambda e: e.tensor_reduce(out=lsc[:, 0:1], in_=lam[:, 0, :], axis=AX.X, op=ALU.add), [lam], [lsc])
        P.op("dve", lambda e: e.tensor_reduce(out=lsc[:, 1:2], in_=lam[:, 2, :], axis=AX.X, op=ALU.add), [lam], [lsc])
        P.op("act", lambda e: e.activation(out=lsc[:, 0:2], in_=lsc[:, 0:2], func=AF.Exp), [lsc], [lsc])
        P.op("dve", lambda e: e.tensor_tensor(out=lsc[:, 2:3], in0=lsc[:, 1:2], in1=lsc[:, 0:1], op=ALU.subtract),
             [lsc], [lsc])
        P.op("dve", lambda e: e.tensor_tensor(out=lsc[:, 3:4], in0=lsc[:, 2:3], in1=linit[:, 0:1], op=ALU.subtract),
             [lsc, linit], [lsc])

        rr = {"ld": 0, "o": 0, "pacc": 0}
        for tg in range(4):
            t0 = tg * 512
            yb = ybuf[tg % 2]
            for i in range(8):
                yi = yin[rr["ld"] % 2]
                gi = gzin[rr["ld"] % 2]
                rr["ld"] += 1
                P.dma("sp", lambda e, yi=yi, i=i, t0=t0: e.dma_start(out=yi[:, :], in_=y_d.t[i * 128:(i + 1) * 128, t0:t0 + 512]),
                      [y_d], [yi], yi)
                P.dma("sp", lambda e, gi=gi, i=i, t0=t0: e.dma_start(out=gi[:, :], in_=gzs_d.t[i * 128:(i + 1) * 128, t0:t0 + 512]),
                      [gzs_d], [gi], gi)
                P.op("dve", lambda e, yi=yi, gi=gi, i=i: e.tensor_tensor(out=gbuf[:, i, :], in0=yi[:, :], in1=gi[:, :],
                                                                          op=ALU.mult), [yi, gi], [gbuf])
                P.op("act", lambda e, i=i: e.activation(out=tmp[0][:, :], in_=gbuf[:, i, :], func=AF.Square),
                     [gbuf], [tmp[0]])
                P.op("pe", lambda e, i=i: e.matmul(out=paux[0][:, :], lhsT=onesf[:, :], rhs=tmp[0][:, :], start=(i == 0),
                                                   stop=(i == 7)), [onesf, tmp[0]], [paux[0]])
            P.op("act", lambda e: e.activation(out=tmp[1][:, :], in_=paux[0][:, :], func=AF.Ln, bias=EPS,
                                               scale=1.0 / 1024), [paux[0]], [tmp[1]])
            P.op("act", lambda e: e.activation(out=tmp[1][:, :], in_=tmp[1][:, :], func=AF.Exp, scale=-0.5),
                 [tmp[1]], [tmp[1]])
            for i in range(8):
                P.op("dve", lambda e, i=i, yb=yb: e.scalar_tensor_tensor(out=yb[:, i, :], in0=gbuf[:, i, :],
                                                                         scalar=snw[:, i:i + 1], in1=tmp[1][:, :],
                                                                         op0=ALU.mult, op1=ALU.mult),
                     [gbuf, snw, tmp[1]], [yb])
            for i in range(4):
                P.dma("sp", lambda e, i=i, yb=yb, t0=t0: e.dma_start(out=yb[:, 8 + i, :],
                                                              in_=ycf_d.t[i * 128:(i + 1) * 128, t0:t0 + 512]),
                      [ycf_d], [yb], yb)
            for h in range(4):
                o0 = oin[rr["o"] % 4]
                o1 = oin[(rr["o"] + 1) % 4]
                rr["o"] += 2
                gi = gzin[rr["ld"] % 2]
                rr["ld"] += 1
                P.dma("sp", lambda e, o0=o0, h=h, t0=t0: e.dma_start(out=o0[:, :], in_=o_d.t[2 * h, :, t0:t0 + 512]),
                      [o_d], [o0], o0)
                P.dma("sp", lambda e, o1=o1, h=h, t0=t0: e.dma_start(out=o1[:, :], in_=o_d.t[2 * h + 1, :, t0:t0 + 512]),
                      [o_d], [o1], o1)
                P.dma("sp", lambda e, gi=gi, h=h, t0=t0: e.dma_start(out=gi[:, :], in_=gza_d.t[h * 128:(h + 1) * 128, t0:t0 + 512]),
                      [gza_d], [gi], gi)
                P.op("dve", lambda e, o0=o0, o1=o1: e.scalar_tensor_tensor(out=tmp[0][:, :], in0=o1[:, :],
                                                                           scalar=lsc[:, 3:4], in1=o0[:, :],
                                                                           op0=ALU.mult, op1=ALU.add),
                     [o0, o1, lsc], [tmp[0]])
                P.op("act", lambda e: e.activation(out=tmp[2][:, :], in_=tmp[0][:, :], func=AF.Square),
                     [tmp[0]], [tmp[2]])
                P.op("pe", lambda e: e.matmul(out=paux[1][:, :], lhsT=onesf[:, :], rhs=tmp[2][:, :], start=True,
                                              stop=True), [onesf, tmp[2]], [paux[1]])
                P.op("act", lambda e: e.activation(out=tmp[2][:, :], in_=paux[1][:, :], func=AF.Ln, bias=EPS,
                                                   scale=1.0 / 128), [paux[1]], [tmp[2]])
                P.op("act", lambda e: e.activation(out=tmp[2][:, :], in_=tmp[2][:, :], func=AF.Exp, scale=-0.5),
                     [tmp[2]], [tmp[2]])
                P.op("dve", lambda e: e.scalar_tensor_tensor(out=tmp[0][:, :], in0=tmp[0][:, :], scalar=subw[:, 0:1],
                                                             in1=tmp[2][:, :], op0=ALU.mult, op1=ALU.mult),
                     [tmp[0], subw, tmp[2]], [tmp[0]])
                P.op("dve", lambda e, gi=gi, h=h, yb=yb: e.scalar_tensor_tensor(out=yb[:, 12 + h, :], in0=tmp[0][:, :],
                                                                                scalar=linit[:, 1:2], in1=gi[:, :],
                                                                                op0=ALU.mult, op1=ALU.mult),
                     [tmp[0], linit, gi], [yb])
            for tt in range(4):
                r0 = t0 + tt * 128
                xti = xt[tt % 2]
                xoi = xo[tt % 2]
                P.dma("sp", lambda e, xti=xti, r0=r0: e.dma_start(out=xti[:, :], in_=x_d.t[r0:r0 + 128, :]),
                      [x_d], [xti], xti)
                for cg in range(4):
                    pb = pacc[rr["pacc"] % 4]
                    rr["pacc"] += 1
                    for kk in range(16):
                        P.op("pe", lambda e, pb=pb, kk=kk, tt=tt, cg=cg, yb=yb: e.matmul(
                            out=pb[:, :], lhsT=yb[:, kk, tt * 128:(tt + 1) * 128],
                            rhs=Wsb[:, kk, cg * 512:(cg + 1) * 512], start=(kk == 0), stop=(kk == 15)),
                            [yb, Wsb], [pb])
                    P.op("dve", lambda e, pb=pb, cg=cg, xti=xti, xoi=xoi: e.tensor_tensor(
                        out=xoi[:, cg * 512:(cg + 1) * 512], in0=pb[:, :], in1=xti[:, cg * 512:(cg + 1) * 512],
                        op=ALU.add), [pb, xti], [xoi])
                P.dma("sp", lambda e, xoi=xoi, r0=r0: e.dma_start(out=x_o.t[r0:r0 + 128, :], in_=xoi[:, :]),
                      [xoi], [x_o], xoi)
        P.finish([x_o])
        P.emit()
    return nc


def run_B(resA, inp, l, consts):
    nc = _get("B", build_B)
    qT = np.concatenate([np.asarray(r["qT"]) for r in resA], axis=1)
    kT = np.concatenate([np.asarray(r["kT"]) for r in resA], axis=1)
    v = np.concatenate([np.asarray(r["v"]) for r in resA], axis=0)
    xbcT = np.concatenate([np.asarray(r["xbcT"]) for r in resA], axis=1)
    dt = np.concatenate([np.asarray(r["dt"]) for r in resA], axis=0)
    alog = np.asarray(inp["ssd_a_log"][l], np.float32)
    dsk = np.asarray(inp["ssd_d"][l], np.float32)
    in_maps = []
    for u in range(NCORES):
        h, g = u // 2, u // 4
        xs = xbcT[128 * u:128 * (u + 1)].reshape(2, 64, SEQ).transpose(1, 0, 2)
        in_maps.append({
            "qT": np.ascontiguousarray(qT[64 * u:64 * (u + 1)]),
            "kT": np.ascontiguousarray(kT[64 * u:64 * (u + 1)]),
            "v": np.ascontiguousarray(v[:, 128 * h:128 * (h + 1)]),
            "xsT": np.ascontiguousarray(xs),
            "BT": np.ascontiguousarray(xbcT[1024 + 128 * g:1024 + 128 * (g + 1)]),
            "CT": np.ascontiguousarray(xbcT[1280 + 128 * g:1280 + 128 * (g + 1)]),
            "dt2": np.ascontiguousarray(dt[:, 2 * u:2 * u + 2]),
            "xs_tok": np.ascontiguousarray(xbcT[128 * u:128 * (u + 1)].T),
            "b_tok": np.ascontiguousarray(xbcT[1024 + 128 * g:1024 + 128 * (g + 1)].T),
            "alog": np.ascontiguousarray(np.tile(alog[None, 2 * u:2 * u + 2], (128, 1))),
            "dcol": np.ascontiguousarray(np.tile(dsk[None, 2 * u:2 * u + 2], (64, 1))),
            "ones_bf": consts["ones_f"].astype(NPBF), "tri_bf": consts["tri_bf"], "ident_bf": consts["ident_bf"],
            "ones_f": consts["ones_f"], "tri_f": consts["tri_f"],
        })
    res = run_bass_kernel_spmd(nc, in_maps, core_ids=list(range(NCORES)))
    return res.results


def run_C(x_full, resA, resB, inp, l, consts):
    nc = _get("C", build_C)
    linit_v = 0.8 - 0.6 * math.exp(-0.3 * l)
    yT = np.concatenate([np.asarray(r["yT"]).transpose(1, 0, 2).reshape(128, SEQ) for r in resB], axis=0)
    oT = np.stack([np.asarray(r["oT"]) for r in resB], axis=0)
    lamv = np.stack([np.asarray(inp[k][l], np.float32) for k in
                     ("att_lambda_q1", "att_lambda_k1", "att_lambda_q2", "att_lambda_k2")], axis=0)
    shared = {
        "w_out": np.ascontiguousarray(inp["w_out"][l]),
        "snw": _chunk_cols(inp["ssd_norm_w"][l], 8),
        "subw": np.ascontiguousarray(np.asarray(inp["att_subln_w"][l], np.float32).reshape(128, 1)),
        "lamv": np.ascontiguousarray(np.tile(lamv[None], (128, 1, 1))),
        "linit": np.ascontiguousarray(np.tile(np.array([[linit_v, 1.0 - linit_v]], np.float32), (128, 1))),
        "ones_f": consts["ones_f"],
    }
    in_maps = []
    for c in range(NCORES):
        sl = slice(c * TOK, (c + 1) * TOK)
        m = dict(shared)
        m.update({
            "yssdT": np.ascontiguousarray(yT[:, sl]),
            "gz_ssdT": np.asarray(resA[c]["gz_ssdT"]),
            "ycfmT": np.asarray(resA[c]["ycfmT"]),
            "oT": np.ascontiguousarray(oT[:, :, sl]),
            "gz_attT": np.asarray(resA[c]["gz_attT"]),
            "x": np.ascontiguousarray(x_full[sl]),
        })
        in_maps.append(m)
    res = run_bass_kernel_spmd(nc, in_maps, core_ids=list(range(NCORES)))
    return np.concatenate([np.asarray(r["x_out"]) for r in res.results], axis=0)


def kernel(**inputs):
    inp = {k: np.asarray(v) for k, v in inputs.items()}
    consts = _consts()
    cos, sin = _rope_tables()
    x = np.ascontiguousarray(inp["x"][0], dtype=np.float32)
    for l in range(2):
        resA = run_A(x, inp, l, consts, cos, sin)
        resB = run_B(resA, inp, l, consts)
        x = run_C(x, resA, resB, inp, l, consts)
    return x[None].astype(np.float32)
```

```python
import math
from contextlib import ExitStack

import numpy as np
import ml_dtypes

import concourse.bass as bass
import concourse.mybir as mybir
from concourse.bass_utils import run_bass_kernel_spmd

F32 = mybir.dt.float32
BF16 = mybir.dt.bfloat16
AF = mybir.ActivationFunctionType
ALU = mybir.AluOpType
AX = mybir.AxisListType
NPBF = ml_dtypes.bfloat16

NCORES = 8
SEQ = 16384
DM = 2048
DIN = 6160
TOK = SEQ // NCORES
EPS = 1e-6
ENGS = ["pe", "dve", "act", "pool", "sp"]


class Buf:
    def __init__(self, t, name, is_ap=False):
        self.t = t
        self.name = name
        self.is_ap = is_ap
        self.w = None
        self.r = {}
        self.dsem = None
        self.dcnt = 0

    def __getitem__(self, k):
        return self.t[k]

    def ap(self):
        return self.t if self.is_ap else self.t[:]


def alias(dst, srcs):
    for s in srcs:
        if s.w is not None:
            dst.r[("w", id(s.w[0]))] = s.w
        for k, v in s.r.items():
            dst.r[("r", k)] = v


class Prog:
    def __init__(self, nc, es):
        self.nc = nc
        self.es = es
        self.sem = {e: es.enter_context(nc.semaphore("sem_" + e)) for e in ENGS}
        self.cnt = {e: 0 for e in ENGS}
        self.q = {e: [] for e in ENGS}
        self.waited = {e: {} for e in ENGS}
        self.nds = 0

    def sb(self, name, shape, dt):
        return Buf(self.es.enter_context(self.nc.sbuf_tensor(name, list(shape), dt)), name)

    def ps(self, name, shape, dt):
        return Buf(self.es.enter_context(self.nc.psum_tensor(name, list(shape), dt)), name)

    def dram(self, name, shape, dt, kind):
        return Buf(self.nc.dram_tensor(name, list(shape), dt, kind=kind).ap(), name, is_ap=True)

    def _deps(self, eng, reads, writes):
        deps = []
        for b in reads:
            if b.w is not None:
                deps.append(b.w)
        for b in writes:
            if b.w is not None:
                deps.append(b.w)
            deps.extend(b.r.values())
        out = []
        for (sem, val, src) in deps:
            if src == "pe" and eng == "pe":
                continue
            key = id(sem)
            if self.waited[eng].get(key, 0) >= val:
                continue
            self.waited[eng][key] = val
            out.append((sem, val))
        return out

    def _mark(self, ev, reads, writes):
        for b in reads:
            b.r[id(ev[0])] = ev
        for b in writes:
            b.w = ev
            b.r = {}

    def op(self, eng, fn, reads=(), writes=()):
        waits = self._deps(eng, reads, writes)
        self.cnt[eng] += 1
        ev = (self.sem[eng], self.cnt[eng], eng)
        self._mark(ev, reads, writes)
        self.q[eng].append((waits, fn, (self.sem[eng], 1)))

    def dma(self, queue, fn, reads, writes, sembuf):
        waits = self._deps(queue, reads, writes)
        if sembuf.dsem is None:
            sembuf.dsem = self.es.enter_context(self.nc.semaphore("ds%d" % self.nds))
            self.nds += 1
        sembuf.dcnt += 1
        ev = (sembuf.dsem, 16 * sembuf.dcnt, "dma")
        self._mark(ev, reads, writes)
        self.q[queue].append((waits, fn, (sembuf.dsem, 16)))

    def finish(self, outs):
        waits = []
        for b in outs:
            if b.w is not None:
                waits.append((b.w[0], b.w[1]))
        self.q["sp"].append((waits, None, None))

    def emit(self):
        nc = self.nc

        def replay(name, e):
            for (waits, fn, inc) in self.q[name]:
                for (sem, val) in waits:
                    e.wait_ge(sem, val)
                if fn is None:
                    continue
                ins = fn(e)
                ins.then_inc(inc[0], inc[1])

        with nc.Block() as block:
            @block.tensor
            def _(e):
                replay("pe", e)

            @block.vector
            def _(e):
                replay("dve", e)

            @block.scalar
            def _(e):
                replay("act", e)

            @block.gpsimd
            def _(e):
                replay("pool", e)

            @block.sync
            def _(e):
                replay("sp", e)


def bc3(ap2, n):
    p, k = ap2.shape
    return ap2.unsqueeze(2).to_broadcast([p, k, n])


def _consts():
    c = {}
    c["ident_bf"] = np.eye(128, dtype=np.float32).astype(NPBF)
    c["ident_f"] = np.eye(128, dtype=np.float32)
    c["ones_f"] = np.ones((128, 128), np.float32)
    blk = np.zeros((128, 128), np.float32)
    blk[:64, :64] = 1.0
    blk[64:, 64:] = 1.0
    c["blk_f"] = blk
    rot = np.zeros((128, 128), np.float32)
    for b0 in (0, 64):
        for m in range(32):
            rot[b0 + m + 32, b0 + m] = -1.0
        for m in range(32, 64):
            rot[b0 + m - 32, b0 + m] = 1.0
    c["rot_f"] = rot
    tri = np.triu(np.ones((128, 128), np.float32))
    c["tri_f"] = tri
    c["tri_bf"] = tri.astype(NPBF)
    return c


def _rope_tables():
    inv_freq = (1.0 / (10000.0 ** (np.arange(0, 64, 2, dtype=np.float32) / np.float32(64)))).astype(np.float32)
    pos = np.arange(SEQ, dtype=np.float32)
    ang = (pos[:, None] * inv_freq[None, :]).astype(np.float32)
    return np.cos(ang).astype(np.float32), np.sin(ang).astype(np.float32)


NT_A = 17
TA = NT_A * 128


def build_A():
    nc = bass.Bass("TRN2", target_bir_lowering=False)
    with ExitStack() as es:
        P = Prog(nc, es)
        D = lambda n, s, d, k="ExternalInput": P.dram(n, s, d, k)
        xe = D("xe", [TA, DM], F32)
        w_in = D("w_in", [DM, DIN], F32)
        normw_d = D("normw", [128, 16], F32)
        scw_d = D("scw", [128, 12, 4], F32)
        scb_d = D("scb", [128, 12], F32)
        dtb_d = D("dtb", [128, 16], F32)
        cfw_d = D("cfw", [128, 4, 31], F32)
        cfb_d = D("cfb", [128, 4], F32)
        lnw_d = D("lnw", [128, 4], F32)
        lnb_d = D("lnb", [128, 4], F32)
        qkw_d = D("qkw", [128, 2], F32)
        cos_d = D("cosT", [128, TOK], F32)
        sin_d = D("sinT", [128, TOK], F32)
        identb_d = D("ident_bf", [128, 128], BF16)
        identf_d = D("ident_f", [128, 128], F32)
        ones_d = D("ones_f", [128, 128], F32)
        blk_d = D("blk_f", [128, 128], F32)
        rot_d = D("rot_f", [128, 128], F32)
        O = "ExternalOutput"
        gzs_o = D("gz_ssdT", [1024, TOK], BF16, O)
        xbc_o = D("xbcT", [1536, TOK], BF16, O)
        dt_o = D("dt", [TOK, 16], F32, O)
        ycf_o = D("ycfmT", [512, TOK], BF16, O)
        q_o = D("qT", [512, TOK], BF16, O)
        k_o = D("kT", [512, TOK], BF16, O)
        v_o = D("v", [TOK, 512], BF16, O)
        gza_o = D("gz_attT", [512, TOK], BF16, O)

        xnT = P.sb("xnT", [128, 16, TA], BF16)
        rstd_bc = P.sb("rstd_bc", [128, TA], F32)
        ssq = P.sb("ssq", [128, NT_A], F32)
        rstd_all = P.sb("rstd_all", [128, NT_A], F32)
        normw = P.sb("normw_s", [128, 16], F32)
        scw = P.sb("scw_s", [128, 12, 4], F32)
        scb = P.sb("scb_s", [128, 12], F32)
        dtb = P.sb("dtb_s", [128, 16], F32)
        cfw = P.sb("cfw_s", [128, 4, 31], F32)
        cfb = P.sb("cfb_s", [128, 4], F32)
        lnw = P.sb("lnw_s", [128, 4], F32)
        lnb = P.sb("lnb_s", [128, 4], F32)
        qkw = P.sb("qkw_s", [128, 2], F32)
        identb = P.sb("identb_s", [128, 128], BF16)
        identf = P.sb("identf_s", [128, 128], F32)
        ones = P.sb("ones_s", [128, 128], F32)
        blk = P.sb("blk_s", [128, 128], F32)
        rot = P.sb("rot_s", [128, 128], F32)
        cc_t = es.enter_context(nc.sbuf_tensor("cc_t", [128, 4, TA], F32))
        junk = Buf(cc_t[:, 0, :], "junk", is_ap=True)
        accb = Buf(cc_t[:, 1, :], "accb", is_ap=True)
        big = [Buf(cc_t[:, 2, :], "big0", is_ap=True), Buf(cc_t[:, 3, :], "big1", is_ap=True)]
        cc = Buf(cc_t[:, :, :], "cc", is_ap=True)
        cosT = Buf(cc_t[:, 0, 0:TOK], "cosT", is_ap=True)
        sinT = Buf(cc_t[:, 1, 0:TOK], "sinT", is_ap=True)
        xb = P.sb("xb", [128, DM], BF16)
        diag = P.sb("diag", [128, 128], F32)
        wslab = [P.sb("wslab%d" % i, [128, 16, 512], BF16) for i in range(2)]
        wdt = P.sb("wdt", [128, 16, 16], BF16)
        tmp = [P.sb("tmp%d" % i, [128, 512], F32) for i in range(3)]
        ostage = [P.sb("ostage%d" % i, [128, TOK], BF16) for i in range(2)]
        vstage = [P.sb("vstage%d" % i, [128, 512], BF16) for i in range(2)]
        ubf = P.sb("ubf", [128, TA], BF16)
        dg = P.sb("dg", [128, 31, 128], BF16)
        mean_t = P.sb("mean_t", [128, 512], F32)
        rln_t = P.sb("rln_t", [128, 512], F32)
        dtbuf = P.sb("dtbuf", [128, 16, 16], F32)

        pT = P.ps("pT", [128, DM], BF16)
        pbc = P.ps("pbc", [128, 512], F32)
        pacc = [P.ps("pacc%d" % i, [128, 512], F32) for i in range(3)]
        paux = [P.ps("paux%d" % i, [128, 512], F32) for i in range(2)]

        def load(dst, src, q="sp"):
            P.dma(q, lambda e: e.dma_start(out=dst.ap(), in_=src.ap()), [src], [dst], dst)

        for dst, src in [(normw, normw_d), (scw, scw_d), (scb, scb_d), (dtb, dtb_d), (cfw, cfw_d), (cfb, cfb_d),
                         (lnw, lnw_d), (lnb, lnb_d), (qkw, qkw_d), (identb, identb_d),
                         (identf, identf_d), (ones, ones_d), (blk, blk_d), (rot, rot_d)]:
            load(dst, src)
        P.op("dve", lambda e: e.tensor_scalar(out=qkw[:, 0:1], in0=qkw[:, 0:1], scalar1=0.125, scalar2=None,
                                              op0=ALU.mult), [qkw], [qkw])

        GCOLS = [0, 512, 1024, 1536, 2048, 2576, 3088, 3600, 4112, 4624, 5136, 5648]
        slab_state = {"next": 0}

        def issue_slab(gi):
            slot = wslab[gi % 2]
            c0 = GCOLS[gi]
            for part in range(4):
                src = w_in.t[part * 512:(part + 1) * 512, c0:c0 + 512].rearrange("(k p) c -> p k c", p=128)
                dst = slot[:, part * 4:(part + 1) * 4, :]
                P.dma("pool", lambda e, s=src, d=dst: e.dma_start(out=d, in_=s), [w_in], [slot], slot)

        def need_slab(gi):
            while slab_state["next"] <= min(gi + 1, len(GCOLS) - 1):
                issue_slab(slab_state["next"])
                slab_state["next"] += 1
            return wslab[gi % 2]

        P.dma("pool", lambda e: e.dma_start(out=wdt[:, :, :],
                                            in_=w_in.t[:, 2560:2576].rearrange("(k p) c -> p k c", p=128)),
              [w_in], [wdt], wdt)
        need_slab(0)

        for t in range(NT_A):
            xt = big[t % 2]
            P.dma("sp", lambda e, t=t, xt=xt: e.dma_start(out=xt[:, 0:DM], in_=xe.t[t * 128:(t + 1) * 128, :]),
                  [xe], [xt], xt)
            P.op("pool", lambda e, xt=xt: e.tensor_copy(out=xb[:, :], in_=xt[:, 0:DM]), [xt], [xb])
            P.op("act", lambda e, xt=xt: e.activation(out=junk[:, 0:DM], in_=xt[:, 0:DM], func=AF.Square),
                 [xt], [junk])
            P.op("dve", lambda e, t=t: e.tensor_reduce(out=ssq[:, t:t + 1], in_=junk[:, 0:DM], axis=AX.X,
                                                       op=ALU.add), [junk], [ssq])
            for k in range(16):
                P.op("pe", lambda e, k=k: e.transpose(out=pT[:, k * 128:(k + 1) * 128],
                                                      in_=xb[:, k * 128:(k + 1) * 128], identity=identb[:, :]),
                     [xb, identb], [pT])
            P.op("dve", lambda e, t=t: e.tensor_tensor(
                out=xnT[:, :, t * 128:(t + 1) * 128],
                in0=pT[:, :].rearrange("p (k t) -> p k t", k=16),
                in1=bc3(normw[:, :], 128), op=ALU.mult), [pT, normw], [xnT])
        P.op("act", lambda e: e.activation(out=rstd_all[:, :], in_=ssq[:, :], func=AF.Sqrt, bias=EPS,
                                           scale=1.0 / DM), [ssq], [rstd_all])
        P.op("dve", lambda e: e.reciprocal(out=rstd_all[:, :], in_=rstd_all[:, :]), [rstd_all], [rstd_all])
        for t in range(NT_A):
            P.op("dve", lambda e, t=t: e.tensor_scalar(out=diag[:, :], in0=identf[:, :], scalar1=rstd_all[:, t:t + 1],
                                                       scalar2=None, op0=ALU.mult), [identf, rstd_all], [diag])
            P.op("pe", lambda e: e.matmul(out=pbc[:, 0:128], lhsT=ones[:, :], rhs=diag[:, :], start=True, stop=True),
                 [ones, diag], [pbc])
            P.op("act", lambda e, t=t: e.copy(out=rstd_bc[:, t * 128:(t + 1) * 128], in_=pbc[:, 0:128]),
                 [pbc], [rstd_bc])

        rr = {"pacc": 0, "ost": 0, "vst": 0}

        def fm_mm(slab, j, tok0, n):
            pb = pacc[rr["pacc"] % 3]
            rr["pacc"] += 1
            for k in range(16):
                P.op("pe", lambda e, k=k, pb=pb: e.matmul(out=pb[:, 0:n], lhsT=slab[:, k, j * 128:(j + 1) * 128],
                                                           rhs=xnT[:, k, tok0:tok0 + n], start=(k == 0),
                                                           stop=(k == 15)), [slab, xnT], [pb])
            return pb

        OWN = [(128 + 512 * g, 512, 512 * g) for g in range(4)]
        ALLT = [(0, 128, -128)] + OWN

        def store_rows(ost, dram, row0):
            P.dma("sp", lambda e: e.dma_start(out=dram.t[row0:row0 + 128, :], in_=ost[:, :]), [ost], [dram], ost)

        def gate_group(gi, dram, row_base):
            slab = need_slab(gi)
            for j in range(4):
                ost = ostage[rr["ost"] % 2]
                rr["ost"] += 1
                for (x0, n, o0) in OWN:
                    pb = fm_mm(slab, j, x0, n)
                    P.op("dve", lambda e, pb=pb, x0=x0: e.tensor_tensor(out=tmp[0][:, :], in0=pb[:, :],
                                                                         in1=rstd_bc[:, x0:x0 + 512], op=ALU.mult),
                         [pb, rstd_bc], [tmp[0]])
                    P.op("act", lambda e, ost=ost, o0=o0: e.activation(out=ost[:, o0:o0 + 512], in_=tmp[0][:, :],
                                                                       func=AF.Silu), [tmp[0]], [ost])
                store_rows(ost, dram, row_base + j * 128)

        gate_group(0, gzs_o, 0)
        gate_group(1, gzs_o, 512)

        for g3 in range(3):
            slab = need_slab(2 + g3)
            for j in range(4):
                c = g3 * 4 + j
                xr = junk
                acc = accb
                for (x0, n, o0) in ALLT:
                    pb = fm_mm(slab, j, x0, n)
                    P.op("dve", lambda e, pb=pb, x0=x0, n=n: e.tensor_tensor(out=xr[:, x0:x0 + n], in0=pb[:, 0:n],
                                                                              in1=rstd_bc[:, x0:x0 + n], op=ALU.mult),
                         [pb, rstd_bc], [xr])
                P.op("dve", lambda e, c=c: e.tensor_scalar(out=acc[:, 0:TOK], in0=xr[:, 128:TA],
                                                           scalar1=scw[:, c, 3:4], scalar2=scb[:, c:c + 1],
                                                           op0=ALU.mult, op1=ALU.add), [xr, scw, scb], [acc])
                for s in (1, 2, 3):
                    P.op("dve", lambda e, c=c, s=s: e.scalar_tensor_tensor(
                        out=acc[:, 0:TOK], in0=xr[:, 128 - s:TA - s], scalar=scw[:, c, 3 - s:4 - s],
                        in1=acc[:, 0:TOK], op0=ALU.mult, op1=ALU.add), [xr, scw, acc], [acc])
                ost = ostage[rr["ost"] % 2]
                rr["ost"] += 1
                P.op("act", lambda e, ost=ost: e.activation(out=ost[:, :], in_=acc[:, 0:TOK], func=AF.Silu),
                     [acc], [ost])
                store_rows(ost, xbc_o, c * 128)

        for t in range(1, NT_A):
            pb = paux[t % 2]
            for k in range(16):
                P.op("pe", lambda e, k=k, pb=pb, t=t: e.matmul(out=pb[:, 0:16], lhsT=xnT[:, k, t * 128:(t + 1) * 128],
                                                               rhs=wdt[:, k, :], start=(k == 0), stop=(k == 15)),
                     [xnT, wdt], [pb])
            P.op("dve", lambda e, pb=pb, t=t: e.scalar_tensor_tensor(
                out=dtbuf[:, t - 1, :], in0=pb[:, 0:16], scalar=rstd_all[:, t:t + 1], in1=dtb[:, :],
                op0=ALU.mult, op1=ALU.add), [pb, rstd_all, dtb], [dtbuf])
        P.op("act", lambda e: e.activation(out=dtbuf[:, :, :], in_=dtbuf[:, :, :], func=AF.Exp), [dtbuf], [dtbuf])
        P.op("act", lambda e: e.activation(out=dtbuf[:, :, :], in_=dtbuf[:, :, :], func=AF.Ln, bias=1.0, scale=1.0),
             [dtbuf], [dtbuf])
        P.dma("sp", lambda e: e.dma_start(out=dt_o.t.rearrange("(t p) h -> p t h", p=128), in_=dtbuf[:, :, :]),
              [dtbuf], [dt_o], dtbuf)

        slab_a = need_slab(5)
        slab_g = wslab[6 % 2]
        alias(cc, [junk, accb, big[0], big[1]])
        for i in range(4):
            for (x0, n, o0) in ALLT:
                pa = fm_mm(slab_a, i, x0, n)
                P.op("dve", lambda e, pa=pa, x0=x0, n=n: e.tensor_tensor(out=tmp[0][:, 0:n], in0=pa[:, 0:n],
                                                                          in1=rstd_bc[:, x0:x0 + n], op=ALU.mult),
                     [pa, rstd_bc], [tmp[0]])
                pg = fm_mm(slab_g, i, x0, n)
                P.op("dve", lambda e, pg=pg, x0=x0, n=n: e.tensor_tensor(out=tmp[1][:, 0:n], in0=pg[:, 0:n],
                                                                          in1=rstd_bc[:, x0:x0 + n], op=ALU.mult),
                     [pg, rstd_bc], [tmp[1]])
                P.op("act", lambda e, n=n: e.activation(out=tmp[1][:, 0:n], in_=tmp[1][:, 0:n], func=AF.Sigmoid),
                     [tmp[1]], [tmp[1]])
                P.op("dve", lambda e, x0=x0, n=n: e.tensor_tensor(out=ubf[:, x0:x0 + n], in0=tmp[0][:, 0:n],
                                                                   in1=tmp[1][:, 0:n], op=ALU.mult),
                     [tmp[0], tmp[1]], [ubf])
            for k in range(31):
                P.op("dve", lambda e, i=i, k=k: e.tensor_scalar(out=dg[:, k, :], in0=identf[:, :],
                                                                scalar1=cfw[:, i, k:k + 1], scalar2=None,
                                                                op0=ALU.mult), [identf, cfw], [dg])
            for (x0, n, o0) in OWN:
                pb = pacc[rr["pacc"] % 3]
                rr["pacc"] += 1
                for k in range(31):
                    P.op("pe", lambda e, pb=pb, k=k, x0=x0: e.matmul(
                        out=pb[:, :], lhsT=dg[:, k, :], rhs=ubf[:, x0 - 30 + k:x0 - 30 + k + 512],
                        start=(k == 0), stop=(k == 30)), [dg, ubf], [pb])
                P.op("dve", lambda e, pb=pb, i=i, o0=o0: e.tensor_scalar(out=cc[:, i, o0:o0 + 512], in0=pb[:, :],
                                                                          scalar1=cfb[:, i:i + 1], scalar2=None,
                                                                          op0=ALU.add), [pb, cfb], [cc])
        for (x0, n, o0) in OWN:
            for i in range(4):
                P.op("act", lambda e, i=i, o0=o0: e.activation(out=tmp[2][:, :], in_=cc[:, i, o0:o0 + 512],
                                                               func=AF.Square), [cc], [tmp[2]])
                P.op("pe", lambda e, i=i, o0=o0: e.matmul(out=paux[0][:, :], lhsT=ones[:, :], rhs=cc[:, i, o0:o0 + 512],
                                                          start=(i == 0), stop=(i == 3)), [ones, cc], [paux[0]])
                P.op("pe", lambda e, i=i: e.matmul(out=paux[1][:, :], lhsT=ones[:, :], rhs=tmp[2][:, :],
                                                   start=(i == 0), stop=(i == 3)), [ones, tmp[2]], [paux[1]])
            P.op("dve", lambda e: e.tensor_scalar(out=mean_t[:, :], in0=paux[0][:, :], scalar1=1.0 / 512,
                                                  scalar2=None, op0=ALU.mult), [paux[0]], [mean_t])
            P.op("dve", lambda e: e.tensor_tensor(out=tmp[0][:, :], in0=mean_t[:, :], in1=mean_t[:, :], op=ALU.mult),
                 [mean_t], [tmp[0]])
            P.op("dve", lambda e: e.scalar_tensor_tensor(out=tmp[0][:, :], in0=paux[1][:, :], scalar=1.0 / 512,
                                                         in1=tmp[0][:, :], op0=ALU.mult, op1=ALU.subtract),
                 [paux[1], tmp[0]], [tmp[0]])
            P.op("act", lambda e: e.activation(out=tmp[0][:, :], in_=tmp[0][:, :], func=AF.Ln, bias=EPS, scale=1.0),
                 [tmp[0]], [tmp[0]])
            P.op("act", lambda e: e.activation(out=rln_t[:, :], in_=tmp[0][:, :], func=AF.Exp, scale=-0.5),
                 [tmp[0]], [rln_t])
            for i in range(4):
                P.op("dve", lambda e, i=i, o0=o0: e.tensor_tensor(out=cc[:, i, o0:o0 + 512], in0=cc[:, i, o0:o0 + 512],
                                                                   in1=mean_t[:, :], op=ALU.subtract),
                     [cc, mean_t], [cc])
                P.op("dve", lambda e, i=i, o0=o0: e.tensor_tensor(out=cc[:, i, o0:o0 + 512], in0=cc[:, i, o0:o0 + 512],
                                                                   in1=rln_t[:, :], op=ALU.mult),
                     [cc, rln_t], [cc])
        slab = need_slab(7)
        for i in range(4):
            ost = ostage[rr["ost"] % 2]
            rr["ost"] += 1
            for (x0, n, o0) in OWN:
                pz = fm_mm(slab, i, x0, n)
                P.op("dve", lambda e, pz=pz, x0=x0: e.tensor_tensor(out=tmp[0][:, :], in0=pz[:, :],
                                                                     in1=rstd_bc[:, x0:x0 + 512], op=ALU.mult),
                     [pz, rstd_bc], [tmp[0]])
                P.op("act", lambda e: e.activation(out=tmp[0][:, :], in_=tmp[0][:, :], func=AF.Silu),
                     [tmp[0]], [tmp[0]])
                P.op("act", lambda e, i=i, o0=o0: e.activation(out=tmp[1][:, :], in_=cc[:, i, o0:o0 + 512], func=AF.Silu,
                                                               bias=lnb[:, i:i + 1], scale=lnw[:, i:i + 1]),
                     [cc, lnb, lnw], [tmp[1]])
                P.op("dve", lambda e, ost=ost, o0=o0: e.tensor_tensor(out=ost[:, o0:o0 + 512], in0=tmp[1][:, :],
                                                                       in1=tmp[0][:, :], op=ALU.mult),
                     [tmp[0], tmp[1]], [ost])
            store_rows(ost, ycf_o, i * 128)

        alias(cosT, [cc])
        alias(sinT, [cc])
        P.dma("sp", lambda e: e.dma_start(out=cosT.ap(), in_=cos_d.ap()), [cos_d], [cosT], cosT)
        P.dma("sp", lambda e: e.dma_start(out=sinT.ap(), in_=sin_d.ap()), [sin_d], [sinT], sinT)

        for (gi, dram, wc) in ((8, q_o, 0), (9, k_o, 1)):
            slab = need_slab(gi)
            for i in range(4):
                ost = ostage[rr["ost"] % 2]
                rr["ost"] += 1
                for (x0, n, o0) in OWN:
                    pq = fm_mm(slab, i, x0, n)
                    P.op("dve", lambda e, pq=pq, x0=x0: e.tensor_tensor(out=tmp[0][:, :], in0=pq[:, :],
                                                                         in1=rstd_bc[:, x0:x0 + 512], op=ALU.mult),
                         [pq, rstd_bc], [tmp[0]])
                    P.op("act", lambda e: e.activation(out=tmp[1][:, :], in_=tmp[0][:, :], func=AF.Square),
                         [tmp[0]], [tmp[1]])
                    P.op("pe", lambda e: e.matmul(out=paux[0][:, :], lhsT=blk[:, :], rhs=tmp[1][:, :], start=True,
                                                  stop=True), [blk, tmp[1]], [paux[0]])
                    P.op("act", lambda e: e.activation(out=tmp[1][:, :], in_=paux[0][:, :], func=AF.Ln, bias=EPS,
                                                       scale=1.0 / 64), [paux[0]], [tmp[1]])
                    P.op("act", lambda e: e.activation(out=tmp[1][:, :], in_=tmp[1][:, :], func=AF.Exp, scale=-0.5),
                         [tmp[1]], [tmp[1]])
                    P.op("dve", lambda e, wc=wc: e.scalar_tensor_tensor(out=tmp[0][:, :], in0=tmp[0][:, :],
                                                                        scalar=qkw[:, wc:wc + 1], in1=tmp[1][:, :],
                                                                        op0=ALU.mult, op1=ALU.mult),
                         [tmp[0], tmp[1], qkw], [tmp[0]])
                    P.op("pe", lambda e: e.matmul(out=paux[1][:, :], lhsT=rot[:, :], rhs=tmp[0][:, :], start=True,
                                                  stop=True), [rot, tmp[0]], [paux[1]])
                    P.op("dve", lambda e, o0=o0: e.tensor_tensor(out=tmp[1][:, :], in0=tmp[0][:, :],
                                                                 in1=cosT[:, o0:o0 + 512], op=ALU.mult),
                         [tmp[0], cosT], [tmp[1]])
                    P.op("dve", lambda e, o0=o0: e.tensor_tensor(out=tmp[2][:, :], in0=paux[1][:, :],
                                                                 in1=sinT[:, o0:o0 + 512], op=ALU.mult),
                         [paux[1], sinT], [tmp[2]])
                    P.op("dve", lambda e, ost=ost, o0=o0: e.tensor_tensor(out=ost[:, o0:o0 + 512], in0=tmp[1][:, :],
                                                                           in1=tmp[2][:, :], op=ALU.add),
                         [tmp[1], tmp[2]], [ost])
                store_rows(ost, dram, i * 128)

        slab = need_slab(10)
        for t in range(1, NT_A):
            pb = pacc[rr["pacc"] % 3]
            rr["pacc"] += 1
            for k in range(16):
                P.op("pe", lambda e, k=k, pb=pb, t=t: e.matmul(out=pb[:, :], lhsT=xnT[:, k, t * 128:(t + 1) * 128],
                                                               rhs=slab[:, k, :], start=(k == 0), stop=(k == 15)),
                     [xnT, slab], [pb])
            vs = vstage[rr["vst"] % 2]
            rr["vst"] += 1
            P.op("act", lambda e, pb=pb, vs=vs, t=t: e.activation(out=vs[:, :], in_=pb[:, :], func=AF.Copy,
                                                                  scale=rstd_all[:, t:t + 1]),
                 [pb, rstd_all], [vs])
            P.dma("sp", lambda e, vs=vs, t=t: e.dma_start(out=v_o.t[(t - 1) * 128:t * 128, :], in_=vs[:, :]),
                  [vs], [v_o], vs)

        gate_group(11, gza_o, 0)

        P.finish([gzs_o, xbc_o, dt_o, ycf_o, q_o, k_o, v_o, gza_o])
        P.emit()
    return nc


_NC_CACHE = {}


def _get(name, fn):
    if name not in _NC_CACHE:
        _NC_CACHE[name] = fn()
    return _NC_CACHE[name]


def _chunk_cols(vec, nchunk):
    return np.ascontiguousarray(np.asarray(vec, np.float32).reshape(nchunk, 128).T)


def run_A(x_full, inp, l, consts, cos, sin):
    nc = _get("A", build_A)
    in_maps = []
    scw = np.ascontiguousarray(np.asarray(inp["ssd_conv_w"][l], np.float32).T.reshape(12, 128, 4).transpose(1, 0, 2))
    cfw = np.ascontiguousarray(np.asarray(inp["cfm_conv_w"][l], np.float32).T.reshape(4, 128, 31).transpose(1, 0, 2))
    qkw = np.ascontiguousarray(np.stack([np.tile(np.asarray(inp["att_q_norm_w"][l], np.float32), 2),
                                         np.tile(np.asarray(inp["att_k_norm_w"][l], np.float32), 2)], axis=1))
    shared = {
        "w_in": np.ascontiguousarray(inp["w_in"][l]),
        "normw": _chunk_cols(inp["norm_w"][l], 16),
        "scw": scw, "scb": _chunk_cols(inp["ssd_conv_b"][l], 12),
        "dtb": np.ascontiguousarray(np.tile(np.asarray(inp["ssd_dt_bias"][l], np.float32)[None, :], (128, 1))),
        "cfw": cfw, "cfb": _chunk_cols(inp["cfm_conv_b"][l], 4),
        "lnw": _chunk_cols(inp["cfm_ln_w"][l], 4), "lnb": _chunk_cols(inp["cfm_ln_b"][l], 4),
        "qkw": qkw,
        "ident_bf": consts["ident_bf"], "ident_f": consts["ident_f"], "ones_f": consts["ones_f"],
        "blk_f": consts["blk_f"], "rot_f": consts["rot_f"],
    }
    for c in range(NCORES):
        t0 = c * TOK
        xe = np.zeros((TA, DM), np.float32)
        xe[128:] = x_full[t0:t0 + TOK]
        if c > 0:
            xe[:128] = x_full[t0 - 128:t0]
        cT = np.ascontiguousarray(np.tile(cos[t0:t0 + TOK].T, (4, 1)))
        sT = np.ascontiguousarray(np.tile(sin[t0:t0 + TOK].T, (4, 1)))
        m = dict(shared)
        m.update({"xe": xe, "cosT": cT, "sinT": sT})
        in_maps.append(m)
    res = run_bass_kernel_spmd(nc, in_maps, core_ids=list(range(NCORES)))
    return res.results


NQT = SEQ // 512
NCH = SEQ // 128
SEG = 2048


def build_B():
    nc = bass.Bass("TRN2", target_bir_lowering=False)
    with ExitStack() as es:
        P = Prog(nc, es)
        D = lambda n, s, d, k="ExternalInput": P.dram(n, s, d, k)
        q_d = D("qT", [64, SEQ], BF16)
        k_d = D("kT", [64, SEQ], BF16)
        v_d = D("v", [SEQ, 128], BF16)
        xs_d = D("xsT", [64, 2, SEQ], BF16)
        b_d = D("BT", [128, SEQ], BF16)
        c_d = D("CT", [128, SEQ], BF16)
        dt_d = D("dt2", [SEQ, 2], F32)
        xtok_d = D("xs_tok", [SEQ, 128], BF16)
        btok_d = D("b_tok", [SEQ, 128], BF16)
        alog_d = D("alog", [128, 2], F32)
        dcol_d = D("dcol", [64, 2], F32)
        onesb_d = D("ones_bf", [128, 128], BF16)
        trib_d = D("tri_bf", [128, 128], BF16)
        identb_d = D("ident_bf", [128, 128], BF16)
        onesf_d = D("ones_f", [128, 128], F32)
        trif_d = D("tri_f", [128, 128], F32)
        o_o = D("oT", [128, SEQ], F32, "ExternalOutput")
        y_o = D("yT", [64, 2, SEQ], F32, "ExternalOutput")

        qs = P.sb("qs", [64, SEQ], BF16)
        ks = P.sb("ks", [64, SEQ], BF16)
        vs = P.sb("vs", [128, NCH, 128], BF16)
        onesb = P.sb("onesb", [128, 128], BF16)
        trib = P.sb("trib", [128, 128], BF16)
        identb = P.sb("identb", [128, 128], BF16)
        onesf = P.sb("onesf", [128, 128], F32)
        trif = P.sb("trif", [128, 128], F32)
        alog = P.sb("alog_s", [128, 2], F32)
        dcol = P.sb("dcol_s", [64, 2], F32)
        dts = P.sb("dts", [128, NCH, 2], F32)
        ptile = [P.sb("ptile%d" % i, [128, 512], BF16) for i in range(3)]
        rl = P.sb("rl", [128, 512], F32)
        ost = [P.sb("ost%d" % i, [128, 512], F32) for i in range(2)]
        xseg = [P.sb("xseg%d" % i, [64, 2, SEG], BF16) for i in range(2)]
        bseg = [P.sb("bseg%d" % i, [128, SEG], BF16) for i in range(2)]
        cseg = [P.sb("cseg%d" % i, [128, SEG], BF16) for i in range(2)]
        yst = [P.sb("yst%d" % i, [64, 2, SEG], F32) for i in range(2)]
        xtseg = [P.sb("xtseg%d" % i, [128, 16, 128], BF16) for i in range(2)]
        btseg = [P.sb("btseg%d" % i, [128, 16, 128], BF16) for i in range(2)]
        a_t = [P.sb("a_t%d" % i, [128, 2], F32) for i in range(2)]
        acs = [P.sb("acs%d" % i, [128, 2], F32) for i in range(2)]
        dsta = [P.sb("dsta%d" % i, [128, 2], F32) for i in range(2)]
        X = [P.sb("X%d" % i, [128, 2, 128], F32) for i in range(2)]
        D1 = [P.sb("D1%d" % i, [128, 2, 128], F32) for i in range(2)]
        ER = [P.sb("ER%d" % i, [128, 2, 128], F32) for i in range(2)]
        Gm = [P.sb("Gm%d" % i, [128, 128], F32) for i in range(2)]
        LT = [P.sb("LT%d" % i, [128, 2, 128], BF16) for i in range(2)]
        Ce = [P.sb("Ce%d" % i, [128, 2, 128], BF16) for i in range(2)]
        xdt = [P.sb("xdt%d" % i, [128, 2, 64], BF16) for i in range(2)]
        xw = [P.sb("xw%d" % i, [128, 2, 64], BF16) for i in range(2)]
        Bt = P.sb("Bt", [128, 128], BF16)
        S = P.sb("S", [128, 2, 64], F32)
        Sbf = [P.sb("Sbf%d" % i, [128, 2, 64], BF16) for i in range(4)]

        ps_s = [P.ps("ps_s%d" % i, [128, 512], F32) for i in range(2)]
        po = P.ps("po", [128, 512], F32)
        pl = P.ps("pl", [128, 512], F32)
        bankA = [P.ps("bankA%d" % i, [128, 512], F32) for i in range(2)]
        bankB = [P.ps("bankB%d" % i, [128, 512], F32) for i in range(2)]

        def load(dst, src, q="sp"):
            P.dma(q, lambda e: e.dma_start(out=dst.ap(), in_=src.ap()), [src], [dst], dst)

        for dst, src in [(onesb, onesb_d), (trib, trib_d), (identb, identb_d), (onesf, onesf_d), (trif, trif_d),
                         (alog, alog_d), (dcol, dcol_d), (qs, q_d), (ks, k_d)]:
            load(dst, src)
        P.dma("sp", lambda e: e.dma_start(out=dts[:, :, :], in_=dt_d.t.rearrange("(j t) h -> t j h", t=128)),
              [dt_d], [dts], dts)
        for part in range(8):
            P.dma("pool", lambda e, part=part: e.dma_start(
                out=vs[:, part * 16:(part + 1) * 16, :],
                in_=v_d.t[part * 2048:(part + 1) * 2048, :].rearrange("(b p) e -> p b e", p=128)),
                [v_d], [vs], vs)
        P.op("act", lambda e: e.activation(out=alog[:, :], in_=alog[:, :], func=AF.Exp), [alog], [alog])
        P.op("dve", lambda e: e.tensor_scalar(out=alog[:, :], in0=alog[:, :], scalar1=-1.0, scalar2=None, op0=ALU.mult),
             [alog], [alog])
        P.op("dve", lambda e: e.memset(S[:, :, :], 0.0), [], [S])
        P.op("dve", lambda e: e.memset(Sbf[0][:, :, :], 0.0), [], [Sbf[0]])

        st = {"pt": 0, "ost": 0}
        _DM = "block"
        _DN = 4
        pend = []

        def Q(eng, fn, reads=(), writes=()):
            pend.append(("op", eng, fn, reads, writes))

        def QD(queue, fn, reads, writes, sembuf):
            pend.append(("dma", queue, fn, reads, writes, sembuf))

        def drain(n):
            while n > 0 and pend:
                it = pend.pop(0)
                if it[0] == "op":
                    P.op(it[1], it[2], it[3], it[4])
                else:
                    P.dma(it[1], it[2], it[3], it[4], it[5])
                n -= 1

        def attn_qtile(qi):
            nkv = 4 * (qi + 1)
            q0 = qi * 512

            def col0(j):
                r = j - 4 * qi
                return 128 * r if r > 0 else 0

            def s_mm(j):
                c0 = col0(j)
                pss = ps_s[j % 2]
                P.op("pe", lambda e: e.matmul(out=pss[:, c0:512], lhsT=ks[:, j * 128:(j + 1) * 128],
                                              rhs=qs[:, q0 + c0:q0 + 512], start=True, stop=True), [ks, qs], [pss])

            s_mm(0)
            for j in range(nkv):
                if j + 1 < nkv:
                    s_mm(j + 1)
                r = j - 4 * qi
                c0 = col0(j)
                pss = ps_s[j % 2]
                pt = ptile[st["pt"] % 3]
                st["pt"] += 1
                P.op("act", lambda e, pss=pss, pt=pt, c0=c0: e.activation(out=pt[:, c0:512], in_=pss[:, c0:512],
                                                                          func=AF.Exp), [pss], [pt])
                if r >= 0:
                    P.op("pool", lambda e, pt=pt, c0=c0: e.tensor_tensor(out=pt[:, c0:c0 + 128], in0=pt[:, c0:c0 + 128],
                                                                         in1=trib[:, :], op=ALU.mult), [pt, trib], [pt])
                P.op("pe", lambda e, pt=pt, j=j, c0=c0: e.matmul(out=po[:, c0:512], lhsT=vs[:, j, :], rhs=pt[:, c0:512],
                                                                 start=(j == 0), stop=(j == nkv - 1)), [vs, pt], [po])
                P.op("pe", lambda e, pt=pt, j=j, c0=c0: e.matmul(out=pl[:, c0:512], lhsT=onesb[:, :], rhs=pt[:, c0:512],
                                                                 start=(j == 0), stop=(j == nkv - 1)), [onesb, pt], [pl])
                drain(_DN)
            o = ost[st["ost"] % 2]
            st["ost"] += 1
            P.op("dve", lambda e: e.reciprocal(out=rl[:, :], in_=pl[:, :]), [pl], [rl])
            P.op("dve", lambda e, o=o: e.tensor_tensor(out=o[:, :], in0=po[:, :], in1=rl[:, :], op=ALU.mult),
                 [po, rl], [o])
            P.dma("sp", lambda e, o=o: e.dma_start(out=o_o.t[:, q0:q0 + 512], in_=o[:, :]), [o], [o_o], o)

        def load_seg(sg, dma=None):
            dma = dma or P.dma
            s0 = sg * SEG
            dma("sp", lambda e: e.dma_start(out=xseg[sg % 2][:, :, :], in_=xs_d.t[:, :, s0:s0 + SEG]),
                [xs_d], [xseg[sg % 2]], xseg[sg % 2])
            dma("sp", lambda e: e.dma_start(out=bseg[sg % 2][:, :], in_=b_d.t[:, s0:s0 + SEG]),
                [b_d], [bseg[sg % 2]], bseg[sg % 2])
            dma("sp", lambda e: e.dma_start(out=cseg[sg % 2][:, :], in_=c_d.t[:, s0:s0 + SEG]),
                [c_d], [cseg[sg % 2]], cseg[sg % 2])
            dma("sp", lambda e: e.dma_start(out=xtseg[sg % 2][:, :, :],
                                            in_=xtok_d.t[s0:s0 + SEG, :].rearrange("(j t) c -> t j c", t=128)),
                [xtok_d], [xtseg[sg % 2]], xtseg[sg % 2])
            dma("sp", lambda e: e.dma_start(out=btseg[sg % 2][:, :, :],
                                            in_=btok_d.t[s0:s0 + SEG, :].rearrange("(j t) c -> t j c", t=128)),
                [btok_d], [btseg[sg % 2]], btseg[sg % 2])

        def ssd_chunk(j):
            H1, H2 = [], []
            cur = [H1]

            def Q(eng, fn, reads=(), writes=()):
                cur[0].append(("op", eng, fn, reads, writes))

            def QD(queue, fn, reads, writes, sembuf):
                cur[0].append(("dma", queue, fn, reads, writes, sembuf))

            p = j % 2
            sg = j // 16
            c = (j % 16) * 128
            jj = j % 16
            xg, bg, cg, yg = xseg[sg % 2], bseg[sg % 2], cseg[sg % 2], yst[sg % 2]
            xtg, btg = xtseg[sg % 2], btseg[sg % 2]
            bA, bB = bankA[p], bankB[p]
            a_, acs_, dsta_, X_, D1_, ER_, Gm_, LT_, Ce_, xdt_, xw_ = (a_t[p], acs[p], dsta[p], X[p], D1[p], ER[p],
                                                                      Gm[p], LT[p], Ce[p], xdt[p], xw[p])
            sb_in, sb_out = Sbf[j % 4], Sbf[(j + 1) % 4]
            R3 = lambda: bA[:, 0:256].rearrange("p (h l) -> p h l", h=2)
            Q("dve", lambda e: e.tensor_tensor(out=a_[:, :], in0=dts[:, j, :], in1=alog[:, :], op=ALU.mult),
              [dts, alog], [a_])
            Q("pe", lambda e: e.matmul(out=bB[:, 256:258], lhsT=trif[:, :], rhs=a_[:, :], start=True, stop=True),
              [trif, a_], [bB])
            Q("dve", lambda e: e.tensor_copy(out=acs_[:, :], in_=bB[:, 256:258]), [bB], [acs_])
            for h in range(2):
                Q("dve", lambda e, h=h: e.tensor_scalar(out=X_[:, h, :], in0=trif[:, :], scalar1=a_[:, h:h + 1],
                                                        scalar2=None, op0=ALU.mult), [trif, a_], [X_])
            Q("pe", lambda e: e.matmul(out=bA[:, 0:256], lhsT=onesf[:, :],
                                       rhs=X_[:, :, :].rearrange("p h l -> p (h l)"), start=True, stop=True),
              [onesf, X_], [bA])
            Q("pe", lambda e: e.matmul(out=bB[:, 0:128], lhsT=bg[:, c:c + 128], rhs=cg[:, c:c + 128], start=True,
                                       stop=True), [bg, cg], [bB])
            Q("dve", lambda e: e.tensor_tensor(out=Gm_[:, :], in0=bB[:, 0:128], in1=trif[:, :], op=ALU.mult),
              [bB, trif], [Gm_])
            Q("dve", lambda e: e.tensor_tensor(out=xdt_[:, :, :],
                                               in0=xtg[:, jj, :].rearrange("p (h d) -> p h d", h=2),
                                               in1=bc3(dts[:, j, :], 64), op=ALU.mult), [xtg, dts], [xdt_])
            for h in range(2):
                Q("dve", lambda e, h=h: e.tensor_scalar(out=D1_[:, h, :], in0=bA[:, h * 128:(h + 1) * 128],
                                                        scalar1=acs_[:, h:h + 1], scalar2=0.0, op0=ALU.subtract,
                                                        op1=ALU.min), [bA, acs_], [D1_])
            Q("act", lambda e: e.activation(out=D1_[:, :, :], in_=D1_[:, :, :], func=AF.Exp), [D1_], [D1_])
            Q("act", lambda e: e.activation(out=ER_[:, :, :], in_=R3(), func=AF.Exp), [bA], [ER_])
            Q("dve", lambda e: e.tensor_tensor(out=dsta_[:, :], in0=R3()[:, :, 127], in1=acs_[:, :], op=ALU.subtract),
              [bA, acs_], [dsta_])
            Q("act", lambda e: e.activation(out=dsta_[:, :], in_=dsta_[:, :], func=AF.Exp), [dsta_], [dsta_])
            Q("dve", lambda e: e.tensor_tensor(out=LT_[:, :, :], in0=D1_[:, :, :],
                                               in1=Gm_[:, :].unsqueeze(1).to_broadcast([128, 2, 128]), op=ALU.mult),
              [D1_, Gm_], [LT_])
            Q("dve", lambda e: e.tensor_tensor(out=Ce_[:, :, :], in0=ER_[:, :, :],
                                               in1=cg[:, c:c + 128].unsqueeze(1).to_broadcast([128, 2, 128]),
                                               op=ALU.mult), [ER_, cg], [Ce_])
            Q("dve", lambda e: e.tensor_tensor(out=xw_[:, :, :], in0=xdt_[:, :, :], in1=bc3(dsta_[:, :], 64),
                                               op=ALU.mult), [xdt_, dsta_], [xw_])
            Q("pe", lambda e: e.matmul(out=bB[:, 128:256], lhsT=btg[:, jj, :],
                                       rhs=xw_[:, :, :].rearrange("p h d -> p (h d)"), start=True, stop=True),
              [btg, xw_], [bB])
            cur[0] = H2
            for h in range(2):
                Q("pe", lambda e, h=h: e.matmul(out=bA[0:64, 256 + h * 128:256 + (h + 1) * 128], lhsT=xdt_[:, h, :],
                                                rhs=LT_[:, h, :], start=True, stop=False), [xdt_, LT_], [bA])
                Q("pe", lambda e, h=h: e.matmul(out=bA[0:64, 256 + h * 128:256 + (h + 1) * 128], lhsT=sb_in[:, h, :],
                                                rhs=Ce_[:, h, :], start=False, stop=True), [sb_in, Ce_], [bA])
            for h in range(2):
                Q("dve", lambda e, h=h: e.scalar_tensor_tensor(out=S[:, h, :], in0=S[:, h, :],
                                                               scalar=ER_[:, h, 127:128],
                                                               in1=bB[:, 128 + h * 64:128 + (h + 1) * 64],
                                                               op0=ALU.mult, op1=ALU.add), [S, ER_, bB], [S])
            Q("act", lambda e: e.copy(out=sb_out[:, :, :], in_=S[:, :, :]), [S], [sb_out])
            for h in range(2):
                Q("dve", lambda e, h=h: e.scalar_tensor_tensor(out=yg[:, h, c:c + 128], in0=xg[:, h, c:c + 128],
                                                               scalar=dcol[:, h:h + 1],
                                                               in1=bA[0:64, 256 + h * 128:256 + (h + 1) * 128],
                                                               op0=ALU.mult, op1=ALU.add), [xg, dcol, bA], [yg])
            if j % 16 == 15:
                s0 = sg * SEG
                QD("sp", lambda e: e.dma_start(out=y_o.t[:, :, s0:s0 + SEG], in_=yg[:, :, :]), [yg], [y_o], yg)
            if j % 16 == 0 and sg + 1 < SEQ // SEG:
                load_seg(sg + 1, QD)
            return H1, H2

        def merge(a, b):
            out, ia, ib = [], 0, 0
            na, nb = len(a), len(b)
            while ia < na or ib < nb:
                if ib >= nb or (ia < na and ia * nb <= ib * na):
                    out.append(a[ia]); ia += 1
                else:
                    out.append(b[ib]); ib += 1
            return out

        load_seg(0)
        prev_tail = []
        for j in range(NCH):
            h1, h2 = ssd_chunk(j)
            pend.extend(merge(prev_tail, h1))
            prev_tail = h2
        pend.extend(prev_tail)
        for qi in range(NQT):
            attn_qtile(qi)
        drain(1 << 30)
        P.finish([o_o, y_o])
        P.emit()
    return nc


def build_C():
    nc = bass.Bass("TRN2", target_bir_lowering=False)
    with ExitStack() as es:
        P = Prog(nc, es)
        D = lambda n, s, d, k="ExternalInput": P.dram(n, s, d, k)
        y_d = D("yssdT", [1024, TOK], F32)
        gzs_d = D("gz_ssdT", [1024, TOK], BF16)
        ycf_d = D("ycfmT", [512, TOK], BF16)
        o_d = D("oT", [8, 128, TOK], F32)
        gza_d = D("gz_attT", [512, TOK], BF16)
        x_d = D("x", [TOK, DM], F32)
        w_d = D("w_out", [DM, DM], F32)
        snw_d = D("snw", [128, 8], F32)
        subw_d = D("subw", [128, 1], F32)
        lam_d = D("lamv", [128, 4, 64], F32)
        linit_d = D("linit", [128, 2], F32)
        onesf_d = D("ones_f", [128, 128], F32)
        x_o = D("x_out", [TOK, DM], F32, "ExternalOutput")

        Wsb = P.sb("Wsb", [128, 16, DM], BF16)
        ybuf = [P.sb("ybuf%d" % i, [128, 16, 512], BF16) for i in range(2)]
        gbuf = P.sb("gbuf", [128, 8, 512], F32)
        yin = [P.sb("yin%d" % i, [128, 512], F32) for i in range(2)]
        gzin = [P.sb("gzin%d" % i, [128, 512], BF16) for i in range(2)]
        oin = [P.sb("oin%d" % i, [128, 512], F32) for i in range(4)]
        tmp = [P.sb("tmpc%d" % i, [128, 512], F32) for i in range(3)]
        xt = [P.sb("xt%d" % i, [128, DM], F32) for i in range(2)]
        xo = [P.sb("xo%d" % i, [128, DM], F32) for i in range(2)]
        snw = P.sb("snw_s", [128, 8], F32)
        subw = P.sb("subw_s", [128, 1], F32)
        lam = P.sb("lam_s", [128, 4, 64], F32)
        linit = P.sb("linit_s", [128, 2], F32)
        onesf = P.sb("onesf_s", [128, 128], F32)
        lsc = P.sb("lsc", [128, 4], F32)
        pacc = [P.ps("pacc%d" % i, [128, 512], F32) for i in range(4)]
        paux = [P.ps("paux%d" % i, [128, 512], F32) for i in range(2)]

        def load(dst, src, q="sp"):
            P.dma(q, lambda e: e.dma_start(out=dst.ap(), in_=src.ap()), [src], [dst], dst)

        for dst, src in [(snw, snw_d), (subw, subw_d), (lam, lam_d), (linit, linit_d), (onesf, onesf_d)]:
            load(dst, src)
        for part in range(4):
            P.dma("pool", lambda e, part=part: e.dma_start(
                out=Wsb[:, part * 4:(part + 1) * 4, :],
                in_=w_d.t[part * 512:(part + 1) * 512, :].rearrange("(k p) c -> p k c", p=128)), [w_d], [Wsb], Wsb)
        P.op("dve", lambda e: e.tensor_tensor(out=lam[:, 0, :], in0=lam[:, 0, :], in1=lam[:, 1, :], op=ALU.mult),
             [lam], [lam])
        P.op("dve", lambda e: e.tensor_tensor(out=lam[:, 2, :], in0=lam[:, 2, :], in1=lam[:, 3, :], op=ALU.mult),
             [lam], [lam])
        P.op("dve", lambda e: e.tensor_reduce(out=lsc[:, 0:1], in_=lam[:, 0, :], axis=AX.X, op=ALU.add), [lam], [lsc])
        P.op("dve", lambda e: e.tensor_reduce(out=lsc[:, 1:2], in_=lam[:, 2, :], axis=AX.X, op=ALU.add), [lam], [lsc])
        P.op("act", lambda e: e.activation(out=lsc[:, 0:2], in_=lsc[:, 0:2], func=AF.Exp), [lsc], [lsc])
        P.op("dve", lambda e: e.tensor_tensor(out=lsc[:, 2:3], in0=lsc[:, 1:2], in1=lsc[:, 0:1], op=ALU.subtract),
             [lsc], [lsc])
        P.op("dve", lambda e: e.tensor_tensor(out=lsc[:, 3:4], in0=lsc[:, 2:3], in1=linit[:, 0:1], op=ALU.subtract),
             [lsc, linit], [lsc])

        rr = {"ld": 0, "o": 0, "pacc": 0}
        for tg in range(4):
            t0 = tg * 512
            yb = ybuf[tg % 2]
            for i in range(8):
                yi = yin[rr["ld"] % 2]
                gi = gzin[rr["ld"] % 2]
                rr["ld"] += 1
                P.dma("sp", lambda e, yi=yi, i=i, t0=t0: e.dma_start(out=yi[:, :], in_=y_d.t[i * 128:(i + 1) * 128, t0:t0 + 512]),
                      [y_d], [yi], yi)
                P.dma("sp", lambda e, gi=gi, i=i, t0=t0: e.dma_start(out=gi[:, :], in_=gzs_d.t[i * 128:(i + 1) * 128, t0:t0 + 512]),
                      [gzs_d], [gi], gi)
                P.op("dve", lambda e, yi=yi, gi=gi, i=i: e.tensor_tensor(out=gbuf[:, i, :], in0=yi[:, :], in1=gi[:, :],
                                                                          op=ALU.mult), [yi, gi], [gbuf])
                P.op("act", lambda e, i=i: e.activation(out=tmp[0][:, :], in_=gbuf[:, i, :], func=AF.Square),
                     [gbuf], [tmp[0]])
                P.op("pe", lambda e, i=i: e.matmul(out=paux[0][:, :], lhsT=onesf[:, :], rhs=tmp[0][:, :], start=(i == 0),
                                                   stop=(i == 7)), [onesf, tmp[0]], [paux[0]])
            P.op("act", lambda e: e.activation(out=tmp[1][:, :], in_=paux[0][:, :], func=AF.Ln, bias=EPS,
                                               scale=1.0 / 1024), [paux[0]], [tmp[1]])
            P.op("act", lambda e: e.activation(out=tmp[1][:, :], in_=tmp[1][:, :], func=AF.Exp, scale=-0.5),
                 [tmp[1]], [tmp[1]])
            for i in range(8):
                P.op("dve", lambda e, i=i, yb=yb: e.scalar_tensor_tensor(out=yb[:, i, :], in0=gbuf[:, i, :],
                                                                         scalar=snw[:, i:i + 1], in1=tmp[1][:, :],
                                                                         op0=ALU.mult, op1=ALU.mult),
                     [gbuf, snw, tmp[1]], [yb])
            for i in range(4):
                P.dma("sp", lambda e, i=i, yb=yb, t0=t0: e.dma_start(out=yb[:, 8 + i, :],
                                                              in_=ycf_d.t[i * 128:(i + 1) * 128, t0:t0 + 512]),
                      [ycf_d], [yb], yb)
            for h in range(4):
                o0 = oin[rr["o"] % 4]
                o1 = oin[(rr["o"] + 1) % 4]
                rr["o"] += 2
                gi = gzin[rr["ld"] % 2]
                rr["ld"] += 1
                P.dma("sp", lambda e, o0=o0, h=h, t0=t0: e.dma_start(out=o0[:, :], in_=o_d.t[2 * h, :, t0:t0 + 512]),
                      [o_d], [o0], o0)
                P.dma("sp", lambda e, o1=o1, h=h, t0=t0: e.dma_start(out=o1[:, :], in_=o_d.t[2 * h + 1, :, t0:t0 + 512]),
                      [o_d], [o1], o1)
                P.dma("sp", lambda e, gi=gi, h=h, t0=t0: e.dma_start(out=gi[:, :], in_=gza_d.t[h * 128:(h + 1) * 128, t0:t0 + 512]),
                      [gza_d], [gi], gi)
                P.op("dve", lambda e, o0=o0, o1=o1: e.scalar_tensor_tensor(out=tmp[0][:, :], in0=o1[:, :],
                                                                           scalar=lsc[:, 3:4], in1=o0[:, :],
                                                                           op0=ALU.mult, op1=ALU.add),
                     [o0, o1, lsc], [tmp[0]])
                P.op("act", lambda e: e.activation(out=tmp[2][:, :], in_=tmp[0][:, :], func=AF.Square),
                     [tmp[0]], [tmp[2]])
                P.op("pe", lambda e: e.matmul(out=paux[1][:, :], lhsT=onesf[:, :], rhs=tmp[2][:, :], start=True,
                                              stop=True), [onesf, tmp[2]], [paux[1]])
                P.op("act", lambda e: e.activation(out=tmp[2][:, :], in_=paux[1][:, :], func=AF.Ln, bias=EPS,
                                                   scale=1.0 / 128), [paux[1]], [tmp[2]])
                P.op("act", lambda e: e.activation(out=tmp[2][:, :], in_=tmp[2][:, :], func=AF.Exp, scale=-0.5),
                     [tmp[2]], [tmp[2]])
                P.op("dve", lambda e: e.scalar_tensor_tensor(out=tmp[0][:, :], in0=tmp[0][:, :], scalar=subw[:, 0:1],
                                                             in1=tmp[2][:, :], op0=ALU.mult, op1=ALU.mult),
                     [tmp[0], subw, tmp[2]], [tmp[0]])
                P.op("dve", lambda e, gi=gi, h=h, yb=yb: e.scalar_tensor_tensor(out=yb[:, 12 + h, :], in0=tmp[0][:, :],
                                                                                scalar=linit[:, 1:2], in1=gi[:, :],
                                                                                op0=ALU.mult, op1=ALU.mult),
                     [tmp[0], linit, gi], [yb])
            for tt in range(4):
                r0 = t0 + tt * 128
                xti = xt[tt % 2]
                xoi = xo[tt % 2]
                P.dma("sp", lambda e, xti=xti, r0=r0: e.dma_start(out=xti[:, :], in_=x_d.t[r0:r0 + 128, :]),
                      [x_d], [xti], xti)
                for cg in range(4):
                    pb = pacc[rr["pacc"] % 4]
                    rr["pacc"] += 1
                    for kk in range(16):
                        P.op("pe", lambda e, pb=pb, kk=kk, tt=tt, cg=cg, yb=yb: e.matmul(
                            out=pb[:, :], lhsT=yb[:, kk, tt * 128:(tt + 1) * 128],
                            rhs=Wsb[:, kk, cg * 512:(cg + 1) * 512], start=(kk == 0), stop=(kk == 15)),
                            [yb, Wsb], [pb])
                    P.op("dve", lambda e, pb=pb, cg=cg, xti=xti, xoi=xoi: e.tensor_tensor(
                        out=xoi[:, cg * 512:(cg + 1) * 512], in0=pb[:, :], in1=xti[:, cg * 512:(cg + 1) * 512],
                        op=ALU.add), [pb, xti], [xoi])
                P.dma("sp", lambda e, xoi=xoi, r0=r0: e.dma_start(out=x_o.t[r0:r0 + 128, :], in_=xoi[:, :]),
                      [xoi], [x_o], xoi)
        P.finish([x_o])
        P.emit()
    return nc


def run_B(resA, inp, l, consts):
    nc = _get("B", build_B)
    qT = np.concatenate([np.asarray(r["qT"]) for r in resA], axis=1)
    kT = np.concatenate([np.asarray(r["kT"]) for r in resA], axis=1)
    v = np.concatenate([np.asarray(r["v"]) for r in resA], axis=0)
    xbcT = np.concatenate([np.asarray(r["xbcT"]) for r in resA], axis=1)
    dt = np.concatenate([np.asarray(r["dt"]) for r in resA], axis=0)
    alog = np.asarray(inp["ssd_a_log"][l], np.float32)
    dsk = np.asarray(inp["ssd_d"][l], np.float32)
    in_maps = []
    for u in range(NCORES):
        h, g = u // 2, u // 4
        xs = xbcT[128 * u:128 * (u + 1)].reshape(2, 64, SEQ).transpose(1, 0, 2)
        in_maps.append({
            "qT": np.ascontiguousarray(qT[64 * u:64 * (u + 1)]),
            "kT": np.ascontiguousarray(kT[64 * u:64 * (u + 1)]),
            "v": np.ascontiguousarray(v[:, 128 * h:128 * (h + 1)]),
            "xsT": np.ascontiguousarray(xs),
            "BT": np.ascontiguousarray(xbcT[1024 + 128 * g:1024 + 128 * (g + 1)]),
            "CT": np.ascontiguousarray(xbcT[1280 + 128 * g:1280 + 128 * (g + 1)]),
            "dt2": np.ascontiguousarray(dt[:, 2 * u:2 * u + 2]),
            "xs_tok": np.ascontiguousarray(xbcT[128 * u:128 * (u + 1)].T),
            "b_tok": np.ascontiguousarray(xbcT[1024 + 128 * g:1024 + 128 * (g + 1)].T),
            "alog": np.ascontiguousarray(np.tile(alog[None, 2 * u:2 * u + 2], (128, 1))),
            "dcol": np.ascontiguousarray(np.tile(dsk[None, 2 * u:2 * u + 2], (64, 1))),
            "ones_bf": consts["ones_f"].astype(NPBF), "tri_bf": consts["tri_bf"], "ident_bf": consts["ident_bf"],
            "ones_f": consts["ones_f"], "tri_f": consts["tri_f"],
        })
    res = run_bass_kernel_spmd(nc, in_maps, core_ids=list(range(NCORES)))
    return res.results


def run_C(x_full, resA, resB, inp, l, consts):
    nc = _get("C", build_C)
    linit_v = 0.8 - 0.6 * math.exp(-0.3 * l)
    yT = np.concatenate([np.asarray(r["yT"]).transpose(1, 0, 2).reshape(128, SEQ) for r in resB], axis=0)
    oT = np.stack([np.asarray(r["oT"]) for r in resB], axis=0)
    lamv = np.stack([np.asarray(inp[k][l], np.float32) for k in
                     ("att_lambda_q1", "att_lambda_k1", "att_lambda_q2", "att_lambda_k2")], axis=0)
    shared = {
        "w_out": np.ascontiguousarray(inp["w_out"][l]),
        "snw": _chunk_cols(inp["ssd_norm_w"][l], 8),
        "subw": np.ascontiguousarray(np.asarray(inp["att_subln_w"][l], np.float32).reshape(128, 1)),
        "lamv": np.ascontiguousarray(np.tile(lamv[None], (128, 1, 1))),
        "linit": np.ascontiguousarray(np.tile(np.array([[linit_v, 1.0 - linit_v]], np.float32), (128, 1))),
        "ones_f": consts["ones_f"],
    }
    in_maps = []
    for c in range(NCORES):
        sl = slice(c * TOK, (c + 1) * TOK)
        m = dict(shared)
        m.update({
            "yssdT": np.ascontiguousarray(yT[:, sl]),
            "gz_ssdT": np.asarray(resA[c]["gz_ssdT"]),
            "ycfmT": np.asarray(resA[c]["ycfmT"]),
            "oT": np.ascontiguousarray(oT[:, :, sl]),
            "gz_attT": np.asarray(resA[c]["gz_attT"]),
            "x": np.ascontiguousarray(x_full[sl]),
        })
        in_maps.append(m)
    res = run_bass_kernel_spmd(nc, in_maps, core_ids=list(range(NCORES)))
    return np.concatenate([np.asarray(r["x_out"]) for r in res.results], axis=0)


def kernel(**inputs):
    inp = {k: np.asarray(v) for k, v in inputs.items()}
    consts = _consts()
    cos, sin = _rope_tables()
    x = np.ascontiguousarray(inp["x"][0], dtype=np.float32)
    for l in range(2):
        resA = run_A(x, inp, l, consts, cos, sin)
        resB = run_B(resA, inp, l, consts)
        x = run_C(x, resA, resB, inp, l, consts)
    return x[None].astype(np.float32)
```

```python
import math
from contextlib import ExitStack

import numpy as np
import ml_dtypes

import concourse.bass as bass
import concourse.mybir as mybir
from concourse.bass_utils import run_bass_kernel_spmd

F32 = mybir.dt.float32
BF16 = mybir.dt.bfloat16
AF = mybir.ActivationFunctionType
ALU = mybir.AluOpType
AX = mybir.AxisListType
NPBF = ml_dtypes.bfloat16

NCORES = 8
SEQ = 16384
DM = 2048
DIN = 6160
TOK = SEQ // NCORES
EPS = 1e-6
ENGS = ["pe", "dve", "act", "pool", "sp"]


class Buf:
    def __init__(self, t, name, is_ap=False):
        self.t = t
        self.name = name
        self.is_ap = is_ap
        self.w = None
        self.r = {}
        self.dsem = None
        self.dcnt = 0

    def __getitem__(self, k):
        return self.t[k]

    def ap(self):
        return self.t if self.is_ap else self.t[:]


def alias(dst, srcs):
    for s in srcs:
        if s.w is not None:
            dst.r[("w", id(s.w[0]))] = s.w
        for k, v in s.r.items():
            dst.r[("r", k)] = v


class Prog:
    def __init__(self, nc, es):
        self.nc = nc
        self.es = es
        self.sem = {e: es.enter_context(nc.semaphore("sem_" + e)) for e in ENGS}
        self.cnt = {e: 0 for e in ENGS}
        self.q = {e: [] for e in ENGS}
        self.waited = {e: {} for e in ENGS}
        self.nds = 0

    def sb(self, name, shape, dt):
        return Buf(self.es.enter_context(self.nc.sbuf_tensor(name, list(shape), dt)), name)

    def ps(self, name, shape, dt):
        return Buf(self.es.enter_context(self.nc.psum_tensor(name, list(shape), dt)), name)

    def dram(self, name, shape, dt, kind):
        return Buf(self.nc.dram_tensor(name, list(shape), dt, kind=kind).ap(), name, is_ap=True)

    def _deps(self, eng, reads, writes):
        deps = []
        for b in reads:
            if b.w is not None:
                deps.append(b.w)
        for b in writes:
            if b.w is not None:
                deps.append(b.w)
            deps.extend(b.r.values())
        out = []
        for (sem, val, src) in deps:
            if src == "pe" and eng == "pe":
                continue
            key = id(sem)
            if self.waited[eng].get(key, 0) >= val:
                continue
            self.waited[eng][key] = val
            out.append((sem, val))
        return out

    def _mark(self, ev, reads, writes):
        for b in reads:
            b.r[id(ev[0])] = ev
        for b in writes:
            b.w = ev
            b.r = {}

    def op(self, eng, fn, reads=(), writes=()):
        waits = self._deps(eng, reads, writes)
        self.cnt[eng] += 1
        ev = (self.sem[eng], self.cnt[eng], eng)
        self._mark(ev, reads, writes)
        self.q[eng].append((waits, fn, (self.sem[eng], 1)))

    def dma(self, queue, fn, reads, writes, sembuf):
        waits = self._deps(queue, reads, writes)
        if sembuf.dsem is None:
            sembuf.dsem = self.es.enter_context(self.nc.semaphore("ds%d" % self.nds))
            self.nds += 1
        sembuf.dcnt += 1
        ev = (sembuf.dsem, 16 * sembuf.dcnt, "dma")
        self._mark(ev, reads, writes)
        self.q[queue].append((waits, fn, (sembuf.dsem, 16)))

    def finish(self, outs):
        waits = []
        for b in outs:
            if b.w is not None:
                waits.append((b.w[0], b.w[1]))
        self.q["sp"].append((waits, None, None))

    def emit(self):
        nc = self.nc

        def replay(name, e):
            for (waits, fn, inc) in self.q[name]:
                for (sem, val) in waits:
                    e.wait_ge(sem, val)
                if fn is None:
                    continue
                ins = fn(e)
                ins.then_inc(inc[0], inc[1])

        with nc.Block() as block:
            @block.tensor
            def _(e):
                replay("pe", e)

            @block.vector
            def _(e):
                replay("dve", e)

            @block.scalar
            def _(e):
                replay("act", e)

            @block.gpsimd
            def _(e):
                replay("pool", e)

            @block.sync
            def _(e):
                replay("sp", e)


def bc3(ap2, n):
    p, k = ap2.shape
    return ap2.unsqueeze(2).to_broadcast([p, k, n])


def _consts():
    c = {}
    c["ident_bf"] = np.eye(128, dtype=np.float32).astype(NPBF)
    c["ident_f"] = np.eye(128, dtype=np.float32)
    c["ones_f"] = np.ones((128, 128), np.float32)
    blk = np.zeros((128, 128), np.float32)
    blk[:64, :64] = 1.0
    blk[64:, 64:] = 1.0
    c["blk_f"] = blk
    rot = np.zeros((128, 128), np.float32)
    for b0 in (0, 64):
        for m in range(32):
            rot[b0 + m + 32, b0 + m] = -1.0
        for m in range(32, 64):
            rot[b0 + m - 32, b0 + m] = 1.0
    c["rot_f"] = rot
    tri = np.triu(np.ones((128, 128), np.float32))
    c["tri_f"] = tri
    c["tri_bf"] = tri.astype(NPBF)
    return c


def _rope_tables():
    inv_freq = (1.0 / (10000.0 ** (np.arange(0, 64, 2, dtype=np.float32) / np.float32(64)))).astype(np.float32)
    pos = np.arange(SEQ, dtype=np.float32)
    ang = (pos[:, None] * inv_freq[None, :]).astype(np.float32)
    return np.cos(ang).astype(np.float32), np.sin(ang).astype(np.float32)


NT_A = 17
TA = NT_A * 128


def build_A():
    nc = bass.Bass("TRN2", target_bir_lowering=False)
    with ExitStack() as es:
        P = Prog(nc, es)
        D = lambda n, s, d, k="ExternalInput": P.dram(n, s, d, k)
        xe = D("xe", [TA, DM], F32)
        w_in = D("w_in", [DM, DIN], F32)
        normw_d = D("normw", [128, 16], F32)
        scw_d = D("scw", [128, 12, 4], F32)
        scb_d = D("scb", [128, 12], F32)
        dtb_d = D("dtb", [128, 16], F32)
        cfw_d = D("cfw", [128, 4, 31], F32)
        cfb_d = D("cfb", [128, 4], F32)
        lnw_d = D("lnw", [128, 4], F32)
        lnb_d = D("lnb", [128, 4], F32)
        qkw_d = D("qkw", [128, 2], F32)
        cos_d = D("cosT", [128, TOK], F32)
        sin_d = D("sinT", [128, TOK], F32)
        identb_d = D("ident_bf", [128, 128], BF16)
        identf_d = D("ident_f", [128, 128], F32)
        ones_d = D("ones_f", [128, 128], F32)
        blk_d = D("blk_f", [128, 128], F32)
        rot_d = D("rot_f", [128, 128], F32)
        O = "ExternalOutput"
        gzs_o = D("gz_ssdT", [1024, TOK], BF16, O)
        xbc_o = D("xbcT", [1536, TOK], BF16, O)
        dt_o = D("dt", [TOK, 16], F32, O)
        ycf_o = D("ycfmT", [512, TOK], BF16, O)
        q_o = D("qT", [512, TOK], BF16, O)
        k_o = D("kT", [512, TOK], BF16, O)
        v_o = D("v", [TOK, 512], BF16, O)
        gza_o = D("gz_attT", [512, TOK], BF16, O)

        xnT = P.sb("xnT", [128, 16, TA], BF16)
        rstd_bc = P.sb("rstd_bc", [128, TA], F32)
        ssq = P.sb("ssq", [128, NT_A], F32)
        rstd_all = P.sb("rstd_all", [128, NT_A], F32)
        normw = P.sb("normw_s", [128, 16], F32)
        scw = P.sb("scw_s", [128, 12, 4], F32)
        scb = P.sb("scb_s", [128, 12], F32)
        dtb = P.sb("dtb_s", [128, 16], F32)
        cfw = P.sb("cfw_s", [128, 4, 31], F32)
        cfb = P.sb("cfb_s", [128, 4], F32)
        lnw = P.sb("lnw_s", [128, 4], F32)
        lnb = P.sb("lnb_s", [128, 4], F32)
        qkw = P.sb("qkw_s", [128, 2], F32)
        identb = P.sb("identb_s", [128, 128], BF16)
        identf = P.sb("identf_s", [128, 128], F32)
        ones = P.sb("ones_s", [128, 128], F32)
        blk = P.sb("blk_s", [128, 128], F32)
        rot = P.sb("rot_s", [128, 128], F32)
        cc_t = es.enter_context(nc.sbuf_tensor("cc_t", [128, 4, TA], F32))
        junk = Buf(cc_t[:, 0, :], "junk", is_ap=True)
        accb = Buf(cc_t[:, 1, :], "accb", is_ap=True)
        big = [Buf(cc_t[:, 2, :], "big0", is_ap=True), Buf(cc_t[:, 3, :], "big1", is_ap=True)]
        cc = Buf(cc_t[:, :, :], "cc", is_ap=True)
        cosT = Buf(cc_t[:, 0, 0:TOK], "cosT", is_ap=True)
        sinT = Buf(cc_t[:, 1, 0:TOK], "sinT", is_ap=True)
        xb = P.sb("xb", [128, DM], BF16)
        diag = P.sb("diag", [128, 128], F32)
        wslab = [P.sb("wslab%d" % i, [128, 16, 512], BF16) for i in range(2)]
        wdt = P.sb("wdt", [128, 16, 16], BF16)
        tmp = [P.sb("tmp%d" % i, [128, 512], F32) for i in range(3)]
        ostage = [P.sb("ostage%d" % i, [128, TOK], BF16) for i in range(2)]
        vstage = [P.sb("vstage%d" % i, [128, 512], BF16) for i in range(2)]
        ubf = P.sb("ubf", [128, TA], BF16)
        dg = P.sb("dg", [128, 31, 128], BF16)
        mean_t = P.sb("mean_t", [128, 512], F32)
        rln_t = P.sb("rln_t", [128, 512], F32)
        dtbuf = P.sb("dtbuf", [128, 16, 16], F32)

        pT = P.ps("pT", [128, DM], BF16)
        pbc = P.ps("pbc", [128, 512], F32)
        pacc = [P.ps("pacc%d" % i, [128, 512], F32) for i in range(3)]
        paux = [P.ps("paux%d" % i, [128, 512], F32) for i in range(2)]

        def load(dst, src, q="sp"):
            P.dma(q, lambda e: e.dma_start(out=dst.ap(), in_=src.ap()), [src], [dst], dst)

        for dst, src in [(normw, normw_d), (scw, scw_d), (scb, scb_d), (dtb, dtb_d), (cfw, cfw_d), (cfb, cfb_d),
                         (lnw, lnw_d), (lnb, lnb_d), (qkw, qkw_d), (identb, identb_d),
                         (identf, identf_d), (ones, ones_d), (blk, blk_d), (rot, rot_d)]:
            load(dst, src)
        P.op("dve", lambda e: e.tensor_scalar(out=qkw[:, 0:1], in0=qkw[:, 0:1], scalar1=0.125, scalar2=None,
                                              op0=ALU.mult), [qkw], [qkw])

        GCOLS = [0, 512, 1024, 1536, 2048, 2576, 3088, 3600, 4112, 4624, 5136, 5648]
        slab_state = {"next": 0}

        def issue_slab(gi):
            slot = wslab[gi % 2]
            c0 = GCOLS[gi]
            for part in range(4):
                src = w_in.t[part * 512:(part + 1) * 512, c0:c0 + 512].rearrange("(k p) c -> p k c", p=128)
                dst = slot[:, part * 4:(part + 1) * 4, :]
                P.dma("pool", lambda e, s=src, d=dst: e.dma_start(out=d, in_=s), [w_in], [slot], slot)

        def need_slab(gi):
            while slab_state["next"] <= min(gi + 1, len(GCOLS) - 1):
                issue_slab(slab_state["next"])
                slab_state["next"] += 1
            return wslab[gi % 2]

        P.dma("pool", lambda e: e.dma_start(out=wdt[:, :, :],
                                            in_=w_in.t[:, 2560:2576].rearrange("(k p) c -> p k c", p=128)),
              [w_in], [wdt], wdt)
        need_slab(0)

        for t in range(NT_A):
            xt = big[t % 2]
            P.dma("sp", lambda e, t=t, xt=xt: e.dma_start(out=xt[:, 0:DM], in_=xe.t[t * 128:(t + 1) * 128, :]),
                  [xe], [xt], xt)
            P.op("pool", lambda e, xt=xt: e.tensor_copy(out=xb[:, :], in_=xt[:, 0:DM]), [xt], [xb])
            P.op("act", lambda e, xt=xt: e.activation(out=junk[:, 0:DM], in_=xt[:, 0:DM], func=AF.Square),
                 [xt], [junk])
            P.op("dve", lambda e, t=t: e.tensor_reduce(out=ssq[:, t:t + 1], in_=junk[:, 0:DM], axis=AX.X,
                                                       op=ALU.add), [junk], [ssq])
            for k in range(16):
                P.op("pe", lambda e, k=k: e.transpose(out=pT[:, k * 128:(k + 1) * 128],
                                                      in_=xb[:, k * 128:(k + 1) * 128], identity=identb[:, :]),
                     [xb, identb], [pT])
            P.op("dve", lambda e, t=t: e.tensor_tensor(
                out=xnT[:, :, t * 128:(t + 1) * 128],
                in0=pT[:, :].rearrange("p (k t) -> p k t", k=16),
                in1=bc3(normw[:, :], 128), op=ALU.mult), [pT, normw], [xnT])
        P.op("act", lambda e: e.activation(out=rstd_all[:, :], in_=ssq[:, :], func=AF.Sqrt, bias=EPS,
                                           scale=1.0 / DM), [ssq], [rstd_all])
        P.op("dve", lambda e: e.reciprocal(out=rstd_all[:, :], in_=rstd_all[:, :]), [rstd_all], [rstd_all])
        for t in range(NT_A):
            P.op("dve", lambda e, t=t: e.tensor_scalar(out=diag[:, :], in0=identf[:, :], scalar1=rstd_all[:, t:t + 1],
                                                       scalar2=None, op0=ALU.mult), [identf, rstd_all], [diag])
            P.op("pe", lambda e: e.matmul(out=pbc[:, 0:128], lhsT=ones[:, :], rhs=diag[:, :], start=True, stop=True),
                 [ones, diag], [pbc])
            P.op("act", lambda e, t=t: e.copy(out=rstd_bc[:, t * 128:(t + 1) * 128], in_=pbc[:, 0:128]),
                 [pbc], [rstd_bc])

        rr = {"pacc": 0, "ost": 0, "vst": 0}

        def fm_mm(slab, j, tok0, n):
            pb = pacc[rr["pacc"] % 3]
            rr["pacc"] += 1
            for k in range(16):
                P.op("pe", lambda e, k=k, pb=pb: e.matmul(out=pb[:, 0:n], lhsT=slab[:, k, j * 128:(j + 1) * 128],
                                                           rhs=xnT[:, k, tok0:tok0 + n], start=(k == 0),
                                                           stop=(k == 15)), [slab, xnT], [pb])
            return pb

        OWN = [(128 + 512 * g, 512, 512 * g) for g in range(4)]
        ALLT = [(0, 128, -128)] + OWN

        def store_rows(ost, dram, row0):
            P.dma("sp", lambda e: e.dma_start(out=dram.t[row0:row0 + 128, :], in_=ost[:, :]), [ost], [dram], ost)

        def gate_group(gi, dram, row_base):
            slab = need_slab(gi)
            for j in range(4):
                ost = ostage[rr["ost"] % 2]
                rr["ost"] += 1
                for (x0, n, o0) in OWN:
                    pb = fm_mm(slab, j, x0, n)
                    P.op("dve", lambda e, pb=pb, x0=x0: e.tensor_tensor(out=tmp[0][:, :], in0=pb[:, :],
                                                                         in1=rstd_bc[:, x0:x0 + 512], op=ALU.mult),
                         [pb, rstd_bc], [tmp[0]])
                    P.op("act", lambda e, ost=ost, o0=o0: e.activation(out=ost[:, o0:o0 + 512], in_=tmp[0][:, :],
                                                                       func=AF.Silu), [tmp[0]], [ost])
                store_rows(ost, dram, row_base + j * 128)

        gate_group(0, gzs_o, 0)
        gate_group(1, gzs_o, 512)

        for g3 in range(3):
            slab = need_slab(2 + g3)
            for j in range(4):
                c = g3 * 4 + j
                xr = junk
                acc = accb
                for (x0, n, o0) in ALLT:
                    pb = fm_mm(slab, j, x0, n)
                    P.op("dve", lambda e, pb=pb, x0=x0, n=n: e.tensor_tensor(out=xr[:, x0:x0 + n], in0=pb[:, 0:n],
                                                                              in1=rstd_bc[:, x0:x0 + n], op=ALU.mult),
                         [pb, rstd_bc], [xr])
                P.op("dve", lambda e, c=c: e.tensor_scalar(out=acc[:, 0:TOK], in0=xr[:, 128:TA],
                                                           scalar1=scw[:, c, 3:4], scalar2=scb[:, c:c + 1],
                                                           op0=ALU.mult, op1=ALU.add), [xr, scw, scb], [acc])
                for s in (1, 2, 3):
                    P.op("dve", lambda e, c=c, s=s: e.scalar_tensor_tensor(
                        out=acc[:, 0:TOK], in0=xr[:, 128 - s:TA - s], scalar=scw[:, c, 3 - s:4 - s],
                        in1=acc[:, 0:TOK], op0=ALU.mult, op1=ALU.add), [xr, scw, acc], [acc])
                ost = ostage[rr["ost"] % 2]
                rr["ost"] += 1
                P.op("act", lambda e, ost=ost: e.activation(out=ost[:, :], in_=acc[:, 0:TOK], func=AF.Silu),
                     [acc], [ost])
                store_rows(ost, xbc_o, c * 128)

        for t in range(1, NT_A):
            pb = paux[t % 2]
            for k in range(16):
                P.op("pe", lambda e, k=k, pb=pb, t=t: e.matmul(out=pb[:, 0:16], lhsT=xnT[:, k, t * 128:(t + 1) * 128],
                                                               rhs=wdt[:, k, :], start=(k == 0), stop=(k == 15)),
                     [xnT, wdt], [pb])
            P.op("dve", lambda e, pb=pb, t=t: e.scalar_tensor_tensor(
                out=dtbuf[:, t - 1, :], in0=pb[:, 0:16], scalar=rstd_all[:, t:t + 1], in1=dtb[:, :],
                op0=ALU.mult, op1=ALU.add), [pb, rstd_all, dtb], [dtbuf])
        P.op("act", lambda e: e.activation(out=dtbuf[:, :, :], in_=dtbuf[:, :, :], func=AF.Exp), [dtbuf], [dtbuf])
        P.op("act", lambda e: e.activation(out=dtbuf[:, :, :], in_=dtbuf[:, :, :], func=AF.Ln, bias=1.0, scale=1.0),
             [dtbuf], [dtbuf])
        P.dma("sp", lambda e: e.dma_start(out=dt_o.t.rearrange("(t p) h -> p t h", p=128), in_=dtbuf[:, :, :]),
              [dtbuf], [dt_o], dtbuf)

        slab_a = need_slab(5)
        slab_g = wslab[6 % 2]
        alias(cc, [junk, accb, big[0], big[1]])
        for i in range(4):
            for (x0, n, o0) in ALLT:
                pa = fm_mm(slab_a, i, x0, n)
                P.op("dve", lambda e, pa=pa, x0=x0, n=n: e.tensor_tensor(out=tmp[0][:, 0:n], in0=pa[:, 0:n],
                                                                          in1=rstd_bc[:, x0:x0 + n], op=ALU.mult),
                     [pa, rstd_bc], [tmp[0]])
                pg = fm_mm(slab_g, i, x0, n)
                P.op("dve", lambda e, pg=pg, x0=x0, n=n: e.tensor_tensor(out=tmp[1][:, 0:n], in0=pg[:, 0:n],
                                                                          in1=rstd_bc[:, x0:x0 + n], op=ALU.mult),
                     [pg, rstd_bc], [tmp[1]])
                P.op("act", lambda e, n=n: e.activation(out=tmp[1][:, 0:n], in_=tmp[1][:, 0:n], func=AF.Sigmoid),
                     [tmp[1]], [tmp[1]])
                P.op("dve", lambda e, x0=x0, n=n: e.tensor_tensor(out=ubf[:, x0:x0 + n], in0=tmp[0][:, 0:n],
                                                                   in1=tmp[1][:, 0:n], op=ALU.mult),
                     [tmp[0], tmp[1]], [ubf])
            for k in range(31):
                P.op("dve", lambda e, i=i, k=k: e.tensor_scalar(out=dg[:, k, :], in0=identf[:, :],
                                                                scalar1=cfw[:, i, k:k + 1], scalar2=None,
                                                                op0=ALU.mult), [identf, cfw], [dg])
            for (x0, n, o0) in OWN:
                pb = pacc[rr["pacc"] % 3]
                rr["pacc"] += 1
                for k in range(31):
                    P.op("pe", lambda e, pb=pb, k=k, x0=x0: e.matmul(
                        out=pb[:, :], lhsT=dg[:, k, :], rhs=ubf[:, x0 - 30 + k:x0 - 30 + k + 512],
                        start=(k == 0), stop=(k == 30)), [dg, ubf], [pb])
                P.op("dve", lambda e, pb=pb, i=i, o0=o0: e.tensor_scalar(out=cc[:, i, o0:o0 + 512], in0=pb[:, :],
                                                                          scalar1=cfb[:, i:i + 1], scalar2=None,
                                                                          op0=ALU.add), [pb, cfb], [cc])
        for (x0, n, o0) in OWN:
            for i in range(4):
                P.op("act", lambda e, i=i, o0=o0: e.activation(out=tmp[2][:, :], in_=cc[:, i, o0:o0 + 512],
                                                               func=AF.Square), [cc], [tmp[2]])
                P.op("pe", lambda e, i=i, o0=o0: e.matmul(out=paux[0][:, :], lhsT=ones[:, :], rhs=cc[:, i, o0:o0 + 512],
                                                          start=(i == 0), stop=(i == 3)), [ones, cc], [paux[0]])
                P.op("pe", lambda e, i=i: e.matmul(out=paux[1][:, :], lhsT=ones[:, :], rhs=tmp[2][:, :],
                                                   start=(i == 0), stop=(i == 3)), [ones, tmp[2]], [paux[1]])
            P.op("dve", lambda e: e.tensor_scalar(out=mean_t[:, :], in0=paux[0][:, :], scalar1=1.0 / 512,
                                                  scalar2=None, op0=ALU.mult), [paux[0]], [mean_t])
            P.op("dve", lambda e: e.tensor_tensor(out=tmp[0][:, :], in0=mean_t[:, :], in1=mean_t[:, :], op=ALU.mult),
                 [mean_t], [tmp[0]])
            P.op("dve", lambda e: e.scalar_tensor_tensor(out=tmp[0][:, :], in0=paux[1][:, :], scalar=1.0 / 512,
                                                         in1=tmp[0][:, :], op0=ALU.mult, op1=ALU.subtract),
                 [paux[1], tmp[0]], [tmp[0]])
            P.op("act", lambda e: e.activation(out=tmp[0][:, :], in_=tmp[0][:, :], func=AF.Ln, bias=EPS, scale=1.0),
                 [tmp[0]], [tmp[0]])
            P.op("act", lambda e: e.activation(out=rln_t[:, :], in_=tmp[0][:, :], func=AF.Exp, scale=-0.5),
                 [tmp[0]], [rln_t])
            for i in range(4):
                P.op("dve", lambda e, i=i, o0=o0: e.tensor_tensor(out=cc[:, i, o0:o0 + 512], in0=cc[:, i, o0:o0 + 512],
                                                                   in1=mean_t[:, :], op=ALU.subtract),
                     [cc, mean_t], [cc])
                P.op("dve", lambda e, i=i, o0=o0: e.tensor_tensor(out=cc[:, i, o0:o0 + 512], in0=cc[:, i, o0:o0 + 512],
                                                                   in1=rln_t[:, :], op=ALU.mult),
                     [cc, rln_t], [cc])
        slab = need_slab(7)
        for i in range(4):
            ost = ostage[rr["ost"] % 2]
            rr["ost"] += 1
            for (x0, n, o0) in OWN:
                pz = fm_mm(slab, i, x0, n)
                P.op("dve", lambda e, pz=pz, x0=x0: e.tensor_tensor(out=tmp[0][:, :], in0=pz[:, :],
                                                                     in1=rstd_bc[:, x0:x0 + 512], op=ALU.mult),
                     [pz, rstd_bc], [tmp[0]])
                P.op("act", lambda e: e.activation(out=tmp[0][:, :], in_=tmp[0][:, :], func=AF.Silu),
                     [tmp[0]], [tmp[0]])
                P.op("act", lambda e, i=i, o0=o0: e.activation(out=tmp[1][:, :], in_=cc[:, i, o0:o0 + 512], func=AF.Silu,
                                                               bias=lnb[:, i:i + 1], scale=lnw[:, i:i + 1]),
                     [cc, lnb, lnw], [tmp[1]])
                P.op("dve", lambda e, ost=ost, o0=o0: e.tensor_tensor(out=ost[:, o0:o0 + 512], in0=tmp[1][:, :],
                                                                       in1=tmp[0][:, :], op=ALU.mult),
                     [tmp[0], tmp[1]], [ost])
            store_rows(ost, ycf_o, i * 128)

        alias(cosT, [cc])
        alias(sinT, [cc])
        P.dma("sp", lambda e: e.dma_start(out=cosT.ap(), in_=cos_d.ap()), [cos_d], [cosT], cosT)
        P.dma("sp", lambda e: e.dma_start(out=sinT.ap(), in_=sin_d.ap()), [sin_d], [sinT], sinT)

        for (gi, dram, wc) in ((8, q_o, 0), (9, k_o, 1)):
            slab = need_slab(gi)
            for i in range(4):
                ost = ostage[rr["ost"] % 2]
                rr["ost"] += 1
                for (x0, n, o0) in OWN:
                    pq = fm_mm(slab, i, x0, n)
                    P.op("dve", lambda e, pq=pq, x0=x0: e.tensor_tensor(out=tmp[0][:, :], in0=pq[:, :],
                                                                         in1=rstd_bc[:, x0:x0 + 512], op=ALU.mult),
                         [pq, rstd_bc], [tmp[0]])
                    P.op("act", lambda e: e.activation(out=tmp[1][:, :], in_=tmp[0][:, :], func=AF.Square),
                         [tmp[0]], [tmp[1]])
                    P.op("pe", lambda e: e.matmul(out=paux[0][:, :], lhsT=blk[:, :], rhs=tmp[1][:, :], start=True,
                                                  stop=True), [blk, tmp[1]], [paux[0]])
                    P.op("act", lambda e: e.activation(out=tmp[1][:, :], in_=paux[0][:, :], func=AF.Ln, bias=EPS,
                                                       scale=1.0 / 64), [paux[0]], [tmp[1]])
                    P.op("act", lambda e: e.activation(out=tmp[1][:, :], in_=tmp[1][:, :], func=AF.Exp, scale=-0.5),
                         [tmp[1]], [tmp[1]])
                    P.op("dve", lambda e, wc=wc: e.scalar_tensor_tensor(out=tmp[0][:, :], in0=tmp[0][:, :],
                                                                        scalar=qkw[:, wc:wc + 1], in1=tmp[1][:, :],
                                                                        op0=ALU.mult, op1=ALU.mult),
                         [tmp[0], tmp[1], qkw], [tmp[0]])
                    P.op("pe", lambda e: e.matmul(out=paux[1][:, :], lhsT=rot[:, :], rhs=tmp[0][:, :], start=True,
                                                  stop=True), [rot, tmp[0]], [paux[1]])
                    P.op("dve", lambda e, o0=o0: e.tensor_tensor(out=tmp[1][:, :], in0=tmp[0][:, :],
                                                                 in1=cosT[:, o0:o0 + 512], op=ALU.mult),
                         [tmp[0], cosT], [tmp[1]])
                    P.op("dve", lambda e, o0=o0: e.tensor_tensor(out=tmp[2][:, :], in0=paux[1][:, :],
                                                                 in1=sinT[:, o0:o0 + 512], op=ALU.mult),
                         [paux[1], sinT], [tmp[2]])
                    P.op("dve", lambda e, ost=ost, o0=o0: e.tensor_tensor(out=ost[:, o0:o0 + 512], in0=tmp[1][:, :],
                                                                           in1=tmp[2][:, :], op=ALU.add),
                         [tmp[1], tmp[2]], [ost])
                store_rows(ost, dram, i * 128)

        slab = need_slab(10)
        for t in range(1, NT_A):
            pb = pacc[rr["pacc"] % 3]
            rr["pacc"] += 1
            for k in range(16):
                P.op("pe", lambda e, k=k, pb=pb, t=t: e.matmul(out=pb[:, :], lhsT=xnT[:, k, t * 128:(t + 1) * 128],
                                                               rhs=slab[:, k, :], start=(k == 0), stop=(k == 15)),
                     [xnT, slab], [pb])
            vs = vstage[rr["vst"] % 2]
            rr["vst"] += 1
            P.op("act", lambda e, pb=pb, vs=vs, t=t: e.activation(out=vs[:, :], in_=pb[:, :], func=AF.Copy,
                                                                  scale=rstd_all[:, t:t + 1]),
                 [pb, rstd_all], [vs])
            P.dma("sp", lambda e, vs=vs, t=t: e.dma_start(out=v_o.t[(t - 1) * 128:t * 128, :], in_=vs[:, :]),
                  [vs], [v_o], vs)

        gate_group(11, gza_o, 0)

        P.finish([gzs_o, xbc_o, dt_o, ycf_o, q_o, k_o, v_o, gza_o])
        P.emit()
    return nc


_NC_CACHE = {}


def _get(name, fn):
    if name not in _NC_CACHE:
        _NC_CACHE[name] = fn()
    return _NC_CACHE[name]


def _chunk_cols(vec, nchunk):
    return np.ascontiguousarray(np.asarray(vec, np.float32).reshape(nchunk, 128).T)


def run_A(x_full, inp, l, consts, cos, sin):
    nc = _get("A", build_A)
    in_maps = []
    scw = np.ascontiguousarray(np.asarray(inp["ssd_conv_w"][l], np.float32).T.reshape(12, 128, 4).transpose(1, 0, 2))
    cfw = np.ascontiguousarray(np.asarray(inp["cfm_conv_w"][l], np.float32).T.reshape(4, 128, 31).transpose(1, 0, 2))
    qkw = np.ascontiguousarray(np.stack([np.tile(np.asarray(inp["att_q_norm_w"][l], np.float32), 2),
                                         np.tile(np.asarray(inp["att_k_norm_w"][l], np.float32), 2)], axis=1))
    shared = {
        "w_in": np.ascontiguousarray(inp["w_in"][l]),
        "normw": _chunk_cols(inp["norm_w"][l], 16),
        "scw": scw, "scb": _chunk_cols(inp["ssd_conv_b"][l], 12),
        "dtb": np.ascontiguousarray(np.tile(np.asarray(inp["ssd_dt_bias"][l], np.float32)[None, :], (128, 1))),
        "cfw": cfw, "cfb": _chunk_cols(inp["cfm_conv_b"][l], 4),
        "lnw": _chunk_cols(inp["cfm_ln_w"][l], 4), "lnb": _chunk_cols(inp["cfm_ln_b"][l], 4),
        "qkw": qkw,
        "ident_bf": consts["ident_bf"], "ident_f": consts["ident_f"], "ones_f": consts["ones_f"],
        "blk_f": consts["blk_f"], "rot_f": consts["rot_f"],
    }
    for c in range(NCORES):
        t0 = c * TOK
        xe = np.zeros((TA, DM), np.float32)
        xe[128:] = x_full[t0:t0 + TOK]
        if c > 0:
            xe[:128] = x_full[t0 - 128:t0]
        cT = np.ascontiguousarray(np.tile(cos[t0:t0 + TOK].T, (4, 1)))
        sT = np.ascontiguousarray(np.tile(sin[t0:t0 + TOK].T, (4, 1)))
        m = dict(shared)
        m.update({"xe": xe, "cosT": cT, "sinT": sT})
        in_maps.append(m)
    res = run_bass_kernel_spmd(nc, in_maps, core_ids=list(range(NCORES)))
    return res.results


NQT = SEQ // 512
NCH = SEQ // 128
SEG = 2048


def build_B():
    nc = bass.Bass("TRN2", target_bir_lowering=False)
    with ExitStack() as es:
        P = Prog(nc, es)
        D = lambda n, s, d, k="ExternalInput": P.dram(n, s, d, k)
        q_d = D("qT", [64, SEQ], BF16)
        k_d = D("kT", [64, SEQ], BF16)
        v_d = D("v", [SEQ, 128], BF16)
        xs_d = D("xsT", [64, 2, SEQ], BF16)
        b_d = D("BT", [128, SEQ], BF16)
        c_d = D("CT", [128, SEQ], BF16)
        dt_d = D("dt2", [SEQ, 2], F32)
        xtok_d = D("xs_tok", [SEQ, 128], BF16)
        btok_d = D("b_tok", [SEQ, 128], BF16)
        alog_d = D("alog", [128, 2], F32)
        dcol_d = D("dcol", [64, 2], F32)
        onesb_d = D("ones_bf", [128, 128], BF16)
        trib_d = D("tri_bf", [128, 128], BF16)
        identb_d = D("ident_bf", [128, 128], BF16)
        onesf_d = D("ones_f", [128, 128], F32)
        trif_d = D("tri_f", [128, 128], F32)
        o_o = D("oT", [128, SEQ], F32, "ExternalOutput")
        y_o = D("yT", [64, 2, SEQ], F32, "ExternalOutput")

        qs = P.sb("qs", [64, SEQ], BF16)
        ks = P.sb("ks", [64, SEQ], BF16)
        vs = P.sb("vs", [128, NCH, 128], BF16)
        onesb = P.sb("onesb", [128, 128], BF16)
        trib = P.sb("trib", [128, 128], BF16)
        identb = P.sb("identb", [128, 128], BF16)
        onesf = P.sb("onesf", [128, 128], F32)
        trif = P.sb("trif", [128, 128], F32)
        alog = P.sb("alog_s", [128, 2], F32)
        dcol = P.sb("dcol_s", [64, 2], F32)
        dts = P.sb("dts", [128, NCH, 2], F32)
        ptile = [P.sb("ptile%d" % i, [128, 512], BF16) for i in range(4)]
        rl = P.sb("rl", [128, 512], F32)
        ost = [P.sb("ost%d" % i, [128, 512], F32) for i in range(2)]
        xseg = [P.sb("xseg%d" % i, [64, 2, SEG], BF16) for i in range(2)]
        bseg = [P.sb("bseg%d" % i, [128, SEG], BF16) for i in range(2)]
        cseg = [P.sb("cseg%d" % i, [128, SEG], BF16) for i in range(2)]
        yst = [P.sb("yst%d" % i, [64, 2, SEG], F32) for i in range(2)]
        xtseg = [P.sb("xtseg%d" % i, [128, 16, 128], BF16) for i in range(2)]
        btseg = [P.sb("btseg%d" % i, [128, 16, 128], BF16) for i in range(2)]
        a_t = [P.sb("a_t%d" % i, [128, 2], F32) for i in range(2)]
        acs = [P.sb("acs%d" % i, [128, 2], F32) for i in range(2)]
        dsta = [P.sb("dsta%d" % i, [128, 2], F32) for i in range(2)]
        X = [P.sb("X%d" % i, [128, 2, 128], F32) for i in range(2)]
        D1 = [P.sb("D1%d" % i, [128, 2, 128], F32) for i in range(2)]
        ER = [P.sb("ER%d" % i, [128, 2, 128], F32) for i in range(2)]
        Gm = [P.sb("Gm%d" % i, [128, 128], F32) for i in range(2)]
        LT = [P.sb("LT%d" % i, [128, 2, 128], BF16) for i in range(2)]
        Ce = [P.sb("Ce%d" % i, [128, 2, 128], BF16) for i in range(2)]
        xdt = [P.sb("xdt%d" % i, [128, 2, 64], BF16) for i in range(2)]
        xw = [P.sb("xw%d" % i, [128, 2, 64], BF16) for i in range(2)]
        Bt = P.sb("Bt", [128, 128], BF16)
        S = P.sb("S", [128, 2, 64], F32)
        Sbf = [P.sb("Sbf%d" % i, [128, 2, 64], BF16) for i in range(4)]

        ps_s = [P.ps("ps_s%d" % i, [128, 512], F32) for i in range(3)]
        po = P.ps("po", [128, 512], F32)
        pl = P.ps("pl", [128, 512], F32)
        bankA = [P.ps("bankA%d" % i, [128, 512], F32) for i in range(2)]
        _bB = P.ps("bankB", [128, 512], F32)
        bankB = [_bB, _bB]

        def load(dst, src, q="sp"):
            P.dma(q, lambda e: e.dma_start(out=dst.ap(), in_=src.ap()), [src], [dst], dst)

        for dst, src in [(onesb, onesb_d), (trib, trib_d), (identb, identb_d), (onesf, onesf_d), (trif, trif_d),
                         (alog, alog_d), (dcol, dcol_d), (qs, q_d), (ks, k_d)]:
            load(dst, src)
        P.dma("sp", lambda e: e.dma_start(out=dts[:, :, :], in_=dt_d.t.rearrange("(j t) h -> t j h", t=128)),
              [dt_d], [dts], dts)
        for part in range(8):
            P.dma("pool", lambda e, part=part: e.dma_start(
                out=vs[:, part * 16:(part + 1) * 16, :],
                in_=v_d.t[part * 2048:(part + 1) * 2048, :].rearrange("(b p) e -> p b e", p=128)),
                [v_d], [vs], vs)
        P.op("act", lambda e: e.activation(out=alog[:, :], in_=alog[:, :], func=AF.Exp), [alog], [alog])
        P.op("dve", lambda e: e.tensor_scalar(out=alog[:, :], in0=alog[:, :], scalar1=-1.0, scalar2=None, op0=ALU.mult),
             [alog], [alog])
        P.op("dve", lambda e: e.memset(S[:, :, :], 0.0), [], [S])
        P.op("dve", lambda e: e.memset(Sbf[0][:, :, :], 0.0), [], [Sbf[0]])

        st = {"pt": 0, "ost": 0}
        _DM = "block"
        _DN = 4
        pend = []

        def Q(eng, fn, reads=(), writes=()):
            pend.append(("op", eng, fn, reads, writes))

        def QD(queue, fn, reads, writes, sembuf):
            pend.append(("dma", queue, fn, reads, writes, sembuf))

        def drain(n):
            while n > 0 and pend:
                it = pend.pop(0)
                if it[0] == "op":
                    P.op(it[1], it[2], it[3], it[4])
                else:
                    P.dma(it[1], it[2], it[3], it[4], it[5])
                n -= 1

        def attn_qtile(qi):
            nkv = 4 * (qi + 1)
            q0 = qi * 512

            def col0(j):
                r = j - 4 * qi
                return 128 * r if r > 0 else 0

            def s_mm(j):
                c0 = col0(j)
                pss = ps_s[j % 3]
                P.op("pe", lambda e: e.matmul(out=pss[:, c0:512], lhsT=ks[:, j * 128:(j + 1) * 128],
                                              rhs=qs[:, q0 + c0:q0 + 512], start=True, stop=True), [ks, qs], [pss])

            s_mm(0)
            s_mm(1)
            for j in range(nkv):
                if j + 2 < nkv:
                    s_mm(j + 2)
                r = j - 4 * qi
                c0 = col0(j)
                pss = ps_s[j % 3]
                pt = ptile[st["pt"] % 4]
                st["pt"] += 1
                P.op("act", lambda e, pss=pss, pt=pt, c0=c0: e.activation(out=pt[:, c0:512], in_=pss[:, c0:512],
                                                                          func=AF.Exp), [pss], [pt])
                if r >= 0:
                    P.op("pool", lambda e, pt=pt, c0=c0: e.tensor_tensor(out=pt[:, c0:c0 + 128], in0=pt[:, c0:c0 + 128],
                                                                         in1=trib[:, :], op=ALU.mult), [pt, trib], [pt])
                P.op("pe", lambda e, pt=pt, j=j, c0=c0: e.matmul(out=po[:, c0:512], lhsT=vs[:, j, :], rhs=pt[:, c0:512],
                                                                 start=(j == 0), stop=(j == nkv - 1)), [vs, pt], [po])
                P.op("pe", lambda e, pt=pt, j=j, c0=c0: e.matmul(out=pl[:, c0:512], lhsT=onesb[:, :], rhs=pt[:, c0:512],
                                                                 start=(j == 0), stop=(j == nkv - 1)), [onesb, pt], [pl])
                drain(_DN)
            o = ost[st["ost"] % 2]
            st["ost"] += 1
            P.op("dve", lambda e: e.reciprocal(out=rl[:, :], in_=pl[:, :]), [pl], [rl])
            P.op("dve", lambda e, o=o: e.tensor_tensor(out=o[:, :], in0=po[:, :], in1=rl[:, :], op=ALU.mult),
                 [po, rl], [o])
            P.dma("sp", lambda e, o=o: e.dma_start(out=o_o.t[:, q0:q0 + 512], in_=o[:, :]), [o], [o_o], o)

        def load_seg(sg, dma=None):
            dma = dma or P.dma
            s0 = sg * SEG
            dma("sp", lambda e: e.dma_start(out=xseg[sg % 2][:, :, :], in_=xs_d.t[:, :, s0:s0 + SEG]),
                [xs_d], [xseg[sg % 2]], xseg[sg % 2])
            dma("sp", lambda e: e.dma_start(out=bseg[sg % 2][:, :], in_=b_d.t[:, s0:s0 + SEG]),
                [b_d], [bseg[sg % 2]], bseg[sg % 2])
            dma("sp", lambda e: e.dma_start(out=cseg[sg % 2][:, :], in_=c_d.t[:, s0:s0 + SEG]),
                [c_d], [cseg[sg % 2]], cseg[sg % 2])
            dma("sp", lambda e: e.dma_start(out=xtseg[sg % 2][:, :, :],
                                            in_=xtok_d.t[s0:s0 + SEG, :].rearrange("(j t) c -> t j c", t=128)),
                [xtok_d], [xtseg[sg % 2]], xtseg[sg % 2])
            dma("sp", lambda e: e.dma_start(out=btseg[sg % 2][:, :, :],
                                            in_=btok_d.t[s0:s0 + SEG, :].rearrange("(j t) c -> t j c", t=128)),
                [btok_d], [btseg[sg % 2]], btseg[sg % 2])

        def ssd_chunk(j):
            H1, H2 = [], []
            cur = [H1]

            def Q(eng, fn, reads=(), writes=()):
                cur[0].append(("op", eng, fn, reads, writes))

            def QD(queue, fn, reads, writes, sembuf):
                cur[0].append(("dma", queue, fn, reads, writes, sembuf))

            p = j % 2
            sg = j // 16
            c = (j % 16) * 128
            jj = j % 16
            xg, bg, cg, yg = xseg[sg % 2], bseg[sg % 2], cseg[sg % 2], yst[sg % 2]
            xtg, btg = xtseg[sg % 2], btseg[sg % 2]
            bA, bB = bankA[p], bankB[p]
            g0 = 130 * p
            a_, acs_, dsta_, X_, D1_, ER_, Gm_, LT_, Ce_, xdt_, xw_ = (a_t[p], acs[p], dsta[p], X[p], D1[p], ER[p],
                                                                      Gm[p], LT[p], Ce[p], xdt[p], xw[p])
            sb_in, sb_out = Sbf[j % 4], Sbf[(j + 1) % 4]
            R3 = lambda: bA[:, 0:256].rearrange("p (h l) -> p h l", h=2)
            Q("dve", lambda e: e.tensor_tensor(out=a_[:, :], in0=dts[:, j, :], in1=alog[:, :], op=ALU.mult),
              [dts, alog], [a_])
            Q("pe", lambda e: e.matmul(out=bB[:, g0 + 128:g0 + 130], lhsT=trif[:, :], rhs=a_[:, :], start=True, stop=True),
              [trif, a_], [bB])
            Q("dve", lambda e: e.tensor_copy(out=acs_[:, :], in_=bB[:, g0 + 128:g0 + 130]), [bB], [acs_])
            for h in range(2):
                Q("dve", lambda e, h=h: e.tensor_scalar(out=X_[:, h, :], in0=trif[:, :], scalar1=a_[:, h:h + 1],
                                                        scalar2=None, op0=ALU.mult), [trif, a_], [X_])
            Q("pe", lambda e: e.matmul(out=bA[:, 0:256], lhsT=onesf[:, :],
                                       rhs=X_[:, :, :].rearrange("p h l -> p (h l)"), start=True, stop=True),
              [onesf, X_], [bA])
            Q("pe", lambda e: e.matmul(out=bB[:, g0:g0 + 128], lhsT=bg[:, c:c + 128], rhs=cg[:, c:c + 128], start=True,
                                       stop=True), [bg, cg], [bB])
            Q("dve", lambda e: e.tensor_tensor(out=Gm_[:, :], in0=bB[:, g0:g0 + 128], in1=trif[:, :], op=ALU.mult),
              [bB, trif], [Gm_])
            Q("dve", lambda e: e.tensor_tensor(out=xdt_[:, :, :],
                                               in0=xtg[:, jj, :].rearrange("p (h d) -> p h d", h=2),
                                               in1=bc3(dts[:, j, :], 64), op=ALU.mult), [xtg, dts], [xdt_])
            for h in range(2):
                Q("dve", lambda e, h=h: e.tensor_scalar(out=D1_[:, h, :], in0=bA[:, h * 128:(h + 1) * 128],
                                                        scalar1=acs_[:, h:h + 1], scalar2=0.0, op0=ALU.subtract,
                                                        op1=ALU.min), [bA, acs_], [D1_])
            Q("act", lambda e: e.activation(out=D1_[:, :, :], in_=D1_[:, :, :], func=AF.Exp), [D1_], [D1_])
            Q("act", lambda e: e.activation(out=ER_[:, :, :], in_=R3(), func=AF.Exp), [bA], [ER_])
            Q("dve", lambda e: e.tensor_tensor(out=dsta_[:, :], in0=R3()[:, :, 127], in1=acs_[:, :], op=ALU.subtract),
              [bA, acs_], [dsta_])
            Q("act", lambda e: e.activation(out=dsta_[:, :], in_=dsta_[:, :], func=AF.Exp), [dsta_], [dsta_])
            Q("dve", lambda e: e.tensor_tensor(out=LT_[:, :, :], in0=D1_[:, :, :],
                                               in1=Gm_[:, :].unsqueeze(1).to_broadcast([128, 2, 128]), op=ALU.mult),
              [D1_, Gm_], [LT_])
            Q("dve", lambda e: e.tensor_tensor(out=Ce_[:, :, :], in0=ER_[:, :, :],
                                               in1=cg[:, c:c + 128].unsqueeze(1).to_broadcast([128, 2, 128]),
                                               op=ALU.mult), [ER_, cg], [Ce_])
            Q("dve", lambda e: e.tensor_tensor(out=xw_[:, :, :], in0=xdt_[:, :, :], in1=bc3(dsta_[:, :], 64),
                                               op=ALU.mult), [xdt_, dsta_], [xw_])
            Q("pe", lambda e: e.matmul(out=bB[:, g0:g0 + 128], lhsT=btg[:, jj, :],
                                       rhs=xw_[:, :, :].rearrange("p h d -> p (h d)"), start=True, stop=True),
              [btg, xw_], [bB])
            cur[0] = H2
            for h in range(2):
                Q("pe", lambda e, h=h: e.matmul(out=bA[0:64, 256 + h * 128:256 + (h + 1) * 128], lhsT=xdt_[:, h, :],
                                                rhs=LT_[:, h, :], start=True, stop=False), [xdt_, LT_], [bA])
                Q("pe", lambda e, h=h: e.matmul(out=bA[0:64, 256 + h * 128:256 + (h + 1) * 128], lhsT=sb_in[:, h, :],
                                                rhs=Ce_[:, h, :], start=False, stop=True), [sb_in, Ce_], [bA])
            for h in range(2):
                Q("dve", lambda e, h=h: e.scalar_tensor_tensor(out=S[:, h, :], in0=S[:, h, :],
                                                               scalar=ER_[:, h, 127:128],
                                                               in1=bB[:, g0 + h * 64:g0 + (h + 1) * 64],
                                                               op0=ALU.mult, op1=ALU.add), [S, ER_, bB], [S])
            Q("act", lambda e: e.copy(out=sb_out[:, :, :], in_=S[:, :, :]), [S], [sb_out])
            for h in range(2):
                Q("dve", lambda e, h=h: e.scalar_tensor_tensor(out=yg[:, h, c:c + 128], in0=xg[:, h, c:c + 128],
                                                               scalar=dcol[:, h:h + 1],
                                                               in1=bA[0:64, 256 + h * 128:256 + (h + 1) * 128],
                                                               op0=ALU.mult, op1=ALU.add), [xg, dcol, bA], [yg])
            if j % 16 == 15:
                s0 = sg * SEG
                QD("sp", lambda e: e.dma_start(out=y_o.t[:, :, s0:s0 + SEG], in_=yg[:, :, :]), [yg], [y_o], yg)
            if j % 16 == 0 and sg + 1 < SEQ // SEG:
                load_seg(sg + 1, QD)
            return H1, H2

        def merge(a, b):
            out, ia, ib = [], 0, 0
            na, nb = len(a), len(b)
            while ia < na or ib < nb:
                if ib >= nb or (ia < na and ia * nb <= ib * na):
                    out.append(a[ia]); ia += 1
                else:
                    out.append(b[ib]); ib += 1
            return out

        load_seg(0)
        prev_tail = []
        for j in range(NCH):
            h1, h2 = ssd_chunk(j)
            pend.extend(merge(prev_tail, h1))
            prev_tail = h2
        pend.extend(prev_tail)
        for qi in range(NQT):
            attn_qtile(qi)
        drain(1 << 30)
        P.finish([o_o, y_o])
        P.emit()
    return nc


def build_C():
    nc = bass.Bass("TRN2", target_bir_lowering=False)
    with ExitStack() as es:
        P = Prog(nc, es)
        D = lambda n, s, d, k="ExternalInput": P.dram(n, s, d, k)
        y_d = D("yssdT", [1024, TOK], F32)
        gzs_d = D("gz_ssdT", [1024, TOK], BF16)
        ycf_d = D("ycfmT", [512, TOK], BF16)
        o_d = D("oT", [8, 128, TOK], F32)
        gza_d = D("gz_attT", [512, TOK], BF16)
        x_d = D("x", [TOK, DM], F32)
        w_d = D("w_out", [DM, DM], F32)
        snw_d = D("snw", [128, 8], F32)
        subw_d = D("subw", [128, 1], F32)
        lam_d = D("lamv", [128, 4, 64], F32)
        linit_d = D("linit", [128, 2], F32)
        onesf_d = D("ones_f", [128, 128], F32)
        x_o = D("x_out", [TOK, DM], F32, "ExternalOutput")

        Wsb = P.sb("Wsb", [128, 16, DM], BF16)
        ybuf = [P.sb("ybuf%d" % i, [128, 16, 512], BF16) for i in range(2)]
        gbuf = P.sb("gbuf", [128, 8, 512], F32)
        yin = [P.sb("yin%d" % i, [128, 512], F32) for i in range(2)]
        gzin = [P.sb("gzin%d" % i, [128, 512], BF16) for i in range(2)]
        oin = [P.sb("oin%d" % i, [128, 512], F32) for i in range(4)]
        tmp = [P.sb("tmpc%d" % i, [128, 512], F32) for i in range(3)]
        xt = [P.sb("xt%d" % i, [128, DM], F32) for i in range(2)]
        xo = [P.sb("xo%d" % i, [128, DM], F32) for i in range(2)]
        snw = P.sb("snw_s", [128, 8], F32)
        subw = P.sb("subw_s", [128, 1], F32)
        lam = P.sb("lam_s", [128, 4, 64], F32)
        linit = P.sb("linit_s", [128, 2], F32)
        onesf = P.sb("onesf_s", [128, 128], F32)
        lsc = P.sb("lsc", [128, 4], F32)
        pacc = [P.ps("pacc%d" % i, [128, 512], F32) for i in range(4)]
        paux = [P.ps("paux%d" % i, [128, 512], F32) for i in range(2)]

        def load(dst, src, q="sp"):
            P.dma(q, lambda e: e.dma_start(out=dst.ap(), in_=src.ap()), [src], [dst], dst)

        for dst, src in [(snw, snw_d), (subw, subw_d), (lam, lam_d), (linit, linit_d), (onesf, onesf_d)]:
            load(dst, src)
        for part in range(4):
            P.dma("pool", lambda e, part=part: e.dma_start(
                out=Wsb[:, part * 4:(part + 1) * 4, :],
                in_=w_d.t[part * 512:(part + 1) * 512, :].rearrange("(k p) c -> p k c", p=128)), [w_d], [Wsb], Wsb)
        P.op("dve", lambda e: e.tensor_tensor(out=lam[:, 0, :], in0=lam[:, 0, :], in1=lam[:, 1, :], op=ALU.mult),
             [lam], [lam])
        P.op("dve", lambda e: e.tensor_tensor(out=lam[:, 2, :], in0=lam[:, 2, :], in1=lam[:, 3, :], op=ALU.mult),
             [lam], [lam])
        P.op("dve", lambda e: e.tensor_reduce(out=lsc[:, 0:1], in_=lam[:, 0, :], axis=AX.X, op=ALU.add), [lam], [lsc])
        P.op("dve", lambda e: e.tensor_reduce(out=lsc[:, 1:2], in_=lam[:, 2, :], axis=AX.X, op=ALU.add), [lam], [lsc])
        P.op("act", lambda e: e.activation(out=lsc[:, 0:2], in_=lsc[:, 0:2], func=AF.Exp), [lsc], [lsc])
        P.op("dve", lambda e: e.tensor_tensor(out=lsc[:, 2:3], in0=lsc[:, 1:2], in1=lsc[:, 0:1], op=ALU.subtract),
             [lsc], [lsc])
        P.op("dve", lambda e: e.tensor_tensor(out=lsc[:, 3:4], in0=lsc[:, 2:3], in1=linit[:, 0:1], op=ALU.subtract),
             [lsc, linit], [lsc])

        rr = {"ld": 0, "o": 0, "pacc": 0}
        for tg in range(4):
            t0 = tg * 512
            yb = ybuf[tg % 2]
            for i in range(8):
                yi = yin[rr["ld"] % 2]
                gi = gzin[rr["ld"] % 2]
                rr["ld"] += 1
                P.dma("sp", lambda e, yi=yi, i=i, t0=t0: e.dma_start(out=yi[:, :], in_=y_d.t[i * 128:(i + 1) * 128, t0:t0 + 512]),
                      [y_d], [yi], yi)
                P.dma("sp", lambda e, gi=gi, i=i, t0=t0: e.dma_start(out=gi[:, :], in_=gzs_d.t[i * 128:(i + 1) * 128, t0:t0 + 512]),
                      [gzs_d], [gi], gi)
                P.op("dve", lambda e, yi=yi, gi=gi, i=i: e.tensor_tensor(out=gbuf[:, i, :], in0=yi[:, :], in1=gi[:, :],
                                                                          op=ALU.mult), [yi, gi], [gbuf])
                P.op("act", lambda e, i=i: e.activation(out=tmp[0][:, :], in_=gbuf[:, i, :], func=AF.Square),
                     [gbuf], [tmp[0]])
                P.op("pe", lambda e, i=i: e.matmul(out=paux[0][:, :], lhsT=onesf[:, :], rhs=tmp[0][:, :], start=(i == 0),
                                                   stop=(i == 7)), [onesf, tmp[0]], [paux[0]])
            P.op("act", lambda e: e.activation(out=tmp[1][:, :], in_=paux[0][:, :], func=AF.Ln, bias=EPS,
                                               scale=1.0 / 1024), [paux[0]], [tmp[1]])
            P.op("act", lambda e: e.activation(out=tmp[1][:, :], in_=tmp[1][:, :], func=AF.Exp, scale=-0.5),
                 [tmp[1]], [tmp[1]])
            for i in range(8):
                P.op("dve", lambda e, i=i, yb=yb: e.scalar_tensor_tensor(out=yb[:, i, :], in0=gbuf[:, i, :],
                                                                         scalar=snw[:, i:i + 1], in1=tmp[1][:, :],
                                                                         op0=ALU.mult, op1=ALU.mult),
                     [gbuf, snw, tmp[1]], [yb])
            for i in range(4):
                P.dma("sp", lambda e, i=i, yb=yb, t0=t0: e.dma_start(out=yb[:, 8 + i, :],
                                                              in_=ycf_d.t[i * 128:(i + 1) * 128, t0:t0 + 512]),
                      [ycf_d], [yb], yb)
            for h in range(4):
                o0 = oin[rr["o"] % 4]
                o1 = oin[(rr["o"] + 1) % 4]
                rr["o"] += 2
                gi = gzin[rr["ld"] % 2]
                rr["ld"] += 1
                P.dma("sp", lambda e, o0=o0, h=h, t0=t0: e.dma_start(out=o0[:, :], in_=o_d.t[2 * h, :, t0:t0 + 512]),
                      [o_d], [o0], o0)
                P.dma("sp", lambda e, o1=o1, h=h, t0=t0: e.dma_start(out=o1[:, :], in_=o_d.t[2 * h + 1, :, t0:t0 + 512]),
                      [o_d], [o1], o1)
                P.dma("sp", lambda e, gi=gi, h=h, t0=t0: e.dma_start(out=gi[:, :], in_=gza_d.t[h * 128:(h + 1) * 128, t0:t0 + 512]),
                      [gza_d], [gi], gi)
                P.op("dve", lambda e, o0=o0, o1=o1: e.scalar_tensor_tensor(out=tmp[0][:, :], in0=o1[:, :],
                                                                           scalar=lsc[:, 3:4], in1=o0[:, :],
                                                                           op0=ALU.mult, op1=ALU.add),
                     [o0, o1, lsc], [tmp[0]])
                P.op("act", lambda e: e.activation(out=tmp[2][:, :], in_=tmp[0][:, :], func=AF.Square),
                     [tmp[0]], [tmp[2]])
                P.op("pe", lambda e: e.matmul(out=paux[1][:, :], lhsT=onesf[:, :], rhs=tmp[2][:, :], start=True,
                                              stop=True), [onesf, tmp[2]], [paux[1]])
                P.op("act", lambda e: e.activation(out=tmp[2][:, :], in_=paux[1][:, :], func=AF.Ln, bias=EPS,
                                                   scale=1.0 / 128), [paux[1]], [tmp[2]])
                P.op("act", lambda e: e.activation(out=tmp[2][:, :], in_=tmp[2][:, :], func=AF.Exp, scale=-0.5),
                     [tmp[2]], [tmp[2]])
                P.op("dve", lambda e: e.scalar_tensor_tensor(out=tmp[0][:, :], in0=tmp[0][:, :], scalar=subw[:, 0:1],
                                                             in1=tmp[2][:, :], op0=ALU.mult, op1=ALU.mult),
                     [tmp[0], subw, tmp[2]], [tmp[0]])
                P.op("dve", lambda e, gi=gi, h=h, yb=yb: e.scalar_tensor_tensor(out=yb[:, 12 + h, :], in0=tmp[0][:, :],
                                                                                scalar=linit[:, 1:2], in1=gi[:, :],
                                                                                op0=ALU.mult, op1=ALU.mult),
                     [tmp[0], linit, gi], [yb])
            for tt in range(4):
                r0 = t0 + tt * 128
                xti = xt[tt % 2]
                xoi = xo[tt % 2]
                P.dma("sp", lambda e, xti=xti, r0=r0: e.dma_start(out=xti[:, :], in_=x_d.t[r0:r0 + 128, :]),
                      [x_d], [xti], xti)
                for cg in range(4):
                    pb = pacc[rr["pacc"] % 4]
                    rr["pacc"] += 1
                    for kk in range(16):
                        P.op("pe", lambda e, pb=pb, kk=kk, tt=tt, cg=cg, yb=yb: e.matmul(
                            out=pb[:, :], lhsT=yb[:, kk, tt * 128:(tt + 1) * 128],
                            rhs=Wsb[:, kk, cg * 512:(cg + 1) * 512], start=(kk == 0), stop=(kk == 15)),
                            [yb, Wsb], [pb])
                    P.op("dve", lambda e, pb=pb, cg=cg, xti=xti, xoi=xoi: e.tensor_tensor(
                        out=xoi[:, cg * 512:(cg + 1) * 512], in0=pb[:, :], in1=xti[:, cg * 512:(cg + 1) * 512],
                        op=ALU.add), [pb, xti], [xoi])
                P.dma("sp", lambda e, xoi=xoi, r0=r0: e.dma_start(out=x_o.t[r0:r0 + 128, :], in_=xoi[:, :]),
                      [xoi], [x_o], xoi)
        P.finish([x_o])
        P.emit()
    return nc


def run_B(resA, inp, l, consts):
    nc = _get("B", build_B)
    qT = np.concatenate([np.asarray(r["qT"]) for r in resA], axis=1)
    kT = np.concatenate([np.asarray(r["kT"]) for r in resA], axis=1)
    v = np.concatenate([np.asarray(r["v"]) for r in resA], axis=0)
    xbcT = np.concatenate([np.asarray(r["xbcT"]) for r in resA], axis=1)
    dt = np.concatenate([np.asarray(r["dt"]) for r in resA], axis=0)
    alog = np.asarray(inp["ssd_a_log"][l], np.float32)
    dsk = np.asarray(inp["ssd_d"][l], np.float32)
    in_maps = []
    for u in range(NCORES):
        h, g = u // 2, u // 4
        xs = xbcT[128 * u:128 * (u + 1)].reshape(2, 64, SEQ).transpose(1, 0, 2)
        in_maps.append({
            "qT": np.ascontiguousarray(qT[64 * u:64 * (u + 1)]),
            "kT": np.ascontiguousarray(kT[64 * u:64 * (u + 1)]),
            "v": np.ascontiguousarray(v[:, 128 * h:128 * (h + 1)]),
            "xsT": np.ascontiguousarray(xs),
            "BT": np.ascontiguousarray(xbcT[1024 + 128 * g:1024 + 128 * (g + 1)]),
            "CT": np.ascontiguousarray(xbcT[1280 + 128 * g:1280 + 128 * (g + 1)]),
            "dt2": np.ascontiguousarray(dt[:, 2 * u:2 * u + 2]),
            "xs_tok": np.ascontiguousarray(xbcT[128 * u:128 * (u + 1)].T),
            "b_tok": np.ascontiguousarray(xbcT[1024 + 128 * g:1024 + 128 * (g + 1)].T),
            "alog": np.ascontiguousarray(np.tile(alog[None, 2 * u:2 * u + 2], (128, 1))),
            "dcol": np.ascontiguousarray(np.tile(dsk[None, 2 * u:2 * u + 2], (64, 1))),
            "ones_bf": consts["ones_f"].astype(NPBF), "tri_bf": consts["tri_bf"], "ident_bf": consts["ident_bf"],
            "ones_f": consts["ones_f"], "tri_f": consts["tri_f"],
        })
    res = run_bass_kernel_spmd(nc, in_maps, core_ids=list(range(NCORES)))
    return res.results


def run_C(x_full, resA, resB, inp, l, consts):
    nc = _get("C", build_C)
    linit_v = 0.8 - 0.6 * math.exp(-0.3 * l)
    yT = np.concatenate([np.asarray(r["yT"]).transpose(1, 0, 2).reshape(128, SEQ) for r in resB], axis=0)
    oT = np.stack([np.asarray(r["oT"]) for r in resB], axis=0)
    lamv = np.stack([np.asarray(inp[k][l], np.float32) for k in
                     ("att_lambda_q1", "att_lambda_k1", "att_lambda_q2", "att_lambda_k2")], axis=0)
    shared = {
        "w_out": np.ascontiguousarray(inp["w_out"][l]),
        "snw": _chunk_cols(inp["ssd_norm_w"][l], 8),
        "subw": np.ascontiguousarray(np.asarray(inp["att_subln_w"][l], np.float32).reshape(128, 1)),
        "lamv": np.ascontiguousarray(np.tile(lamv[None], (128, 1, 1))),
        "linit": np.ascontiguousarray(np.tile(np.array([[linit_v, 1.0 - linit_v]], np.float32), (128, 1))),
        "ones_f": consts["ones_f"],
    }
    in_maps = []
    for c in range(NCORES):
        sl = slice(c * TOK, (c + 1) * TOK)
        m = dict(shared)
        m.update({
            "yssdT": np.ascontiguousarray(yT[:, sl]),
            "gz_ssdT": np.asarray(resA[c]["gz_ssdT"]),
            "ycfmT": np.asarray(resA[c]["ycfmT"]),
            "oT": np.ascontiguousarray(oT[:, :, sl]),
            "gz_attT": np.asarray(resA[c]["gz_attT"]),
            "x": np.ascontiguousarray(x_full[sl]),
        })
        in_maps.append(m)
    res = run_bass_kernel_spmd(nc, in_maps, core_ids=list(range(NCORES)))
    return np.concatenate([np.asarray(r["x_out"]) for r in res.results], axis=0)


def kernel(**inputs):
    inp = {k: np.asarray(v) for k, v in inputs.items()}
    consts = _consts()
    cos, sin = _rope_tables()
    x = np.ascontiguousarray(inp["x"][0], dtype=np.float32)
    for l in range(2):
        resA = run_A(x, inp, l, consts, cos, sin)
        resB = run_B(resA, inp, l, consts)
        x = run_C(x, resA, resB, inp, l, consts)
    return x[None].astype(np.float32)
```
